# Optimizing a Trainium2 kernel written in Bass

```python
import math, functools
import jax, jax.numpy as jnp
from jax import lax
import numpy as np

D_MODEL = 1024
BATCH = 8
SEQ = 2048
DEPTH = 2
DEC_BATCH = 128
DEC_SEQ = 8
PAST_LEN = 2048
PAGE_SIZE = 128

D_MIX = D_MODEL
W_BRANCH = D_MIX // 4
A_HEADS = 4
A_QK = W_BRANCH // A_HEADS // 2
A_V = 2 * A_QK
Q_BLOCK = 128
LRU_BLOCKS = 4
LRU_BW = W_BRANCH // LRU_BLOCKS
CONV_W = 4
LRU_C = 8.0
RW_HEADS = 4
RW_HS = W_BRANCH // RW_HEADS
RW_LORA_W = 32
RW_LORA_A = 32
RW_NPROJ = 5
RW_GN_EPS = RW_HS * 1e-5
S5_CH = 16
S5_GROUPS = W_BRANCH // S5_CH
S5_STATE = 64
NORM_EPS = 1e-6
NEG_BIG = -1e30

OFF_AQ = 0
OFF_AK = OFF_AQ + W_BRANCH
OFF_AV = OFF_AK + W_BRANCH
OFF_B = OFF_AV + W_BRANCH
OFF_C = OFF_B + W_BRANCH
OFF_D = OFF_C + RW_NPROJ * W_BRANCH
OFF_G = OFF_D + W_BRANCH
D_IN = OFF_G + D_MIX

kernel_name = 'hymba_style_diffattn_rglru_rwkv7_s5_step'


def rms_norm(x, g):
    xf = x.astype(jnp.float32)
    y = xf * lax.rsqrt(jnp.mean(xf * xf, axis=-1, keepdims=True) + NORM_EPS)
    return (y * g.astype(jnp.float32)).astype(x.dtype)


def diff_attn_core(q, k, v, q_pos, k_pos, lam):
    s = jnp.einsum('bqhcd,bkhcd->bhcqk', q, k).astype(jnp.float32) * (A_QK ** -0.5)
    mask = k_pos[None, :] <= q_pos[:, None]
    p = jax.nn.softmax(jnp.where(mask, s, NEG_BIG), axis=-1)
    w = p[:, :, 0] - lam * p[:, :, 1]
    return jnp.einsum('bhqk,bkhd->bqhd', w.astype(v.dtype), v)


def diff_attn_prompt(q, k, v, lam):
    b, t = q.shape[0], q.shape[1]
    nblk = t // Q_BLOCK
    qb = jnp.swapaxes(q.reshape(b, nblk, Q_BLOCK, A_HEADS, 2, A_QK), 0, 1)
    k_pos = jnp.arange(t)

    def block(args):
        q_blk, i = args
        q_pos = i * Q_BLOCK + jnp.arange(Q_BLOCK)
        return diff_attn_core(q_blk, k, v, q_pos, k_pos, lam)

    o = lax.map(block, (qb, jnp.arange(nblk)))
    return jnp.swapaxes(o, 0, 1).reshape(b, t, A_HEADS, A_V)


def diff_attn_sample(q, k, v, lam, cache_k_l, cache_v_l, page_table):
    nb, tq = q.shape[0], q.shape[1]
    past = page_table.shape[1] * PAGE_SIZE
    k_past = cache_k_l[page_table].reshape(nb, past, A_HEADS, 2, A_QK).astype(k.dtype)
    v_past = cache_v_l[page_table].reshape(nb, past, A_HEADS, A_V).astype(v.dtype)
    k_all = jnp.concatenate([k_past, k], axis=1)
    v_all = jnp.concatenate([v_past, v], axis=1)
    q_pos = past + jnp.arange(tq)
    k_pos = jnp.arange(past + tq)
    return diff_attn_core(q, k_all, v_all, q_pos, k_pos, lam)


def causal_conv(xb, buf, w, bias):
    xp = jnp.concatenate([buf.astype(xb.dtype), xb], axis=1)
    t = xb.shape[1]
    y = bias + xp[:, 0:t] * w[0]
    for j in range(1, CONV_W):
        y = y + xp[:, j:j + t] * w[j]
    return y, xp[:, xp.shape[1] - (CONV_W - 1):]


def linear_scan(a, b, h0):
    b = b.at[:, 0].add(a[:, 0] * h0)

    def comb(l, r):
        return (l[0] * r[0], r[0] * l[1] + r[1])

    _, h = lax.associative_scan(comb, (a, b), axis=1)
    return h


def rglru(xc, h0, w_a, b_a, w_x, b_x, lam):
    bt, t, _ = xc.shape
    xf = xc.astype(jnp.float32)
    xblk = xf.reshape(bt, t, LRU_BLOCKS, LRU_BW)
    r = jax.nn.sigmoid(jnp.einsum('btnc,ncd->btnd', xblk, w_a).reshape(bt, t, W_BRANCH) + b_a)
    i = jax.nn.sigmoid(jnp.einsum('btnc,ncd->btnd', xblk, w_x).reshape(bt, t, W_BRANCH) + b_x)
    log_a = -LRU_C * r * jax.nn.softplus(-lam.astype(jnp.float32))
    a = jnp.exp(log_a)
    mult = jnp.sqrt(-jnp.expm1(2.0 * log_a))
    h = linear_scan(a, mult * (i * xf), h0.astype(jnp.float32))
    return h, h[:, -1]


def rwkv7_mix(pc, prev, S0, mu, w0, w1, w2, a0, a1, a2, k_k, k_a, r_k, gn_w, gn_b):
    bt, t, _ = pc.shape
    f32 = jnp.float32
    p = pc.astype(f32)
    p_prev = jnp.concatenate([prev.astype(f32)[:, None], p[:, :-1]], axis=1)
    xm = p + (p_prev - p) * mu.astype(f32)
    xr, xw, xk, xv, xa = jnp.split(xm, RW_NPROJ, axis=-1)
    w = -jax.nn.softplus(-(w0 + jnp.tanh(xw @ w1) @ w2)) - 0.5
    decay = jnp.exp(-jnp.exp(w))
    a = jax.nn.sigmoid(a0 + (xa @ a1) @ a2)

    def heads(z):
        return z.reshape(bt, t, RW_HEADS, RW_HS)

    kk = heads(xk * k_k)
    kk = kk / jnp.maximum(jnp.sqrt(jnp.sum(kk * kk, axis=-1, keepdims=True)), 1e-12)
    k = heads(xk * (1.0 + (a - 1.0) * k_a))
    r, v, a_h, dec_h = heads(xr), heads(xv), heads(a), heads(decay)

    def step(S, inp):
        r_t, w_t, k_t, v_t, kk_t, a_t = inp
        sa = jnp.einsum('bhij,bhj->bhi', S, -kk_t)
        S = (S * w_t[:, :, None, :] + sa[..., None] * (kk_t * a_t)[:, :, None, :]
             + v_t[..., None] * k_t[:, :, None, :])
        return S, jnp.einsum('bhij,bhj->bhi', S, r_t)

    seq = tuple(jnp.swapaxes(z, 0, 1) for z in (r, dec_h, k, v, kk, a_h))
    S, y = lax.scan(step, S0.astype(f32), seq)
    y = jnp.swapaxes(y, 0, 1)
    mean = jnp.mean(y, axis=-1, keepdims=True)
    var = jnp.mean(jnp.square(y - mean), axis=-1, keepdims=True)
    y = ((y - mean) * lax.rsqrt(var + RW_GN_EPS) * gn_w.reshape(RW_HEADS, RW_HS)
         + gn_b.reshape(RW_HEADS, RW_HS))
    y = y + jnp.sum(r * k * r_k, axis=-1, keepdims=True) * v
    return y.reshape(bt, t, W_BRANCH), pc[:, -1], S


def s5_mix(u, x0, lam_re, lam_im, log_dt, b_re, b_im, c_re, c_im, d, w_glu, b_glu):
    bt, t, _ = u.shape
    f32 = jnp.float32
    uf = u.astype(f32)
    ug = uf.reshape(bt, t, S5_GROUPS, S5_CH)
    lr, li = lam_re.astype(f32), lam_im.astype(f32)
    dt = jnp.exp(log_dt.astype(f32))[:, None]
    mag = jnp.exp(lr * dt)
    ab_re, ab_im = mag * jnp.cos(li * dt), mag * jnp.sin(li * dt)
    den = lr * lr + li * li
    pr = ab_re - 1.0
    f_re = (pr * lr + ab_im * li) / den
    f_im = (ab_im * lr - pr * li) / den
    bb_re = f_re[..., None] * b_re - f_im[..., None] * b_im
    bb_im = f_re[..., None] * b_im + f_im[..., None] * b_re
    bu_re = jnp.einsum('btgc,gnc->btgn', ug, bb_re)
    bu_im = jnp.einsum('btgc,gnc->btgn', ug, bb_im)
    h0_re, h0_im = x0[..., 0].astype(f32), x0[..., 1].astype(f32)
    bu_re = bu_re.at[:, 0].add(ab_re * h0_re - ab_im * h0_im)
    bu_im = bu_im.at[:, 0].add(ab_re * h0_im + ab_im * h0_re)
    a_re = jnp.broadcast_to(ab_re, bu_re.shape)
    a_im = jnp.broadcast_to(ab_im, bu_im.shape)

    def comb(l, r):
        lar, lai, lbr, lbi = l
        rar, rai, rbr, rbi = r
        return (rar * lar - rai * lai, rar * lai + rai * lar,
                rar * lbr - rai * lbi + rbr, rar * lbi + rai * lbr + rbi)

    _, _, h_re, h_im = lax.associative_scan(comb, (a_re, a_im, bu_re, bu_im), axis=1)
    y = jnp.einsum('btgn,gcn->btgc', h_re, c_re) - jnp.einsum('btgn,gcn->btgc', h_im, c_im)
    y = y.reshape(bt, t, W_BRANCH) + d * uf
    z = jax.nn.gelu(y)
    out = z * jax.nn.sigmoid(z @ w_glu + b_glu)
    return out, jnp.stack([h_re[:, -1], h_im[:, -1]], axis=-1)


def mixer_layer(x, attend, states, lam_init, norm_pre, norm_post, w_in, w_out,
                lam_q1, lam_k1, lam_q2, lam_k2, subln_w, conv_w, conv_b,
                lru_wa, lru_ba, lru_wx, lru_bx, lru_lam,
                rw_mu, rw_w0, rw_w1, rw_w2, rw_a0, rw_a1, rw_a2, rw_kk, rw_ka, rw_rk, rw_gnw, rw_gnb,
                s5_lre, s5_lim, s5_logdt, s5_bre, s5_bim, s5_cre, s5_cim, s5_d, s5_wglu, s5_bglu):
    conv_buf, lru_h, shift_prev, wkv_S, ssm_x = states
    b, t, _ = x.shape
    f32 = jnp.float32
    h = rms_norm(x, norm_pre)
    proj = jnp.einsum('btd,de->bte', h, w_in)
    q = proj[..., OFF_AQ:OFF_AK].reshape(b, t, A_HEADS, 2, A_QK)
    k = proj[..., OFF_AK:OFF_AV].reshape(b, t, A_HEADS, 2, A_QK)
    v = proj[..., OFF_AV:OFF_B].reshape(b, t, A_HEADS, A_V)
    xb = proj[..., OFF_B:OFF_C]
    pc = proj[..., OFF_C:OFF_D]
    u = proj[..., OFF_D:OFF_G]
    gate = proj[..., OFF_G:]
    lam = (jnp.exp(jnp.sum(lam_q1.astype(f32) * lam_k1.astype(f32)))
           - jnp.exp(jnp.sum(lam_q2.astype(f32) * lam_k2.astype(f32))) + lam_init)
    oa = attend(q, k, v, lam)
    oa = (rms_norm(oa, subln_w) * (1.0 - lam_init)).reshape(b, t, W_BRANCH)
    xc, new_conv = causal_conv(xb, conv_buf, conv_w, conv_b)
    ob, new_h = rglru(xc, lru_h, lru_wa, lru_ba, lru_wx, lru_bx, lru_lam)
    oc, new_shift, new_S = rwkv7_mix(pc, shift_prev, wkv_S, rw_mu, rw_w0, rw_w1, rw_w2,
                                     rw_a0, rw_a1, rw_a2, rw_kk, rw_ka, rw_rk, rw_gnw, rw_gnb)
    od, new_ssm = s5_mix(u, ssm_x, s5_lre, s5_lim, s5_logdt, s5_bre, s5_bim, s5_cre, s5_cim,
                         s5_d, s5_wglu, s5_bglu)
    o = jnp.concatenate([oa.astype(x.dtype), ob.astype(x.dtype), oc.astype(x.dtype),
                         od.astype(x.dtype)], axis=-1) * jax.nn.silu(gate)
    y = x + rms_norm(jnp.einsum('bte,ed->btd', o, w_out), norm_post)
    new_states = (k.reshape(b, t, A_HEADS, 2 * A_QK), v, new_conv, new_h, new_shift, new_S, new_ssm)
    return y, new_states


def setup_inputs(seed: int = 0) -> dict:
    key = jax.random.key(seed)
    keys = iter(jax.random.split(key, 64))
    f32 = jnp.float32

    def nrm(shape, scale):
        return jax.random.normal(next(keys), shape, f32) * scale

    def unif(shape, lo, hi):
        return jax.random.uniform(next(keys), shape, f32, lo, hi)

    W = W_BRANCH
    n_pages = PAST_LEN // PAGE_SIZE
    n_used = DEC_BATCH * n_pages
    n_pool = n_used + n_used // 4
    x_prompt = nrm((BATCH, SEQ, D_MODEL), 1.0)
    x_sample = nrm((DEC_BATCH, DEC_SEQ, D_MODEL), 1.0)
    cache_k = nrm((DEPTH, n_pool, PAGE_SIZE, A_HEADS, 2 * A_QK), 1.0)
    cache_v = nrm((DEPTH, n_pool, PAGE_SIZE, A_HEADS, A_V), 1.0)
    page_table = jax.random.permutation(next(keys), n_pool)[:n_used].reshape(DEC_BATCH, n_pages).astype(jnp.int32)
    state_conv = nrm((DEPTH, DEC_BATCH, CONV_W - 1, W), 1.0)
    state_lru = nrm((DEPTH, DEC_BATCH, W), 0.5)
    state_shift = nrm((DEPTH, DEC_BATCH, RW_NPROJ * W), 1.0)
    state_wkv = nrm((DEPTH, DEC_BATCH, RW_HEADS, RW_HS, RW_HS), 0.3)
    state_ssm = nrm((DEPTH, DEC_BATCH, S5_GROUPS, S5_STATE, 2), 0.1)
    norm_pre = 1.0 + nrm((DEPTH, D_MODEL), 0.05)
    norm_post = 1.0 + nrm((DEPTH, D_MODEL), 0.05)
    w_in = nrm((DEPTH, D_MODEL, D_IN), D_MODEL ** -0.5)
    w_out = nrm((DEPTH, D_MIX, D_MODEL), D_MIX ** -0.5)
    lam_q1 = nrm((DEPTH, A_QK), 0.1)
    lam_k1 = nrm((DEPTH, A_QK), 0.1)
    lam_q2 = nrm((DEPTH, A_QK), 0.1)
    lam_k2 = nrm((DEPTH, A_QK), 0.1)
    subln_w = 1.0 + nrm((DEPTH, A_V), 0.05)
    conv_w = nrm((DEPTH, CONV_W, W), CONV_W ** -0.5)
    conv_b = nrm((DEPTH, W), 0.01)
    lru_wa = nrm((DEPTH, LRU_BLOCKS, LRU_BW, LRU_BW), LRU_BW ** -0.5)
    lru_ba = nrm((DEPTH, W), 0.01)
    lru_wx = nrm((DEPTH, LRU_BLOCKS, LRU_BW, LRU_BW), LRU_BW ** -0.5)
    lru_bx = nrm((DEPTH, W), 0.01)
    a_init = unif((DEPTH, W), 0.9, 0.999)
    lru_lam = jnp.log(a_init) - jnp.log1p(-a_init)
    rw_mu = unif((DEPTH, RW_NPROJ * W), 0.0, 1.0)
    rw_w0 = unif((DEPTH, W), -5.0, 1.0)
    rw_w1 = nrm((DEPTH, W, RW_LORA_W), W ** -0.5)
    rw_w2 = nrm((DEPTH, RW_LORA_W, W), 0.1 * RW_LORA_W ** -0.5)
    rw_a0 = nrm((DEPTH, W), 0.1)
    rw_a1 = nrm((DEPTH, W, RW_LORA_A), W ** -0.5)
    rw_a2 = nrm((DEPTH, RW_LORA_A, W), 0.1 * RW_LORA_A ** -0.5)
    rw_kk = 0.85 + nrm((DEPTH, W), 0.05)
    rw_ka = 1.0 + nrm((DEPTH, W), 0.05)
    rw_rk = nrm((DEPTH, RW_HEADS, RW_HS), 0.1)
    rw_gnw = 1.0 + nrm((DEPTH, W), 0.05)
    rw_gnb = nrm((DEPTH, W), 0.01)
    s5_lre = -0.5 + nrm((DEPTH, S5_GROUPS, S5_STATE), 0.01)
    s5_lim = math.pi * jnp.arange(S5_STATE, dtype=f32) + nrm((DEPTH, S5_GROUPS, S5_STATE), 0.01)
    s5_logdt = unif((DEPTH, S5_GROUPS), math.log(0.001), math.log(0.1))
    s5_bre = nrm((DEPTH, S5_GROUPS, S5_STATE, S5_CH), (2 * S5_CH) ** -0.5)
    s5_bim = nrm((DEPTH, S5_GROUPS, S5_STATE, S5_CH), (2 * S5_CH) ** -0.5)
    s5_cre = nrm((DEPTH, S5_GROUPS, S5_CH, S5_STATE), (2 * S5_STATE) ** -0.5)
    s5_cim = nrm((DEPTH, S5_GROUPS, S5_CH, S5_STATE), (2 * S5_STATE) ** -0.5)
    s5_d = nrm((DEPTH, W), 0.5)
    s5_wglu = nrm((DEPTH, W, W), W ** -0.5)
    s5_bglu = nrm((DEPTH, W), 0.01)
    return {'x_prompt': x_prompt, 'x_sample': x_sample, 'cache_k': cache_k, 'cache_v': cache_v,
            'page_table': page_table, 'state_conv': state_conv, 'state_lru': state_lru,
            'state_shift': state_shift, 'state_wkv': state_wkv, 'state_ssm': state_ssm,
            'norm_pre': norm_pre, 'norm_post': norm_post, 'w_in': w_in, 'w_out': w_out,
            'lam_q1': lam_q1, 'lam_k1': lam_k1, 'lam_q2': lam_q2, 'lam_k2': lam_k2,
            'subln_w': subln_w, 'conv_w': conv_w, 'conv_b': conv_b,
            'lru_wa': lru_wa, 'lru_ba': lru_ba, 'lru_wx': lru_wx, 'lru_bx': lru_bx, 'lru_lam': lru_lam,
            'rw_mu': rw_mu, 'rw_w0': rw_w0, 'rw_w1': rw_w1, 'rw_w2': rw_w2, 'rw_a0': rw_a0,
            'rw_a1': rw_a1, 'rw_a2': rw_a2, 'rw_kk': rw_kk, 'rw_ka': rw_ka, 'rw_rk': rw_rk,
            'rw_gnw': rw_gnw, 'rw_gnb': rw_gnb,
            's5_lre': s5_lre, 's5_lim': s5_lim, 's5_logdt': s5_logdt, 's5_bre': s5_bre,
            's5_bim': s5_bim, 's5_cre': s5_cre, 's5_cim': s5_cim, 's5_d': s5_d,
            's5_wglu': s5_wglu, 's5_bglu': s5_bglu}


def reference(x_prompt, x_sample, cache_k, cache_v, page_table, state_conv, state_lru,
              state_shift, state_wkv, state_ssm, norm_pre, norm_post, w_in, w_out,
              lam_q1, lam_k1, lam_q2, lam_k2, subln_w, conv_w, conv_b,
              lru_wa, lru_ba, lru_wx, lru_bx, lru_lam,
              rw_mu, rw_w0, rw_w1, rw_w2, rw_a0, rw_a1, rw_a2, rw_kk, rw_ka, rw_rk, rw_gnw, rw_gnb,
              s5_lre, s5_lim, s5_logdt, s5_bre, s5_bim, s5_cre, s5_cim, s5_d, s5_wglu, s5_bglu):
    weights = (norm_pre, norm_post, w_in, w_out, lam_q1, lam_k1, lam_q2, lam_k2, subln_w,
               conv_w, conv_b, lru_wa, lru_ba, lru_wx, lru_bx, lru_lam,
               rw_mu, rw_w0, rw_w1, rw_w2, rw_a0, rw_a1, rw_a2, rw_kk, rw_ka, rw_rk, rw_gnw, rw_gnb,
               s5_lre, s5_lim, s5_logdt, s5_bre, s5_bim, s5_cre, s5_cim, s5_d, s5_wglu, s5_bglu)
    dt = x_prompt.dtype
    nbp = x_prompt.shape[0]
    zero_states = (jnp.zeros((nbp, CONV_W - 1, W_BRANCH), dt),
                   jnp.zeros((nbp, W_BRANCH), dt),
                   jnp.zeros((nbp, RW_NPROJ * W_BRANCH), dt),
                   jnp.zeros((nbp, RW_HEADS, RW_HS, RW_HS), dt),
                   jnp.zeros((nbp, S5_GROUPS, S5_STATE, 2), dt))
    xp, xs = x_prompt, x_sample
    outs_p = [[] for _ in range(7)]
    outs_s = [[] for _ in range(7)]
    for l in range(DEPTH):
        wl = tuple(w[l] for w in weights)
        lam_init = 0.8 - 0.6 * math.exp(-0.3 * l)
        xp, st_p = mixer_layer(xp, diff_attn_prompt, zero_states, lam_init, *wl)
        attend_s = functools.partial(diff_attn_sample, cache_k_l=cache_k[l],
                                     cache_v_l=cache_v[l], page_table=page_table)
        st_in = (state_conv[l], state_lru[l], state_shift[l], state_wkv[l], state_ssm[l])
        xs, st_s = mixer_layer(xs, attend_s, st_in, lam_init, *wl)
        for i in range(7):
            outs_p[i].append(st_p[i])
            outs_s[i].append(st_s[i])
    k_p, v_p, conv_p, lru_p, shift_p, wkv_p, ssm_p = [jnp.stack(z) for z in outs_p]
    k_s, v_s, conv_s, lru_s, shift_s, wkv_s, ssm_s = [jnp.stack(z) for z in outs_s]
    return (xp, xs, k_p, k_s, v_p, v_s, conv_p, conv_s, lru_p, lru_s,
            shift_p, shift_s, wkv_p, wkv_s, ssm_p, ssm_s)
```

```python
import contextlib
import math
import numpy as np
import concourse.bass as bass
import concourse.mybir as mybir
from concourse.bass_utils import run_bass_kernel_spmd

F32 = mybir.dt.float32
BF16 = mybir.dt.bfloat16
I32 = mybir.dt.int32
AF = mybir.ActivationFunctionType
ALU = mybir.AluOpType
AX = mybir.AxisListType

D_MODEL = 1024
D_IN = 3584
WB = 256
NORM_EPS = 1e-6
TWO_PI = 2.0 * math.pi


class Dep:
    __slots__ = ("w", "r")

    def __init__(self):
        self.w = None
        self.r = {}


class V:
    __slots__ = ("ap", "deps")

    def __init__(self, ap, deps):
        self.ap = ap
        self.deps = deps

    def __getitem__(self, idx):
        return V(self.ap[idx], self.deps)

    def bc(self, shape):
        return V(self.ap.to_broadcast(list(shape)), self.deps)

    def re(self, pattern_, **kw):
        return V(self.ap.rearrange(pattern_, **kw), self.deps)


class T:
    def __init__(self, kb, name, shape, dtype, space="sbuf"):
        nc = kb.nc
        if space == "sbuf":
            self.h = kb.cur.enter_context(nc.sbuf_tensor(name, list(shape), dtype))
        elif space == "psum":
            self.h = kb.es.enter_context(nc.psum_tensor(name, list(shape), dtype))
        else:
            self.h = nc.dram_tensor(name, list(shape), dtype, kind=space)
        self.ap = self.h.ap() if hasattr(self.h, "ap") else self.h[:]
        self.shape = list(shape)
        self.deps = [Dep()]

    def __getitem__(self, idx):
        return V(self.ap[idx], self.deps)

    def v(self):
        return V(self.ap, self.deps)


class Ref:
    def __init__(self):
        self.t = None

    def __getitem__(self, idx):
        return self.t[idx]

    def v(self):
        return self.t.v()

    @property
    def ap(self):
        return self.t.ap

    @property
    def deps(self):
        return self.t.deps


class KB:
    NSLOT = 8

    def __init__(self, nc):
        self.nc = nc
        self.es = contextlib.ExitStack()
        self.cur = self.es
        self.eng = {"pe": nc.tensor, "act": nc.scalar, "dve": nc.vector, "pool": nc.gpsimd, "sp": nc.sync}
        self.sem = {}
        self.cnt = {}
        self.seen = {e: {} for e in self.eng}
        self.epoch = {}
        self.ekey = {}
        for e in self.eng:
            key = (e, 0)
            self.sem[key] = self.es.enter_context(nc.semaphore("s_" + e))
            self.cnt[key] = 0
            self.epoch[e] = 0
            self.ekey[e] = key
        self.dq = {}
        for q in ("sp", "pool"):
            sl = []
            for i in range(self.NSLOT):
                key = ("d", q, i)
                self.sem[key] = self.es.enter_context(nc.semaphore("d_%s_%d" % (q, i)))
                self.cnt[key] = 0
                sl.append(key)
            self.dq[q] = [sl, 0]
        self.n_instr = 0
        self.n_wait = 0

    def _need(self, waits, tok):
        if tok is None:
            return
        k, v = tok
        if waits.get(k, 0) < v:
            waits[k] = v

    def _emit_waits(self, e, waits):
        for k, v in waits.items():
            if e == "pe" and k[0] == "pe":
                continue
            if self.seen[e].get(k, 0) >= v:
                continue
            self.eng[e].wait_ge(self.sem[k], v)
            self.seen[e][k] = v
            self.n_wait += 1

    def _collect(self, reads, writes):
        waits = {}
        for v in reads:
            if v is None:
                continue
            for d in v.deps:
                self._need(waits, d.w)
        for v in writes:
            if v is None:
                continue
            for d in v.deps:
                self._need(waits, d.w)
                for k, val in d.r.items():
                    self._need(waits, (k, val))
        return waits

    def _record(self, tok, reads, writes):
        k, val = tok
        for v in reads:
            if v is None:
                continue
            for d in v.deps:
                if d.r.get(k, 0) < val:
                    d.r[k] = val
        for v in writes:
            if v is None:
                continue
            for d in v.deps:
                d.w = tok
                d.r = {}

    EPOCH = 16000

    def op(self, e, fn, reads, writes):
        waits = self._collect(reads, writes)
        key = self.ekey[e]
        if self.cnt[key] >= self.EPOCH:
            self.epoch[e] += 1
            key = (e, self.epoch[e])
            self.sem[key] = self.es.enter_context(self.nc.semaphore("s_%s_%d" % (e, self.epoch[e])))
            self.cnt[key] = 0
            self.ekey[e] = key
        self._emit_waits(e, waits)
        ins = fn(self.eng[e])
        self.cnt[key] += 1
        ins.then_inc(self.sem[key], 1)
        self._record((key, self.cnt[key]), reads, writes)
        self.n_instr += 1
        return ins

    def dma(self, q, out, in_, fn=None):
        sl, idx = self.dq[q]
        key = sl[idx % self.NSLOT]
        self.dq[q][1] = idx + 1
        waits = self._collect([in_], [out])
        if self.cnt[key] > 0:
            self._need(waits, (key, self.cnt[key]))
        self._emit_waits(q, waits)
        if fn is None:
            ins = self.eng[q].dma_start(out=out.ap, in_=in_.ap)
        else:
            ins = fn(self.eng[q])
        self.cnt[key] += 16
        ins.then_inc(self.sem[key], 16)
        self._record((key, self.cnt[key]), [in_], [out])
        self.n_instr += 1
        return ins

    def barrier(self):
        allw = {k: v for k, v in self.cnt.items() if v > 0}
        for e in ("pe", "act", "dve", "pool", "sp"):
            for k, v in allw.items():
                if k[0] == e and e in ("pe", "sp"):
                    continue
                if self.seen[e].get(k, 0) >= v:
                    continue
                self.eng[e].wait_ge(self.sem[k], v)
                self.seen[e][k] = v
                self.n_wait += 1

    def finish(self):
        self.barrier()

    def mm(self, out, lhsT, rhs, start=True, stop=True, **kw):
        return self.op("pe", lambda E: E.matmul(out.ap, lhsT.ap, rhs.ap, start=start, stop=stop, **kw),
                       [lhsT, rhs] + ([] if start else [out]), [out])

    def tr(self, out, in_, ident):
        return self.op("pe", lambda E: E.transpose(out.ap, in_.ap, ident.ap), [in_, ident], [out])

    def act(self, out, in_, func, bias=None, scale=None):
        kw = {}
        rd = [in_]
        if bias is not None:
            if isinstance(bias, V):
                kw["bias"] = bias.ap
                rd.append(bias)
            else:
                kw["bias"] = bias
        if scale is not None:
            if isinstance(scale, V):
                kw["scale"] = scale.ap
                rd.append(scale)
            else:
                kw["scale"] = scale
        return self.op("act", lambda E: E.activation(out.ap, in_.ap, func, **kw), rd, [out])

    def ts(self, out, in0, s1, op0, s2=None, op1=None, e="dve"):
        rd = [in0]
        a1 = s1
        if isinstance(s1, V):
            a1 = s1.ap
            rd.append(s1)
        a2 = s2
        if isinstance(s2, V):
            a2 = s2.ap
            rd.append(s2)
        kw = {}
        if op1 is not None:
            kw["op1"] = op1
        return self.op(e, lambda E: E.tensor_scalar(out.ap, in0.ap, a1, a2, op0, **kw), rd, [out])

    def tt(self, out, in0, in1, op, e="dve"):
        return self.op(e, lambda E: E.tensor_tensor(out.ap, in0.ap, in1.ap, op), [in0, in1], [out])

    def stt(self, out, in0, s, in1, op0, op1):
        rd = [in0, in1]
        a = s
        if isinstance(s, V):
            a = s.ap
            rd.append(s)
        return self.op("dve", lambda E: E.scalar_tensor_tensor(out.ap, in0.ap, a, in1.ap, op0, op1), rd, [out])

    def scan(self, out, d0, d1, init):
        rd = [d0, d1]
        a = init
        if isinstance(init, V):
            a = init.ap
            rd.append(init)
        return self.op("dve", lambda E: E.tensor_tensor_scan(out.ap, d0.ap, d1.ap, a, ALU.mult, ALU.add), rd, [out])

    def copy(self, out, in_, e="dve"):
        if e == "act":
            return self.op("act", lambda E: E.copy(out.ap, in_.ap), [in_], [out])
        return self.op(e, lambda E: E.tensor_copy(out.ap, in_.ap), [in_], [out])

    def memset(self, out, val, e="dve"):
        return self.op(e, lambda E: E.memset(out.ap, val), [], [out])

    def recip(self, out, in_):
        return self.op("dve", lambda E: E.reciprocal(out.ap, in_.ap), [in_], [out])

    def reduce(self, out, in_, op=ALU.add, axis=AX.X):
        return self.op("dve", lambda E: E.tensor_reduce(out.ap, in_.ap, axis, op), [in_], [out])


class Cfg:
    def __init__(self, TP=2048, NCH=256, SB=16, TS=8, NPG=16, POOL=2560, DEPTH=2):
        self.TP, self.NCH, self.SB, self.TS, self.NPG, self.POOL, self.DEPTH = TP, NCH, SB, TS, NPG, POOL, DEPTH
        self.NS = SB * TS
        self.SC = 128
        self.STOP = 0
        self.SA = 0
        self.NORWKV = 0


IN_SPECS = [
    ("x_prompt", lambda c: [c.TP, D_MODEL], F32), ("x_sample", lambda c: [c.NS, D_MODEL], F32),
    ("cache_k", lambda c: [c.DEPTH, c.POOL * 128, 256], F32), ("cache_v", lambda c: [c.DEPTH, c.POOL * 128, 256], F32),
    ("page_table", lambda c: [1, c.SB * c.NPG], I32),
    ("state_conv", lambda c: [c.DEPTH, c.SB, 3 * WB], F32), ("state_lru", lambda c: [c.DEPTH, c.SB, WB], F32),
    ("state_shift", lambda c: [c.DEPTH, c.SB, 5 * WB], F32), ("state_wkv", lambda c: [c.DEPTH, c.SB, 4 * 64, 64], F32),
    ("state_ssm", lambda c: [c.DEPTH, c.SB, 2048], F32),
    ("norm_pre", lambda c: [c.DEPTH, D_MODEL], F32), ("norm_post", lambda c: [c.DEPTH, D_MODEL], F32),
    ("w_in", lambda c: [c.DEPTH, D_MODEL, D_IN], F32), ("w_out", lambda c: [c.DEPTH, D_MODEL, D_MODEL], F32),
    ("lam_q1", lambda c: [c.DEPTH, 32], F32), ("lam_k1", lambda c: [c.DEPTH, 32], F32),
    ("lam_q2", lambda c: [c.DEPTH, 32], F32), ("lam_k2", lambda c: [c.DEPTH, 32], F32),
    ("subln_w", lambda c: [c.DEPTH, 64], F32), ("conv_w", lambda c: [c.DEPTH, 4, WB], F32),
    ("conv_b", lambda c: [c.DEPTH, WB], F32), ("lru_wa", lambda c: [c.DEPTH, 4, 64, 64], F32),
    ("lru_ba", lambda c: [c.DEPTH, WB], F32), ("lru_wx", lambda c: [c.DEPTH, 4, 64, 64], F32),
    ("lru_bx", lambda c: [c.DEPTH, WB], F32), ("lru_lam", lambda c: [c.DEPTH, WB], F32),
    ("rw_mu", lambda c: [c.DEPTH, 5 * WB], F32), ("rw_w0", lambda c: [c.DEPTH, WB], F32),
    ("rw_w1", lambda c: [c.DEPTH, WB, 32], F32), ("rw_w2", lambda c: [c.DEPTH, 32, WB], F32),
    ("rw_a0", lambda c: [c.DEPTH, WB], F32), ("rw_a1", lambda c: [c.DEPTH, WB, 32], F32),
    ("rw_a2", lambda c: [c.DEPTH, 32, WB], F32), ("rw_kk", lambda c: [c.DEPTH, WB], F32),
    ("rw_ka", lambda c: [c.DEPTH, WB], F32), ("rw_rk", lambda c: [c.DEPTH, WB], F32),
    ("rw_gnw", lambda c: [c.DEPTH, WB], F32), ("rw_gnb", lambda c: [c.DEPTH, WB], F32),
    ("s5_lre", lambda c: [c.DEPTH, 16, 64], F32), ("s5_lim", lambda c: [c.DEPTH, 16, 64], F32),
    ("s5_logdt", lambda c: [c.DEPTH, 16], F32), ("s5_bre", lambda c: [c.DEPTH, 16, 64, 16], F32),
    ("s5_bim", lambda c: [c.DEPTH, 16, 64, 16], F32), ("s5_cre", lambda c: [c.DEPTH, 16, 16, 64], F32),
    ("s5_cim", lambda c: [c.DEPTH, 16, 16, 64], F32), ("s5_d", lambda c: [c.DEPTH, WB], F32),
    ("s5_wglu", lambda c: [c.DEPTH, WB, WB], F32), ("s5_bglu", lambda c: [c.DEPTH, WB], F32),
]

OUT_SPECS = [
    ("y_p", lambda c: [c.TP, D_MODEL]), ("y_s", lambda c: [c.NS, D_MODEL]),
    ("k_p", lambda c: [c.DEPTH, c.TP, 256]), ("k_s", lambda c: [c.DEPTH, c.NS, 256]),
    ("v_p", lambda c: [c.DEPTH, c.TP, 256]), ("v_s", lambda c: [c.DEPTH, c.NS, 256]),
    ("conv_p", lambda c: [c.DEPTH, 1, 768]), ("conv_s", lambda c: [c.DEPTH, c.SB, 768]),
    ("lru_p", lambda c: [c.DEPTH, 1, 256]), ("lru_s", lambda c: [c.DEPTH, c.SB, 256]),
    ("shift_p", lambda c: [c.DEPTH, 1, 1280]), ("shift_s", lambda c: [c.DEPTH, c.SB, 1280]),
    ("wkv_p", lambda c: [c.DEPTH, 1, 256, 64]), ("wkv_s", lambda c: [c.DEPTH, c.SB, 256, 64]),
    ("ssm_p", lambda c: [c.DEPTH, 1, 2048]), ("ssm_s", lambda c: [c.DEPTH, c.SB, 2048]),
]


class StopBuild(Exception):
    pass


class Unit:
    def __init__(self, kind, tok0, N, nseq, Tq):
        self.kind, self.tok0, self.N, self.nseq, self.T = kind, tok0, N, nseq, Tq


def build(cfg):
    nc = bass.Bass("TRN2", target_bir_lowering=False)
    kb = KB(nc)
    ctx = nc.allow_non_contiguous_dma(reason="small strided parameter / state loads")
    ctx.__enter__()
    L = cfg.DEPTH
    NMAX = max(cfg.NCH, cfg.NS)
    din = {n: T(kb, n, f(cfg), dt, space="ExternalInput") for n, f, dt in IN_SPECS}
    dout = {n: T(kb, n, f(cfg), F32, space="ExternalOutput") for n, f in OUT_SPECS}

    NBLK = {"w_in": D_IN // 512, "w_out": D_MODEL // 512}
    wbf = {w_: T(kb, "wbf_" + w_, [L, 128, NBLK[w_], 8, 512], BF16, space="Internal") for w_ in ("w_in", "w_out")}
    for l_ in range(L):
        for kt_ in range(8):
            for which_ in ("w_in", "w_out"):
                nb_ = NBLK[which_]
                for b0_ in range(0, nb_, 4):
                    b1_ = min(nb_, b0_ + 4)
                    src_ = din[which_].ap[l_, kt_ * 128:(kt_ + 1) * 128, b0_ * 512:b1_ * 512].rearrange("p (b c) -> p b c", c=512)
                    kb.dma("pool", wbf[which_][l_, :, b0_:b1_, kt_, :], V(src_, []))

    def dv(name):
        return V(din[name].ap, [])

    def sb(name, shape, dt=F32):
        return T(kb, name, shape, dt)

    def ld(dst, src_ap, q="sp"):
        kb.dma(q, dst, V(src_ap, []))

    banks = [T(kb, "bank%d" % i, [128, 512], F32, space="psum") for i in range(8)]
    rot = [0]

    def nb():
        b = banks[rot[0] % 3]
        rot[0] += 1
        return b

    iop = sb("iop", [128, 1])
    iof = sb("iof", [128, 256])
    kb.op("pool", lambda E: E.iota(iop.ap, [[0, 1]], base=0, channel_multiplier=1,
                                   allow_small_or_imprecise_dtypes=True), [], [iop.v()])
    kb.op("pool", lambda E: E.iota(iof.ap, [[1, 256]], base=0, channel_multiplier=0,
                                   allow_small_or_imprecise_dtypes=True), [], [iof.v()])
    ident = sb("ident", [128, 128])
    kb.ts(ident.v(), iof[:, 0:128], iop[:, 0:1], ALU.is_equal)
    ones_f = sb("ones_f", [128, 128])
    kb.memset(ones_f.v(), 1.0)
    ones_b = sb("ones_b", [128, 128], BF16)
    kb.memset(ones_b.v(), 1.0)
    zeros_f = sb("zeros_f", [128, 512], BF16)
    kb.memset(zeros_f.v(), 0.0)
    NQT = cfg.NCH // 128
    mk = sb("mk", [128, NQT, cfg.NCH], BF16)
    for i in range(NQT):
        kb.ts(mk[:, i, :], iof[:, 0:cfg.NCH], float(-128 * i), ALU.add, s2=iop[:, 0:1], op1=ALU.is_ge)
    bo = sb("bo", [128, 128])
    rowb = sb("rowb", [128, 1])
    kb.ts(rowb.v(), iop.v(), 64.0, ALU.is_ge)
    kb.ts(bo.v(), iof[:, 0:128], 64.0, ALU.is_ge, s2=rowb[:, 0:1], op1=ALU.is_equal)
    mc1 = sb("mc1", [128, 1])
    mc0 = sb("mc0", [128, 1])
    kb.ts(mc1.v(), rowb.v(), -64.0, ALU.mult, s2=iop[:, 0:1], op1=ALU.add)
    kb.ts(mc1.v(), mc1.v(), 32.0, ALU.is_ge)
    kb.ts(mc0.v(), mc1.v(), -1.0, ALU.mult, s2=1.0, op1=ALU.add)
    mcs = [mc0, mc1]
    pm64 = sb("pm64", [128, 1])
    kb.ts(pm64.v(), rowb.v(), -64.0, ALU.mult, s2=iop[:, 0:1], op1=ALU.add)
    mL = sb("mL", [128, 64])
    kb.ts(mL.v(), iof[:, 0:64], pm64[:, 0:1], ALU.is_lt)
    identD = sb("identD", [128, 64])
    kb.ts(identD.v(), iof[:, 0:64], pm64[:, 0:1], ALU.is_equal)
    m4P = sb("m4P", [128, 4, 64])
    m4S = sb("m4S", [128, 4, cfg.TS])
    for s_ in range(4):
        op_ = ALU.is_gt if s_ % 2 == 0 else ALU.is_ge
        kb.ts(m4P[:, s_, :], iof[:, 0:64], pm64[:, 0:1], op_)
        kb.ts(m4S[:, s_, :], iof[:, 0:cfg.TS], pm64[:, 0:1], op_)
    pat01 = sb("pat01", [128, cfg.SB, cfg.TS])
    kb.memset(pat01.v(), 1.0)
    kb.memset(pat01[:, :, 0:1], 0.0)

    NSCR = 12
    scr = [Ref() for i in range(NSCR)]
    scr_i = [0]

    def S(n=None):
        t = scr[scr_i[0] % NSCR]
        scr_i[0] += 1
        return t

    small_i = [0]

    def small(shape, dt=F32):
        small_i[0] += 1
        return sb("sm%d" % small_i[0], shape, dt)

    P = []
    wblkf = [sb("wblk%d" % i, [128, 2048]) for i in range(2)]
    wblk = [V(w_.ap.bitcast(BF16).rearrange("p (k c) -> p k c", k=8), w_.deps) for w_ in wblkf]
    ps_a = wblkf[0][:, 0:cfg.SC]
    ps_b = wblkf[0][:, cfg.SC:2 * cfg.SC]
    ps_ki = V(wblkf[1].ap.bitcast(I32)[:, 0:cfg.SC], wblkf[1].deps)
    SC = cfg.SC
    for l in range(L):
        p = {}
        lam_init = 0.8 - 0.6 * math.exp(-0.3 * l)

        def col(name, src, ncol):
            t = small([128, ncol])
            ld(t.v(), din[src].ap[l].rearrange("(i p) -> p i", p=128))
            p[name] = t
            return t

        col("gpre", "norm_pre", 8)
        col("gpost", "norm_post", 8)
        t = small([128, 64])
        ld(t.v(), din["subln_w"].ap[l:l + 1, :].to_broadcast([128, 64]))
        kb.ts(t.v(), t.v(), 1.0 - lam_init, ALU.mult)
        p["subln"] = t
        ee = []
        for qn, kn in (("lam_q1", "lam_k1"), ("lam_q2", "lam_k2")):
            a = small([128, 32])
            b = small([128, 32])
            ld(a.v(), din[qn].ap[l:l + 1, :].to_broadcast([128, 32]))
            ld(b.v(), din[kn].ap[l:l + 1, :].to_broadcast([128, 32]))
            kb.tt(a.v(), a.v(), b.v(), ALU.mult)
            s = small([128, 1])
            kb.reduce(s.v(), a.v())
            kb.act(s.v(), s.v(), AF.Exp)
            ee.append(s)
        nlam = small([128, 1])
        kb.tt(nlam.v(), ee[1].v(), ee[0].v(), ALU.subtract)
        kb.ts(nlam.v(), nlam.v(), -lam_init, ALU.add)
        p["nlam"] = nlam
        cw = small([128, 2, 4])
        for j in range(4):
            ld(cw[:, :, j], din["conv_w"].ap[l, j].rearrange("(i p) -> p i", p=128))
        p["cw"] = cw
        col("cb", "conv_b", 2)
        for nm, src in (("lwa", "lru_wa"), ("lwx", "lru_wx")):
            t = small([128, 2, 128])
            kb.memset(t.v(), 0.0)
            for n in range(4):
                r0 = (n % 2) * 64
                ld(t[r0:r0 + 64, n // 2, r0:r0 + 64], din[src].ap[l, n])
            p[nm] = t
        col("lba", "lru_ba", 2)
        col("lbx", "lru_bx", 2)
        ll = col("llam", "lru_lam", 2)
        sp_ = small([128, 2])
        kb.act(sp_.v(), ll.v(), AF.Exp, scale=-1.0)
        kb.act(sp_.v(), sp_.v(), AF.Ln, bias=1.0)
        p["m8"] = small([128, 2])
        p["m16"] = small([128, 2])
        kb.ts(p["m8"].v(), sp_.v(), -8.0, ALU.mult)
        kb.ts(p["m16"].v(), sp_.v(), -16.0, ALU.mult)
        col("mu", "rw_mu", 10)
        w0 = col("w0", "rw_w0", 2)
        p["nw0"] = small([128, 2])
        kb.ts(p["nw0"].v(), w0.v(), -1.0, ALU.mult)
        col("a0", "rw_a0", 2)
        col("kkc", "rw_kk", 2)
        ka = col("kac", "rw_ka", 2)
        p["omka"] = small([128, 2])
        kb.ts(p["omka"].v(), ka.v(), -1.0, ALU.mult, s2=1.0, op1=ALU.add)
        col("rkc", "rw_rk", 2)
        col("gnw", "rw_gnw", 2)
        col("gnb", "rw_gnb", 2)
        for nm, src in (("w1", "rw_w1"), ("a1", "rw_a1")):
            t = small([128, 2, 32])
            ld(t.v(), din[src].ap[l].rearrange("(t p) r -> p t r", p=128))
            p[nm] = t
        for nm, src in (("w2", "rw_w2"), ("a2", "rw_a2")):
            t = small([32, 256])
            ld(t.v(), din[src].ap[l])
            p[nm] = t
        lre = small([128, 8])
        lim = small([128, 8])
        ld(lre.v(), din["s5_lre"].ap[l].rearrange("(j g) n -> (g n) j", g=2))
        ld(lim.v(), din["s5_lim"].ap[l].rearrange("(j g) n -> (g n) j", g=2))
        dt_ = small([128, 8])
        ldt = din["s5_logdt"].ap[l].rearrange("(j g) -> g j", g=2)
        for g2 in range(2):
            ld(dt_[g2 * 64:(g2 + 1) * 64, :], ldt[g2:g2 + 1, :].to_broadcast([64, 8]))
        kb.act(dt_.v(), dt_.v(), AF.Exp)
        mag = small([128, 8])
        kb.tt(mag.v(), lre.v(), dt_.v(), ALU.mult)
        kb.act(mag.v(), mag.v(), AF.Exp)
        p["mag"] = mag
        th = small([128, 8])
        kb.tt(th.v(), lim.v(), dt_.v(), ALU.mult)
        cs = small([128, 8, SC])
        sn = small([128, 8, SC])
        ki = ps_ki
        for j in range(8):
            for tab, shift in ((sn, 0.0), (cs, math.pi / 2)):
                a = ps_a
                b = ps_b
                kb.ts(a, iof[:, 0:SC], th[:, j:j + 1], ALU.mult, s2=shift, op1=ALU.add)
                kb.ts(b, a, 1.0 / TWO_PI, ALU.mult)
                kb.copy(ki, b)
                kb.copy(b, ki)
                kb.stt(a, b, -TWO_PI, a, ALU.mult, ALU.add)
                kb.ts(b, a, math.pi, ALU.is_gt, s2=-TWO_PI, op1=ALU.mult)
                kb.tt(a, a, b, ALU.add)
                kb.ts(b, a, -math.pi, ALU.is_lt, s2=TWO_PI, op1=ALU.mult)
                kb.tt(a, a, b, ALU.add)
                kb.ts(a, a, math.pi, ALU.min, s2=-math.pi, op1=ALU.max)
                kb.act(tab[:, j, :], a, AF.Sin)
        p["cs"], p["sn"] = cs, sn
        c1 = small([128, 8])
        s1 = small([128, 8])
        kb.copy(c1.v(), cs[:, :, 1])
        kb.copy(s1.v(), sn[:, :, 1])
        ns1 = small([128, 8])
        kb.ts(ns1.v(), s1.v(), -1.0, ALU.mult)
        p["c1"], p["s1"], p["ns1"] = c1, s1, ns1
        abre = small([128, 8])
        abim = small([128, 8])
        kb.tt(abre.v(), mag.v(), c1.v(), ALU.mult)
        kb.tt(abim.v(), mag.v(), s1.v(), ALU.mult)
        den = small([128, 8])
        t2 = small([128, 8])
        kb.tt(den.v(), lre.v(), lre.v(), ALU.mult)
        kb.tt(t2.v(), lim.v(), lim.v(), ALU.mult)
        kb.tt(den.v(), den.v(), t2.v(), ALU.add)
        kb.recip(den.v(), den.v())
        pr = small([128, 8])
        kb.ts(pr.v(), abre.v(), -1.0, ALU.add)
        fre = small([128, 8])
        fim = small([128, 8])
        kb.tt(fre.v(), pr.v(), lre.v(), ALU.mult)
        kb.tt(t2.v(), abim.v(), lim.v(), ALU.mult)
        kb.tt(fre.v(), fre.v(), t2.v(), ALU.add)
        kb.tt(fre.v(), fre.v(), den.v(), ALU.mult)
        kb.tt(fim.v(), abim.v(), lre.v(), ALU.mult)
        kb.tt(t2.v(), pr.v(), lim.v(), ALU.mult)
        kb.tt(fim.v(), fim.v(), t2.v(), ALU.subtract)
        kb.tt(fim.v(), fim.v(), den.v(), ALU.mult)
        nfre = small([128, 8])
        kb.ts(nfre.v(), fre.v(), -1.0, ALU.mult)
        tA = small([128, 8, SC])
        tB = small([128, 8, SC])
        for j in range(8):
            kb.ts(tA[:, j, :], cs[:, j, :], fre[:, j:j + 1], ALU.mult)
            kb.stt(tA[:, j, :], sn[:, j, :], fim[:, j:j + 1], tA[:, j, :], ALU.mult, ALU.add)
            kb.ts(tB[:, j, :], cs[:, j, :], fim[:, j:j + 1], ALU.mult)
            kb.stt(tB[:, j, :], sn[:, j, :], nfre[:, j:j + 1], tB[:, j, :], ALU.mult, ALU.add)
        p["tA"], p["tB"] = tA, tB
        for nm, src in (("Bre", "s5_bre"), ("Bim", "s5_bim")):
            t = small([128, 2, 128])
            kb.memset(t.v(), 0.0)
            for g in range(16):
                r0 = (g % 8) * 16
                c0 = (g % 2) * 64
                ld(t[r0:r0 + 16, g // 8, c0:c0 + 64], din[src].ap[l, g].rearrange("n c -> c n"))
            p[nm] = t
        for nm, src in (("Cre", "s5_cre"), ("nCim", "s5_cim")):
            t = small([128, 8, 64])
            kb.memset(t.v(), 0.0)
            for g in range(16):
                r0 = (g % 2) * 64
                c0 = ((g // 2) % 2) * 32 + (g % 2) * 16
                ld(t[r0:r0 + 64, g // 2, c0:c0 + 16], din[src].ap[l, g].rearrange("c n -> n c"))
            if nm == "nCim":
                kb.ts(t.v(), t.v(), -1.0, ALU.mult)
            p[nm] = t
        col("dcol", "s5_d", 2)
        col("bglu", "s5_bglu", 2)
        t = small([128, 2, 256])
        ld(t.v(), din["s5_wglu"].ap[l].rearrange("(t p) m -> p t m", p=128))
        p["wglu"] = t
        P.append(p)

    ck_n = [0]
    ck_log = []

    def ckpt(k):
        ck_n[0] += 1
        ck_log.append((ck_n[0], k, kb.n_instr))
        if cfg.STOP == ck_n[0]:
            raise StopBuild()

    PS = []
    for l in range(L):
        st = {"convst": small([128, 2, 3]), "lruh": small([128, 2]), "shst": small([128, 10]),
              "ssr": small([128, 8]), "ssi": small([128, 8]),
              "ST": [[small([128, 64]) for _ in range(2)] for _ in range(2)], "par": 0, "parh": [0, 0]}
        for k_ in ("convst", "lruh", "shst", "ssr", "ssi"):
            kb.memset(st[k_].v(), 0.0)
        for hp in range(2):
            kb.memset(st["ST"][hp][0].v(), 0.0)
        PS.append(st)

    N_ = NMAX
    xT, hT, xp, pcs, u_t, sg, o_bf, big1, rstd_t, ysb, stg, oa_tm = [Ref() for _ in range(12)]
    qTs = [Ref(), Ref()]
    u_m = [Ref(), Ref()]
    kvst = [Ref(), Ref()]
    kvst_i = [0]
    PT = [Ref(), Ref()]
    pt_i = [0]
    KRt, Vt, BtT, KtT, Wsb, Usb, pCt = [Ref() for _ in range(7)]
    XYt = [Ref(), Ref()]
    phase_prompt = [True]
    Rt = [Ref() for _ in range(8)]
    acs = [Ref() for _ in range(3)]
    wb_i = [0]
    NKT = cfg.TP // 128
    Kh = [Ref() for l in range(L)]
    Vh = [Ref() for l in range(L)]
    phase_n = [0]

    def begin_phase(NA, prompt):
        kb.ph = contextlib.ExitStack()
        kb.cur = kb.ph
        phase_n[0] += 1
        tg = "_%d" % phase_n[0]
        for i in range(NSCR):
            scr[i].t = sb("scr%d" % i + tg, [128, NA])
        xT.t = sb("xT" + tg, [128, 8, NA])
        hT.t = sb("hT" + tg, [128, 8, NA], BF16)
        for c in range(2):
            qTs[c].t = sb("qT%d" % c + tg, [128, 2, NA], BF16)
            u_m[c].t = sb("u_m%d" % c + tg, [128, 2, NA])
            if c == 0:
                kvst[0].t = sb("kvst0" + tg, [128, 256])
                kvst[1].t = kvst[0].t
            PT[c].t = sb("PT%d" % c + tg, [128, NA], BF16)
        xp.t = sb("xp" + tg, [128, 2, NA + 3 * cfg.SB])
        pcs.t = sb("pcs" + tg, [128, 10, NA + cfg.SB])
        sg.t = sb("sg" + tg, [128, 8, NA], BF16)
        o_bf.t = sb("o_bf" + tg, [128, 8, NA], BF16)
        big1.t = sb("big1" + tg, [128, 10, NA])
        rstd_t.t = sb("rstd_t" + tg, [128, NA])
        ysb.t = sb("ysb" + tg, [128, 2, NA])
        stg.t = sb("stg" + tg, [128, 512])
        oa_tm.t = sb("oa_tm" + tg, [128, 4, 64])
        for i in range(8):
            Rt[i].t = sb("Rt%d" % i + tg, [128, NA])
        phase_prompt[0] = prompt
        if not prompt:
            for i in range(3):
                acs[i].t = sb("acs%d" % i + tg, [128, 256])
        Cc = 64 if prompt else cfg.TS
        Gc = (NA // Cc) if prompt else min(8, NA // Cc)
        KRt.t = sb("KRt" + tg, [128, 2 * NA])
        for i in range(2):
            XYt[i].t = sb("XYt%d" % i + tg, [128, 2 * Gc * Cc])
        Vt.t = sb("Vt" + tg, [128, Gc * 64])
        BtT.t = sb("BtT" + tg, [128, Gc * 64])
        KtT.t = sb("KtT" + tg, [128, Gc * 64])
        Wsb.t = sb("Wsb" + tg, [128, 64])
        Usb.t = sb("Usb" + tg, [128, 64])
        pCt.t = sb("pCt" + tg, [128, max(4, NA // Cc)])
        if prompt:
            for l in range(L):
                Kh[l].t = sb("Kh%d" % l, [128, 2, cfg.TP], BF16)
                Vh[l].t = sb("Vh%d" % l, [128, NKT, 4, 65], BF16)
                kb.memset(Vh[l][:, :, :, 64:65], 1.0)

    def end_phase():
        kb.barrier()
        kb.ph.close()
        kb.cur = kb.es
        eps_cache.clear()
        stmp.clear()

    def load_weights_block(l, which, blk):
        t = wblk[wb_i[0] % 2]
        wb_i[0] += 1
        src = wbf[which].ap[l, :, blk, :, :]
        kb.dma("sp", t[:, :, :], V(src, wbf[which].deps))
        return t

    def fm_to_rows(srcs, dst_ap, M):
        for g0 in range(0, len(srcs), 4):
            grp = srcs[g0:g0 + 4]
            b = nb()
            for i, s_ in enumerate(grp):
                kb.tr(b[0:M, i * 128:(i + 1) * 128], s_, ident.v())
            kb.copy(stg[0:M, 0:len(grp) * 128], b[0:M, 0:len(grp) * 128], e="act")
            kb.dma("sp", V(dst_ap[:, g0 * 128:(g0 + len(grp)) * 128], []), stg[0:M, 0:len(grp) * 128])

    def rms_stats(src_fn, n_t, N, scale, eps):
        b = nb()
        for kt in range(n_t):
            sq = S()
            sqb = V(sq.ap.bitcast(BF16)[:, 0:N], sq.deps)
            kb.act(sqb, src_fn(kt), AF.Square)
            kb.mm(b[:, 0:N], ones_b.v(), sqb, start=(kt == 0), stop=(kt == n_t - 1))
        kb.act(rstd_t[:, 0:N], b[:, 0:N], AF.Sqrt, bias=eps_col(eps), scale=scale)
        kb.recip(rstd_t[:, 0:N], rstd_t[:, 0:N])

    eps_cache = {}

    def eps_col(val):
        if val not in eps_cache:
            t = small([128, 1])
            kb.memset(t.v(), val)
            eps_cache[val] = t
        return eps_cache[val].v()

    def layer(l, U, SS):
        p = P[l]
        N, nseq, Tq = U.N, U.nseq, U.T
        isP = U.kind == "p"
        st = PS[l]

        def v3(view):
            return view.re("p (b t) -> p b t", b=nseq)

        rms_stats(lambda kt: xT[:, kt, 0:N], 8, N, 1.0 / D_MODEL, NORM_EPS)
        for kt in range(8):
            kb.stt(hT[:, kt, 0:N], xT[:, kt, 0:N], p["gpre"][:, kt:kt + 1], rstd_t[:, 0:N], ALU.mult, ALU.mult)

        XW = N + 3 * nseq
        PW = N + nseq
        xp3 = [xp[:, i, 0:XW].re("p (b t) -> p b t", b=nseq) for i in range(2)]
        pc3 = [pcs[:, i, 0:PW].re("p (b t) -> p b t", b=nseq) for i in range(10)]
        ev = [0]

        def evac(dst, src, func=None):
            ev[0] += 1
            if func is not None:
                kb.act(dst, src, func)
            elif ev[0] % 2 == 0:
                kb.copy(dst, src, e="act")
            else:
                kb.copy(dst, src)

        kT_cur = Kh[l] if isP else SS["kTc"]
        koff = U.tok0 if isP else 0
        for blk in range(7):
            wt = load_weights_block(l, "w_in", blk)
            for mm_ in range(4):
                m = blk * 4 + mm_
                if 4 <= m < 6:
                    continue
                b = nb()
                for kt in range(8):
                    kb.mm(b[:, 0:N], wt[:, kt, mm_ * 128:(mm_ + 1) * 128], hT[:, kt, 0:N], start=(kt == 0), stop=(kt == 7))
                src = b[:, 0:N]
                if m < 2:
                    for c in range(2):
                        kb.ts(qTs[c][:, m, 0:N], src, mcs[c][:, 0:1], ALU.mult)
                        if not isP:
                            kb.ts(SS["qS"][:, m, :, c, :], v3(src), mcs[c][:, 0:1], ALU.mult)
                elif m < 4:
                    evac(kT_cur[:, m - 2, koff:koff + N], src)
                elif m < 6:
                    pass
                elif m < 8:
                    evac(xp3[m - 6][:, :, 3:3 + Tq], v3(src))
                elif m < 18:
                    evac(pc3[m - 8][:, :, 1:1 + Tq], v3(src))
                elif m < 20:
                    for c in range(2):
                        kb.ts(u_m[c][:, m - 18, 0:N], src, mcs[c][:, 0:1], ALU.mult)
                else:
                    evac(sg[:, m - 20, 0:N], src, func=AF.Silu)
            if blk in (0, 1) and isP:
                for tt_ in range(N // 128):
                    b = nb()
                    c0 = 256 if blk == 0 else 0
                    for kt in range(8):
                        kb.mm(b[:, 0:256], hT[:, kt, tt_ * 128:(tt_ + 1) * 128], wt[:, kt, c0:c0 + 256],
                              start=(kt == 0), stop=(kt == 7))
                    ks = kvst[kvst_i[0] % 2]
                    kvst_i[0] += 1
                    kb.copy(ks.v(), b[:, 0:256], e="act")
                    tok = U.tok0 + tt_ * 128
                    kb.dma("sp", V(dout["k_p" if blk == 0 else "v_p"].ap[l, tok:tok + 128, :], []), ks.v())
                    if blk == 1:
                        kb.copy(Vh[l][:, tok // 128, :, 0:64], ks.v().re("p (h d) -> p h d", h=4))
            if blk == 0 and not isP:
                for b_ in range(nseq):
                    b = nb()
                    for kt in range(8):
                        kb.mm(b[0:Tq, 0:256], hT[:, kt, b_ * Tq:(b_ + 1) * Tq], wt[:, kt, 256:512],
                              start=(kt == 0), stop=(kt == 7))
                    ks = kvst[kvst_i[0] % 2]
                    kvst_i[0] += 1
                    kb.copy(ks[0:Tq, :], b[0:Tq, 0:256], e="act")
                    kb.dma("sp", V(dout["k_s"].ap[l, b_ * Tq:(b_ + 1) * Tq, :], []), ks[0:Tq, :])
            if blk == 1 and not isP:
                kb.copy(SS["wv"].v(), wt[:, :, 0:256])

        ckpt(3)
        inv_sqrt = 32.0 ** -0.5
        if isP:
            nq = N // 128
            accs = [[banks[3 + qt * 2 + hh] for hh in range(2)] for qt in range(nq)]
            for qt in range(nq):
                for hh in range(2):
                    kb.mm(accs[qt][hh].v(), zeros_f[:, 0:128], zeros_f[:, 0:512], start=True, stop=False,
                          skip_group_check=True)
            nkt = (U.tok0 + N) // 128
            kt0 = U.tok0 // 128
            its = [(h, c, kt) for h in range(4) for c in range(2) for kt in range(nkt)]

            def score(it_):
                h, c, kt = it_
                pb = (h % 2) * 64
                b = nb()
                kb.mm(b[:, 0:N], Kh[l][pb:pb + 64, h // 2, kt * 128:(kt + 1) * 128], qTs[c][pb:pb + 64, h // 2, 0:N])
                return b

            pend = score(its[0])
            for ii_, (h, c, kt) in enumerate(its):
                b = pend
                if ii_ + 1 < len(its):
                    pend = score(its[ii_ + 1])
                pt = PT[pt_i[0] % 2]
                pt_i[0] += 1
                kb.act(pt[:, 0:N], b[:, 0:N], AF.Exp, scale=inv_sqrt)
                if kt >= kt0:
                    kb.tt(pt[:, 0:N], pt[:, 0:N], mk[:, kt - kt0, 0:N], ALU.mult)
                for qt in range(nq):
                    if kt > kt0 + qt:
                        continue
                    slot = (h % 2) * 2 + c
                    kb.mm(accs[qt][h // 2][:, slot * 65:(slot + 1) * 65], pt[:, qt * 128:(qt + 1) * 128],
                          Vh[l][:, kt, h, :], start=False, stop=False, skip_group_check=True)
            for qt in range(nq):
                attn_combine(l, [accs[qt][0], accs[qt][1]], 128, o_bf, qt * 128)
        else:
            sample_attention(l, U, SS)

        ckpt(4)
        for i in range(2):
            if isP:
                kb.copy(xp3[i][:, :, 0:3], st["convst"][:, i:i + 1, :])
            else:
                kb.copy(xp3[i][:, :, 0:3], SS["conv"][:, i, :, :])
            xc = S()
            xc3 = v3(xc[:, 0:N])
            kb.ts(xc3, xp3[i][:, :, 0:Tq], p["cw"][:, i, 0:1], ALU.mult, s2=p["cb"][:, i:i + 1], op1=ALU.add)
            for j in range(1, 4):
                kb.stt(xc3, xp3[i][:, :, j:j + Tq], p["cw"][:, i, j:j + 1], xc3, ALU.mult, ALU.add)
            if isP:
                kb.copy(st["convst"][:, i:i + 1, :], xp3[i][:, :, Tq:Tq + 3])
            r_ = S()
            ig = S()
            b = nb()
            kb.mm(b[:, 0:N], p["lwa"][:, i, :], xc[:, 0:N])
            kb.act(r_[:, 0:N], b[:, 0:N], AF.Sigmoid, bias=p["lba"][:, i:i + 1])
            b = nb()
            kb.mm(b[:, 0:N], p["lwx"][:, i, :], xc[:, 0:N])
            kb.act(ig[:, 0:N], b[:, 0:N], AF.Sigmoid, bias=p["lbx"][:, i:i + 1])
            a_ = S()
            a2 = S()
            kb.act(a_[:, 0:N], r_[:, 0:N], AF.Exp, scale=p["m8"][:, i:i + 1])
            kb.act(a2[:, 0:N], r_[:, 0:N], AF.Exp, scale=p["m16"][:, i:i + 1])
            kb.ts(a2[:, 0:N], a2[:, 0:N], -1.0, ALU.mult, s2=1.0, op1=ALU.add)
            kb.ts(a2[:, 0:N], a2[:, 0:N], 1e-30, ALU.max)
            kb.act(a2[:, 0:N], a2[:, 0:N], AF.Sqrt)
            kb.tt(a2[:, 0:N], a2[:, 0:N], ig[:, 0:N], ALU.mult)
            kb.tt(a2[:, 0:N], a2[:, 0:N], xc[:, 0:N], ALU.mult)
            h_ = S()
            if isP:
                kb.scan(h_[:, 0:N], a_[:, 0:N], a2[:, 0:N], st["lruh"][:, i:i + 1])
                kb.copy(st["lruh"][:, i:i + 1], h_[:, N - 1:N])
            else:
                a3, b3 = v3(a_[:, 0:N]), v3(a2[:, 0:N])
                tmp = S()
                kb.tt(tmp[:, 0:nseq], a3[:, :, 0], SS["lru"][:, i, :], ALU.mult)
                kb.tt(b3[:, :, 0], b3[:, :, 0], tmp[:, 0:nseq], ALU.add)
                kb.memset(a3[:, :, 0:1], 0.0)
                kb.scan(h_[:, 0:N], a_[:, 0:N], a2[:, 0:N], 0.0)
                kb.copy(SS["lru_o"][:, i, :], v3(h_[:, 0:N])[:, :, Tq - 1])
            kb.tt(o_bf[:, 2 + i, 0:N], h_[:, 0:N], sg[:, 2 + i, 0:N], ALU.mult)

        ckpt(5)
        s5_branch(l, U, SS)
        ckpt(6)

        rwkv_branch(l, U, SS, pc3)
        ckpt(7)

        z = big1
        for blk in range(2):
            wt = load_weights_block(l, "w_out", blk)
            for mm_ in range(4):
                m = blk * 4 + mm_
                b = nb()
                for kt in range(8):
                    kb.mm(b[:, 0:N], wt[:, kt, mm_ * 128:(mm_ + 1) * 128], o_bf[:, kt, 0:N], start=(kt == 0), stop=(kt == 7))
                evac(z[:, m, 0:N], b[:, 0:N])
        rms_stats(lambda kt: z[:, kt, 0:N], 8, N, 1.0 / D_MODEL, NORM_EPS)
        for kt in range(8):
            kb.stt(z[:, kt, 0:N], z[:, kt, 0:N], p["gpost"][:, kt:kt + 1], rstd_t[:, 0:N], ALU.mult, ALU.mult)
            kb.tt(xT[:, kt, 0:N], xT[:, kt, 0:N], z[:, kt, 0:N], ALU.add)
        ckpt(8)

    def attn_combine(l, accl, M, dst, col0):
        p = P[l]
        for hh in range(2):
            acc = accl[hh]
            a4 = acc[0:M, 0:260].re("p (s d) -> p s d", s=4)
            rc = small_tmp([128, 4])
            kb.recip(rc[0:M, :], a4[:, :, 64])
            on = S() if phase_prompt[0] else acs[hh]
            on4 = on[0:M, 0:256].re("p (s d) -> p s d", s=4)
            kb.tt(on4, a4[:, :, 0:64], V(rc.ap[0:M, :].unsqueeze(2).to_broadcast([M, 4, 64]), rc.deps), ALU.mult)
            on5 = on[0:M, 0:256].re("p (h c d) -> p h c d", h=2, c=2)
            kb.stt(oa_tm[0:M, hh * 2:hh * 2 + 2, :], on5[:, :, 1, :], p["nlam"][0:M, 0:1], on5[:, :, 0, :], ALU.mult, ALU.add)
        sq = S() if phase_prompt[0] else acs[2]
        sq3 = sq[0:M, 0:256].re("p (h d) -> p h d", h=4)
        kb.tt(sq3, oa_tm[0:M, :, :], oa_tm[0:M, :, :], ALU.mult)
        ss = small_tmp([128, 4])
        kb.reduce(ss[0:M, :], sq3)
        kb.act(ss[0:M, :], ss[0:M, :], AF.Sqrt, bias=eps_col(NORM_EPS)[0:M, :], scale=1.0 / 64)
        kb.recip(ss[0:M, :], ss[0:M, :])
        kb.tt(oa_tm[0:M, :, :], oa_tm[0:M, :, :], V(ss.ap[0:M, :].unsqueeze(2).to_broadcast([M, 4, 64]), ss.deps), ALU.mult)
        kb.tt(oa_tm[0:M, :, :], oa_tm[0:M, :, :],
              V(p["subln"].ap[0:M, :].unsqueeze(1).to_broadcast([M, 4, 64]), p["subln"].deps), ALU.mult)
        b = nb()
        for i in range(2):
            kb.tr(b[:, i * 128:i * 128 + M], oa_tm[0:M, 2 * i:2 * i + 2, :].re("p h d -> p (h d)"), ident[0:M, 0:M])
        for i in range(2):
            kb.tt(dst[:, i, col0:col0 + M], b[:, i * 128:i * 128 + M], sg[:, i, col0:col0 + M], ALU.mult)

    stmp = {}

    def small_tmp(shape):
        key = tuple(shape)
        if key not in stmp:
            stmp[key] = [[small(shape) for _ in range(4)], 0]
        lst = stmp[key]
        t = lst[0][lst[1] % 4]
        lst[1] += 1
        return t

    def sample_attention(l, U, SS):
        p = P[l]
        Tq = U.T
        NPG = cfg.NPG
        G = min(4, NPG)
        NG = NPG // G
        inv_sqrt = 32.0 ** -0.5
        ck = din["cache_k"].ap.rearrange("l r c -> (l r) c")
        cvv = din["cache_v"].ap.rearrange("l r c -> (l r) c")
        groups = [(b_, gi_) for b_ in range(U.nseq) for gi_ in range(NG)]
        accO = [banks[3], banks[4]]
        accD = banks[5]

        def stageT(n_):
            b_, gi_ = groups[n_]
            g0 = gi_ * G
            kpg = SS["kpg"][n_ % 2]
            vpg = SS["vpg"][n_ % 2]
            for pg in range(G):
                ic = b_ * NPG + g0 + pg
                kb.dma("pool", kpg[:, pg, :], V(ck, SS["idx"][l].deps),
                       fn=lambda E, pg=pg, ic=ic, kpg=kpg: E.indirect_dma_start(
                           out=kpg.ap[:, pg, :], out_offset=None, in_=ck,
                           in_offset=bass.IndirectOffsetOnAxis(ap=SS["idx"][l].ap[:, ic:ic + 1], axis=0)))
                kb.dma("pool", vpg[:, pg, :], V(cvv, SS["idx"][l].deps),
                       fn=lambda E, pg=pg, ic=ic, vpg=vpg: E.indirect_dma_start(
                           out=vpg.ap[:, pg, :], out_offset=None, in_=cvv,
                           in_offset=bass.IndirectOffsetOnAxis(ap=SS["idx"][l].ap[:, ic:ic + 1], axis=0)))
            vpb = SS["vpb"][n_ % 2]
            kb.copy(vpb[:, 0:G, :], vpg[:, 0:G, :])
            kTg = SS["kTg"][n_ % 2]
            for tl in range(2):
                b = nb()
                for pg in range(G):
                    kb.tr(b[:, pg * 128:(pg + 1) * 128], kpg[:, pg, tl * 128:(tl + 1) * 128], ident.v())
                kb.copy(kTg[:, tl, 0:G * 128], b[:, 0:G * 128], e=("act" if tl else "dve"))

        def stageSP(n_):
            b_, gi_ = groups[n_]
            kTg = SS["kTg"][n_ % 2]
            vpb = SS["vpb"][n_ % 2]
            if gi_ == 0:
                for a in accO + [accD]:
                    kb.mm(a.v(), zeros_f[:, 0:128], zeros_f[:, 0:512], start=True, stop=False, skip_group_check=True)
            bsc = [nb(), nb()]
            for h in range(4):
                pb = (h % 2) * 64
                for pg in range(G):
                    c0 = ((h // 2) * G + pg) * 2 * Tq
                    kb.mm(bsc[h % 2][:, c0:c0 + 2 * Tq], kTg[pb:pb + 64, h // 2, pg * 128:(pg + 1) * 128],
                          SS["qS"][pb:pb + 64, h // 2, b_, :, :].re("p c q -> p (c q)"))
            ptt = SS["PTs"]
            HW_ = 4 * G * Tq
            for hs in range(2):
                kb.act(ptt[:, hs * HW_:(hs + 1) * HW_], bsc[hs][:, 0:HW_], AF.Exp, scale=inv_sqrt)
            pt6 = ptt[:, 0:8 * G * Tq].re("p (a g c q) -> p a g c q", a=4, g=G, c=2)
            pts = SS["PTr"]
            kb.reduce(pts[:, 0:8 * Tq].re("p (a c q) -> p a c q", a=4, c=2),
                      ptt[:, 0:8 * G * Tq].re("p (a g c q) -> p a c q g", a=4, g=G, c=2))
            for hc in range(8):
                h = hc // 2
                si = (h % 2) * 4 + (h // 2) * 2 + (hc % 2)
                kb.mm(accD[0:Tq, hc:hc + 1], pts[:, si * Tq:(si + 1) * Tq], ones_f[:, 0:1], start=False, stop=False,
                      skip_group_check=True)
                for pg in range(G):
                    slot = hc % 4
                    kb.mm(accO[h // 2][0:Tq, slot * 65:slot * 65 + 64], pt6[:, (h % 2) * 2 + h // 2, pg, hc % 2, :],
                          vpb[:, pg, h * 64:(h + 1) * 64], start=False, stop=False, skip_group_check=True)
            if gi_ == NG - 1:
                tail(b_)

        def tail(b_):
            b = nb()
            for kt in range(8):
                kb.mm(b[0:Tq, 0:256], hT[:, kt, b_ * Tq:(b_ + 1) * Tq], SS["wv"][:, kt, :], start=(kt == 0), stop=(kt == 7))
            vb = SS["vb"][b_ % 2]
            kb.copy(vb.v(), b[0:Tq, 0:256], e="act")
            kb.dma("sp", V(dout["v_s"].ap[l, b_ * Tq:(b_ + 1) * Tq, :], []), vb.v())
            bsc = [nb(), nb()]
            for h in range(4):
                for c in range(2):
                    pb = (h % 2) * 64
                    sl_ = (h // 2) * 2 + c
                    kb.mm(bsc[h % 2][0:Tq, sl_ * Tq:(sl_ + 1) * Tq], SS["kTc"][pb:pb + 64, h // 2, b_ * Tq:(b_ + 1) * Tq],
                          qTs[c][pb:pb + 64, h // 2, b_ * Tq:(b_ + 1) * Tq])
            ptc = SS["PTc"]
            for hs in range(2):
                kb.act(ptc[0:Tq, hs * 4 * Tq:(hs + 1) * 4 * Tq], bsc[hs][0:Tq, 0:4 * Tq], AF.Exp, scale=inv_sqrt)
            pc3_ = ptc[0:Tq, 0:8 * Tq].re("p (s q) -> p s q", s=8)
            kb.tt(pc3_, pc3_, V(m4P.ap[0:Tq, 1, 0:Tq].unsqueeze(1).to_broadcast([Tq, 8, Tq]), m4P.deps), ALU.mult)
            for hc in range(8):
                h = hc // 2
                slot = hc % 4
                si = (h % 2) * 4 + (h // 2) * 2 + (hc % 2)
                kb.mm(accD[0:Tq, hc:hc + 1], ptc[0:Tq, si * Tq:(si + 1) * Tq], ones_f[0:Tq, 0:1], start=False, stop=False,
                      skip_group_check=True)
                kb.mm(accO[h // 2][0:Tq, slot * 65:slot * 65 + 64], ptc[0:Tq, si * Tq:(si + 1) * Tq],
                      vb[0:Tq, h * 64:(h + 1) * 64], start=False, stop=False, skip_group_check=True)
            for hh in range(2):
                a4 = accO[hh][0:Tq, 0:260].re("p (s d) -> p s d", s=4)
                kb.copy(a4[:, :, 64], accD[0:Tq, hh * 4:hh * 4 + 4])
            attn_combine(l, accO, Tq, o_bf, b_ * Tq)

        stageT(0)
        for n_ in range(len(groups)):
            if n_ + 1 < len(groups):
                stageT(n_ + 1)
            stageSP(n_)

    def s5_prompt(l, U, psY):
        p = P[l]
        st = PS[l]
        N = U.N
        SCn = cfg.SC
        H = []
        for i in range(7):
            t_ = scr[i]
            for hf in range(2):
                d_ = Dep()
                d_.w = t_.deps[0].w
                d_.r = dict(t_.deps[0].r)
                H.append(V(t_.ap[:, hf * 128:(hf + 1) * 128], [d_]))
        its = [(c0, j) for c0 in range(0, N, SCn) for j in range(8)]

        def stageA(n_):
            c0, j = its[n_]
            ct = j // 4
            wb_ = 64 * ((j % 4) // 2)
            um = u_m[j % 2]
            bA, bB = nb(), nb()
            kb.mm(bA[:, 0:SCn], p["Bre"][wb_:wb_ + 64, ct, :], um[wb_:wb_ + 64, ct, c0:c0 + SCn])
            kb.mm(bB[:, 0:SCn], p["Bim"][wb_:wb_ + 64, ct, :], um[wb_:wb_ + 64, ct, c0:c0 + SCn])
            tA, tB = p["tA"][:, j, 0:SCn], p["tB"][:, j, 0:SCn]
            t1, t2, gr, gi = H[0], H[1], H[2], H[3]
            Gr, Gi = H[4 + 2 * (n_ % 2)], H[5 + 2 * (n_ % 2)]
            kb.tt(t1, bA[:, 0:SCn], tA, ALU.mult)
            kb.tt(t2, bB[:, 0:SCn], tB, ALU.mult)
            kb.tt(gr, t1, t2, ALU.subtract)
            kb.tt(t1, bB[:, 0:SCn], tA, ALU.mult)
            kb.tt(t2, bA[:, 0:SCn], tB, ALU.mult)
            kb.tt(gi, t1, t2, ALU.add)
            ir = small_tmp([128, 1])
            ii = small_tmp([128, 1])
            tm = small_tmp([128, 1])
            kb.ts(tm.v(), st["ssr"][:, j:j + 1], p["c1"][:, j:j + 1], ALU.mult)
            kb.stt(ir.v(), st["ssi"][:, j:j + 1], p["ns1"][:, j:j + 1], tm.v(), ALU.mult, ALU.add)
            kb.ts(tm.v(), st["ssi"][:, j:j + 1], p["c1"][:, j:j + 1], ALU.mult)
            kb.stt(ii.v(), st["ssr"][:, j:j + 1], p["s1"][:, j:j + 1], tm.v(), ALU.mult, ALU.add)
            dec = V(p["mag"].ap[:, j:j + 1].to_broadcast([128, SCn]), p["mag"].deps)
            kb.scan(Gr, dec, gr, ir[:, 0:1])
            kb.scan(Gi, dec, gi, ii[:, 0:1])

        def stageB(n_):
            c0, j = its[n_]
            ct = j // 4
            cs_, sn_ = p["cs"][:, j, 0:SCn], p["sn"][:, j, 0:SCn]
            Gr, Gi = H[4 + 2 * (n_ % 2)], H[5 + 2 * (n_ % 2)]
            t3, t4 = H[8], H[9]
            hr, hi = H[10 + 2 * (n_ % 2)], H[11 + 2 * (n_ % 2)]
            kb.tt(t3, Gr, cs_, ALU.mult, e="pool")
            kb.tt(t4, Gi, sn_, ALU.mult, e="pool")
            kb.tt(hr, t3, t4, ALU.subtract, e="pool")
            kb.tt(t3, Gi, cs_, ALU.mult, e="pool")
            kb.tt(t4, Gr, sn_, ALU.mult, e="pool")
            kb.tt(hi, t3, t4, ALU.add, e="pool")
            kb.copy(st["ssr"][:, j:j + 1], hr[:, SCn - 1:SCn])
            kb.copy(st["ssi"][:, j:j + 1], hi[:, SCn - 1:SCn])
            q0 = 64 * ((j % 4) // 2)
            kb.mm(psY[ct][q0:q0 + 64, c0:c0 + SCn], p["Cre"][:, j, :], hr, start=(j % 2 == 0), stop=False)
            kb.mm(psY[ct][q0:q0 + 64, c0:c0 + SCn], p["nCim"][:, j, :], hi, start=False, stop=(j % 2 == 1))

        stageA(0)
        for n_ in range(len(its)):
            if n_ + 1 < len(its):
                stageA(n_ + 1)
            stageB(n_)
        for i in range(7):
            D_ = scr[i].deps[0]
            for hf in range(2):
                d_ = H[2 * i + hf].deps[0]
                toks = list(d_.r.items()) + ([d_.w] if d_.w is not None else [])
                for k_, v_ in toks:
                    if D_.r.get(k_, 0) < v_:
                        D_.r[k_] = v_

    def s5_branch(l, U, SS):
        p = P[l]
        st = PS[l]
        N, nseq, Tq = U.N, U.nseq, U.T
        isP = U.kind == "p"
        SCn = cfg.SC if isP else N
        psY = [banks[3], banks[4]]
        if isP:
            s5_prompt(l, U, psY)
        for c0 in (range(0, N, SCn) if not isP else []):
            for j in range(8):
                ct = j // 4
                rb = 32 * (j % 4)
                bA = nb()
                bB = nb()
                wb_ = 64 * ((j % 4) // 2)
                um = u_m[j % 2]
                kb.mm(bA[:, 0:SCn], p["Bre"][wb_:wb_ + 64, ct, :], um[wb_:wb_ + 64, ct, c0:c0 + SCn])
                kb.mm(bB[:, 0:SCn], p["Bim"][wb_:wb_ + 64, ct, :], um[wb_:wb_ + 64, ct, c0:c0 + SCn])
                if isP:
                    tA, tB = p["tA"][:, j, 0:SCn], p["tB"][:, j, 0:SCn]
                    cs_, sn_ = p["cs"][:, j, 0:SCn], p["sn"][:, j, 0:SCn]
                    w = lambda x: x
                else:
                    def bcv(tab):
                        return V(tab.ap[:, j:j + 1, 0:Tq].to_broadcast([128, nseq, Tq]), tab.deps)
                    tA, tB, cs_, sn_ = bcv(p["tA"]), bcv(p["tB"]), bcv(p["cs"]), bcv(p["sn"])
                    w = lambda x: x.re("p (b t) -> p b t", b=nseq)
                t1, t2, gr, gi = S(), S(), S(), S()
                kb.tt(w(t1[:, 0:SCn]), w(bA[:, 0:SCn]), tA, ALU.mult)
                kb.tt(w(t2[:, 0:SCn]), w(bB[:, 0:SCn]), tB, ALU.mult)
                kb.tt(gr[:, 0:SCn], t1[:, 0:SCn], t2[:, 0:SCn], ALU.subtract)
                kb.tt(w(t1[:, 0:SCn]), w(bB[:, 0:SCn]), tA, ALU.mult)
                kb.tt(w(t2[:, 0:SCn]), w(bA[:, 0:SCn]), tB, ALU.mult)
                kb.tt(gi[:, 0:SCn], t1[:, 0:SCn], t2[:, 0:SCn], ALU.add)
                Gr, Gi = S(), S()
                if isP:
                    ir = small_tmp([128, 1])
                    ii = small_tmp([128, 1])
                    tm = small_tmp([128, 1])
                    kb.ts(tm.v(), st["ssr"][:, j:j + 1], p["c1"][:, j:j + 1], ALU.mult)
                    kb.stt(ir.v(), st["ssi"][:, j:j + 1], p["ns1"][:, j:j + 1], tm.v(), ALU.mult, ALU.add)
                    kb.ts(tm.v(), st["ssi"][:, j:j + 1], p["c1"][:, j:j + 1], ALU.mult)
                    kb.stt(ii.v(), st["ssr"][:, j:j + 1], p["s1"][:, j:j + 1], tm.v(), ALU.mult, ALU.add)
                    dec = V(p["mag"].ap[:, j:j + 1].to_broadcast([128, SCn]), p["mag"].deps)
                    kb.scan(Gr[:, 0:SCn], dec, gr[:, 0:SCn], ir[:, 0:1])
                    kb.scan(Gi[:, 0:SCn], dec, gi[:, 0:SCn], ii[:, 0:1])
                else:
                    dect = S()
                    kb.ts(w(dect[:, 0:N]), pat01.v(), p["mag"][:, j:j + 1], ALU.mult)
                    ir, ii, tm = S(), S(), S()
                    h0r, h0i = SS["ssr"][:, j, :], SS["ssi"][:, j, :]
                    kb.ts(tm[:, 0:nseq], h0r, p["c1"][:, j:j + 1], ALU.mult)
                    kb.stt(ir[:, 0:nseq], h0i, p["ns1"][:, j:j + 1], tm[:, 0:nseq], ALU.mult, ALU.add)
                    kb.ts(tm[:, 0:nseq], h0i, p["c1"][:, j:j + 1], ALU.mult)
                    kb.stt(ii[:, 0:nseq], h0r, p["s1"][:, j:j + 1], tm[:, 0:nseq], ALU.mult, ALU.add)
                    kb.stt(w(gr[:, 0:N])[:, :, 0], ir[:, 0:nseq], p["mag"][:, j:j + 1], w(gr[:, 0:N])[:, :, 0], ALU.mult, ALU.add)
                    kb.stt(w(gi[:, 0:N])[:, :, 0], ii[:, 0:nseq], p["mag"][:, j:j + 1], w(gi[:, 0:N])[:, :, 0], ALU.mult, ALU.add)
                    kb.scan(Gr[:, 0:N], dect[:, 0:N], gr[:, 0:N], 0.0)
                    kb.scan(Gi[:, 0:N], dect[:, 0:N], gi[:, 0:N], 0.0)
                hr, hi = S(), S()
                pe_ = "pool" if isP else "dve"
                if isP:
                    t1, t2 = S(), S()
                kb.tt(w(t1[:, 0:SCn]), w(Gr[:, 0:SCn]), cs_, ALU.mult, e=pe_)
                kb.tt(w(t2[:, 0:SCn]), w(Gi[:, 0:SCn]), sn_, ALU.mult, e=pe_)
                kb.tt(hr[:, 0:SCn], t1[:, 0:SCn], t2[:, 0:SCn], ALU.subtract, e=pe_)
                kb.tt(w(t1[:, 0:SCn]), w(Gi[:, 0:SCn]), cs_, ALU.mult, e=pe_)
                kb.tt(w(t2[:, 0:SCn]), w(Gr[:, 0:SCn]), sn_, ALU.mult, e=pe_)
                kb.tt(hi[:, 0:SCn], t1[:, 0:SCn], t2[:, 0:SCn], ALU.add, e=pe_)
                if isP:
                    kb.copy(st["ssr"][:, j:j + 1], hr[:, SCn - 1:SCn])
                    kb.copy(st["ssi"][:, j:j + 1], hi[:, SCn - 1:SCn])
                else:
                    kb.copy(SS["ssr_o"][:, j, :], w(hr[:, 0:N])[:, :, Tq - 1])
                    kb.copy(SS["ssi_o"][:, j, :], w(hi[:, 0:N])[:, :, Tq - 1])
                q0 = 64 * ((j % 4) // 2)
                kb.mm(psY[ct][q0:q0 + 64, c0:c0 + SCn], p["Cre"][:, j, :], hr[:, 0:SCn], start=(j % 2 == 0), stop=False)
                kb.mm(psY[ct][q0:q0 + 64, c0:c0 + SCn], p["nCim"][:, j, :], hi[:, 0:SCn], start=False, stop=(j % 2 == 1))
        zt = []
        for ct in range(2):
            yv = S()
            kb.stt(yv[:, 0:N], u_m[0][:, ct, 0:N], p["dcol"][:, ct:ct + 1], psY[ct][:, 0:N], ALU.mult, ALU.add)
            kb.stt(yv[:, 0:N], u_m[1][:, ct, 0:N], p["dcol"][:, ct:ct + 1], yv[:, 0:N], ALU.mult, ALU.add)
            z_ = S()
            kb.act(z_[:, 0:N], yv[:, 0:N], AF.Gelu_apprx_tanh)
            zt.append(z_)
        for m in range(2):
            b = nb()
            for kt in range(2):
                kb.mm(b[:, 0:N], p["wglu"][:, kt, m * 128:(m + 1) * 128], zt[kt][:, 0:N], start=(kt == 0), stop=(kt == 1))
            sgm = S()
            kb.act(sgm[:, 0:N], b[:, 0:N], AF.Sigmoid, bias=p["bglu"][:, m:m + 1])
            kb.tt(sgm[:, 0:N], sgm[:, 0:N], zt[m][:, 0:N], ALU.mult)
            kb.tt(o_bf[:, 6 + m, 0:N], sgm[:, 0:N], sg[:, 6 + m, 0:N], ALU.mult)

    def rwkv_branch(l, U, SS, pc3):
        p = P[l]
        st = PS[l]
        N, nseq, Tq = U.N, U.nseq, U.T
        isP = U.kind == "p"
        xm = big1

        def v3(view):
            return view.re("p (b t) -> p b t", b=nseq)

        for i in range(10):
            if isP:
                kb.copy(pc3[i][:, :, 0:1], st["shst"][:, i:i + 1].re("p (a b) -> p a b", a=1))
            else:
                kb.copy(pc3[i][:, :, 0], SS["shift"][:, i, :])
            d = S()
            kb.tt(v3(d[:, 0:N]), pc3[i][:, :, 0:Tq], pc3[i][:, :, 1:Tq + 1], ALU.subtract)
            kb.stt(v3(xm[:, i, 0:N]), v3(d[:, 0:N]), p["mu"][:, i:i + 1], pc3[i][:, :, 1:Tq + 1], ALU.mult, ALU.add)
            if isP:
                kb.copy(st["shst"][:, i:i + 1].re("p (a b) -> p a b", a=1), pc3[i][:, :, Tq:Tq + 1])
            else:
                kb.copy(SS["shift_o"][:, i, :], pc3[i][:, :, Tq])
        xr, xw, xk, xv, xa = [lambda i, o=o: xm[:, o + i, 0:N] for o in (0, 2, 4, 6, 8)]
        b32 = nb()
        for kt in range(2):
            kb.mm(b32[0:32, 0:N], p["w1"][:, kt, :], xw(kt), start=(kt == 0), stop=(kt == 1))
        th = S()
        kb.act(th[0:32, 0:N], b32[0:32, 0:N], AF.Tanh)
        b32 = nb()
        for kt in range(2):
            kb.mm(b32[0:32, 0:N], p["a1"][:, kt, :], xa(kt), start=(kt == 0), stop=(kt == 1))
        ta = S()
        kb.copy(ta[0:32, 0:N], b32[0:32, 0:N])
        R = {k_: [None, None] for k_ in ("w", "k", "ka", "nkk")}
        for i in range(2):
            b = nb()
            kb.mm(b[:, 0:N], p["w2"][0:32, i * 128:(i + 1) * 128], th[0:32, 0:N])
            e0 = S()
            kb.act(e0[:, 0:N], b[:, 0:N], AF.Exp, bias=p["nw0"][:, i:i + 1], scale=-1.0)
            kb.act(e0[:, 0:N], e0[:, 0:N], AF.Ln, bias=1.0)
            kb.act(Rt[i][:, 0:N], e0[:, 0:N], AF.Exp, bias=eps_col(-0.5), scale=-1.0)
            R["w"][i] = Rt[i]
            b = nb()
            kb.mm(b[:, 0:N], p["a2"][0:32, i * 128:(i + 1) * 128], ta[0:32, 0:N])
            a_ = S()
            kb.act(a_[:, 0:N], b[:, 0:N], AF.Sigmoid, bias=p["a0"][:, i:i + 1])
            k1 = S()
            kb.ts(k1[:, 0:N], xk(i), p["kkc"][:, i:i + 1], ALU.mult)
            sq = S()
            kb.tt(sq[:, 0:N], k1[:, 0:N], k1[:, 0:N], ALU.mult)
            b = nb()
            kb.mm(b[:, 0:N], bo.v(), sq[:, 0:N])
            kb.act(sq[:, 0:N], b[:, 0:N], AF.Sqrt)
            kb.ts(sq[:, 0:N], sq[:, 0:N], 1e-12, ALU.max)
            kb.recip(sq[:, 0:N], sq[:, 0:N])
            kk_ = Rt[6 + i]
            kb.tt(kk_[:, 0:N], k1[:, 0:N], sq[:, 0:N], ALU.mult)
            kv_ = Rt[2 + i]
            kb.ts(kv_[:, 0:N], a_[:, 0:N], p["kac"][:, i:i + 1], ALU.mult, s2=p["omka"][:, i:i + 1], op1=ALU.add)
            kb.tt(kv_[:, 0:N], kv_[:, 0:N], xk(i), ALU.mult)
            R["k"][i] = kv_
            kka = Rt[4 + i]
            kb.tt(kka[:, 0:N], kk_[:, 0:N], a_[:, 0:N], ALU.mult)
            R["ka"][i] = kka
            kb.ts(kk_[:, 0:N], kk_[:, 0:N], -1.0, ALU.mult)
            R["nkk"][i] = kk_
        C = 64 if isP else Tq
        nch = N // C
        G = nch if isP else min(8, nch)
        ngrp = nch // G
        KK = int(round(math.log2(C))) - 1
        yb = banks[7]
        fbi = [0]

        def fb():
            b_ = banks[3 + fbi[0] % 4]
            fbi[0] += 1
            return b_

        def rows(fn):
            if C == 64:
                fn(slice(0, 128))
            else:
                for pb_ in (0, 64):
                    fn(slice(pb_, pb_ + C))

        def v3c(view):
            return view.re("p (c t) -> p c t", t=C)

        m4 = m4P if isP else m4S
        MMf = big1[:, 2:6, :].re("p a n -> p (a n)")
        TTf = big1[:, 8:10, :].re("p a n -> p (a n)")
        MM = MMf[:, 0:G * 4 * C].re("p (c s t) -> p c s t", c=G, s=4)
        TT = TTf[:, 0:2 * G * C].re("p (x c t) -> p x c t", x=2, c=G)
        for hp in range(2):
            nlw, lp, p_, pinv, pm1 = S(), S(), S(), S(), S()
            kb.ts(nlw[:, 0:N], R["w"][hp][:, 0:N], -1.0, ALU.mult)
            if isP:
                for c_ in range(nch):
                    kb.scan(lp[:, c_ * C:(c_ + 1) * C], ones_f[:, 0:C], nlw[:, c_ * C:(c_ + 1) * C], 0.0)
            else:
                kb.scan(lp[:, 0:N], pat01.v().re("p c t -> p (c t)"), nlw[:, 0:N], 0.0)
            kb.act(p_[:, 0:N], lp[:, 0:N], AF.Exp)
            kb.act(pinv[:, 0:N], lp[:, 0:N], AF.Exp, scale=-1.0)
            kb.tt(pm1[:, 0:N], lp[:, 0:N], nlw[:, 0:N], ALU.subtract)
            kb.act(pm1[:, 0:N], pm1[:, 0:N], AF.Exp)
            kb.copy(pCt[:, 0:nch], v3c(p_[:, 0:N])[:, :, C - 1])
            KR4 = KRt[:, 0:2 * N].re("p (c x t) -> p c x t", x=2, t=C)
            kb.tt(KR4[:, :, 0, :], v3c(pm1[:, 0:N]), v3c(R["nkk"][hp][:, 0:N]), ALU.mult)
            kb.tt(KR4[:, :, 1, :], v3c(p_[:, 0:N]), v3c(xr(hp)), ALU.mult)
            Bt = R["ka"][hp]
            kb.tt(Bt[:, 0:N], Bt[:, 0:N], pinv[:, 0:N], ALU.mult)
            Kt = R["nkk"][hp]
            kb.tt(Kt[:, 0:N], R["k"][hp][:, 0:N], pinv[:, 0:N], ALU.mult)
            for g in range(ngrp):
                cpb = min(G, 512 // (4 * C))
                for c0_ in range(0, G, cpb):
                    bm = fb()
                    for c in range(c0_, c0_ + cpb):
                        cg = g * G + c
                        for h2 in range(2):
                            pb = 64 * h2
                            o0 = (c - c0_) * 4 * C
                            kr = KR4[pb:pb + 64, cg, :, :].re("p x t -> p (x t)")
                            kb.mm(bm[pb:pb + C, o0:o0 + 2 * C], Bt[pb:pb + 64, cg * C:(cg + 1) * C], kr)
                            kb.mm(bm[pb:pb + C, o0 + 2 * C:o0 + 4 * C], Kt[pb:pb + 64, cg * C:(cg + 1) * C], kr)
                    rows(lambda r_: kb.tt(MM[r_, c0_:c0_ + cpb, :, :],
                                          bm[r_, 0:cpb * 4 * C].re("p (c s t) -> p c s t", c=cpb, s=4),
                                          V(m4.ap[r_, :, :].unsqueeze(1).to_broadcast([r_.stop - r_.start, cpb, 4, C]), m4.deps),
                                          ALU.mult))
                XY = [XYt[0][:, 0:2 * G * C].re("p (x c t) -> p x c t", x=2, c=G),
                      XYt[1][:, 0:2 * G * C].re("p (x c t) -> p x c t", x=2, c=G)]
                bn = fb()
                for c in range(G):
                    cg = g * G + c
                    for h2 in range(2):
                        pb = 64 * h2
                        kb.mm(bn[pb:pb + C, c * C:(c + 1) * C], KR4[pb:pb + 64, cg, 0, :], Bt[pb:pb + 64, cg * C:(cg + 1) * C])
                rows(lambda r_: kb.tt(XY[0][r_, 1, :, :], bn[r_, 0:G * C].re("p (c t) -> p c t", c=G),
                                      V(mL.ap[r_, 0:C].unsqueeze(1).to_broadcast([r_.stop - r_.start, G, C]), mL.deps), ALU.mult))
                rows(lambda r_: kb.copy(XY[0][r_, 0, :, :], MM[r_, :, 0, :]))
                rows(lambda r_: kb.tt(TT[r_, :, :, :], XY[0][r_, :, :, :],
                                      V(identD.ap[r_, 0:C].unsqueeze(1).unsqueeze(1).to_broadcast([r_.stop - r_.start, 2, G, C]),
                                        identD.deps), ALU.add))
                toks = []
                for src, dst in ((xv(hp), Vt), (Bt[:, 0:N], BtT), (Kt[:, 0:N], KtT)):
                    tp = fb()
                    for c in range(G):
                        cg = g * G + c
                        for h2 in range(2):
                            pb = 64 * h2
                            kb.mm(tp[pb:pb + C, c * 64:(c + 1) * 64], src[pb:pb + 64, cg * C:(cg + 1) * C],
                                  ident[pb:pb + 64, pb:pb + 64])
                    rows(lambda r_, tp=tp, dst=dst: kb.copy(dst[r_, 0:G * 64], tp[r_, 0:G * 64], e="act"))
                    toks.append(dst[:, 0:G * 64].re("p (c i) -> p c i", c=G))
                Vt3, BtT3, KtT3 = toks
                for k_ in range(1, KK + 1):
                    cur, nxt = XY[(k_ - 1) % 2], XY[k_ % 2]
                    last = k_ == KK
                    bk = fb()
                    for c in range(G):
                        for h2 in range(2):
                            pb = 64 * h2
                            kb.mm(bk[pb:pb + C, c * C:(c + 1) * C], cur[pb:pb + C, 1, c, :], cur[pb:pb + C, 0, c, :])
                            if not last:
                                kb.mm(bk[pb:pb + C, (G + c) * C:(G + c + 1) * C], cur[pb:pb + C, 0, c, :], cur[pb:pb + C, 1, c, :])
                    nx = 1 if last else 2
                    rows(lambda r_: kb.copy(nxt[r_, 0:nx, :, :], bk[r_, 0:nx * G * C].re("p (x c t) -> p x c t", x=nx, c=G)))
                    bt_ = fb()
                    for c in range(G):
                        for h2 in range(2):
                            pb = 64 * h2
                            kb.mm(bt_[pb:pb + C, c * C:(c + 1) * C], TT[pb:pb + C, 1, c, :], nxt[pb:pb + C, 0, c, :])
                            if not last:
                                kb.mm(bt_[pb:pb + C, (G + c) * C:(G + c + 1) * C], nxt[pb:pb + C, 0, c, :], TT[pb:pb + C, 1, c, :])
                    rows(lambda r_: kb.tt(TT[r_, 0:nx, :, :], TT[r_, 0:nx, :, :],
                                          bt_[r_, 0:nx * G * C].re("p (x c t) -> p x c t", x=nx, c=G), ALU.add))
                for c in range(G):
                    cg = g * G + c
                    if isP:
                        ST0 = st["ST"][hp][st["parh"][hp]]
                        STn = st["ST"][hp][1 - st["parh"][hp]]
                    else:
                        ST0 = SS["ST"][cg][hp][0]
                        STn = SS["ST"][cg][hp][1]
                    bw = fb()
                    for h2 in range(2):
                        pb = 64 * h2
                        kb.mm(bw[pb:pb + C, 0:64], MM[pb:pb + C, c, 2, :], Vt3[pb:pb + C, c, :], start=True, stop=False)
                        kb.mm(bw[pb:pb + C, 0:64], KR4[pb:pb + 64, cg, 0, :], ST0[pb:pb + 64, :], start=False, stop=True)
                    rows(lambda r_: kb.copy(Wsb[r_, :], bw[r_, 0:64], e="act"))
                    bu = fb()
                    for h2 in range(2):
                        pb = 64 * h2
                        kb.mm(bu[pb:pb + C, 0:64], TT[pb:pb + C, 0, c, :], Wsb[pb:pb + C, :])
                    rows(lambda r_: kb.copy(Usb[r_, :], bu[r_, 0:64]))
                    for h2 in range(2):
                        pb = 64 * h2
                        yo = yb[pb:pb + 64, hp * 256 + cg * C:hp * 256 + (cg + 1) * C]
                        kb.mm(yo, ST0[pb:pb + 64, :], KR4[pb:pb + 64, cg, 1, :], start=True, stop=False)
                        kb.mm(yo, Usb[pb:pb + C, :], MM[pb:pb + C, c, 1, :], start=False, stop=False)
                        kb.mm(yo, Vt3[pb:pb + C, c, :], MM[pb:pb + C, c, 3, :], start=False, stop=True)
                    bs = fb()
                    for h2 in range(2):
                        pb = 64 * h2
                        kb.mm(bs[pb:pb + 64, 0:64], BtT3[pb:pb + C, c, :], Usb[pb:pb + C, :], start=True, stop=False)
                        kb.mm(bs[pb:pb + 64, 0:64], KtT3[pb:pb + C, c, :], Vt3[pb:pb + C, c, :], start=False, stop=True)
                    kb.tt(STn.v(), bs[:, 0:64], ST0.v(), ALU.add)
                    kb.ts(STn.v(), STn.v(), pCt[:, cg:cg + 1], ALU.mult)
                    if isP:
                        st["parh"][hp] = 1 - st["parh"][hp]
                    else:
                        SS["STpar"][cg] = 1
        if isP:
            st["par"] = st["parh"][0]
        for hp in range(2):
            kb.copy(ysb[:, hp, 0:N], yb[:, hp * 256:hp * 256 + N], e="act")
        for i in range(2):
            y = ysb[:, i, 0:N]
            b = nb()
            kb.mm(b[:, 0:N], bo.v(), y)
            yc = S()
            kb.stt(yc[:, 0:N], b[:, 0:N], -1.0 / 64, y, ALU.mult, ALU.add)
            sq = S()
            kb.tt(sq[:, 0:N], yc[:, 0:N], yc[:, 0:N], ALU.mult)
            b = nb()
            kb.mm(b[:, 0:N], bo.v(), sq[:, 0:N])
            kb.act(sq[:, 0:N], b[:, 0:N], AF.Sqrt, bias=eps_col(64 * 1e-5), scale=1.0 / 64)
            kb.recip(sq[:, 0:N], sq[:, 0:N])
            kb.tt(yc[:, 0:N], yc[:, 0:N], sq[:, 0:N], ALU.mult)
            kb.ts(yc[:, 0:N], yc[:, 0:N], p["gnw"][:, i:i + 1], ALU.mult, s2=p["gnb"][:, i:i + 1], op1=ALU.add)
            rk_ = S()
            kb.tt(rk_[:, 0:N], xr(i), R["k"][i][:, 0:N], ALU.mult)
            kb.ts(rk_[:, 0:N], rk_[:, 0:N], p["rkc"][:, i:i + 1], ALU.mult)
            b = nb()
            kb.mm(b[:, 0:N], bo.v(), rk_[:, 0:N])
            kb.tt(rk_[:, 0:N], b[:, 0:N], xv(i), ALU.mult)
            kb.tt(yc[:, 0:N], yc[:, 0:N], rk_[:, 0:N], ALU.add)
            kb.tt(o_bf[:, 4 + i, 0:N], yc[:, 0:N], sg[:, 4 + i, 0:N], ALU.mult)

    def load_x(src_ap, N):
        for tt_ in range(N // 128):
            for half in range(2):
                b = nb()
                for i in range(4):
                    xs_ = S()
                    k_ = half * 4 + i
                    ld(xs_[:, 0:128], src_ap[tt_ * 128:(tt_ + 1) * 128, k_ * 128:(k_ + 1) * 128])
                    kb.tr(b[:, i * 128:(i + 1) * 128], xs_[:, 0:128], ident.v())
                kb.copy(xT[:, half * 4:half * 4 + 4, tt_ * 128:(tt_ + 1) * 128], b.v().re("p (k t) -> p k t", k=4),
                        e=("act" if half else "dve"))

    def store_y(dst_ap, N):
        for tt_ in range(N // 128):
            for half in range(2):
                b = nb()
                for i in range(4):
                    kb.tr(b[:, i * 128:(i + 1) * 128], xT[:, half * 4 + i, tt_ * 128:(tt_ + 1) * 128], ident.v())
                kb.copy(stg.v(), b.v(), e="act")
                kb.dma("sp", V(dst_ap[tt_ * 128:(tt_ + 1) * 128, half * 512:(half + 1) * 512], []), stg.v())

    def drive():
        begin_phase(cfg.NCH, True)
        for u in range(cfg.TP // cfg.NCH):
            U = Unit("p", u * cfg.NCH, cfg.NCH, 1, cfg.NCH)
            ckpt(1)
            load_x(din["x_prompt"].ap[U.tok0:U.tok0 + cfg.NCH, :], cfg.NCH)
            for l in range(L):
                layer(l, U, None)
            store_y(dout["y_p"].ap[U.tok0:U.tok0 + cfg.NCH, :], cfg.NCH)
        ckpt(9)
        for l in range(L):
            st = PS[l]
            fm_to_rows([st["convst"][:, i, j:j + 1] for j in range(3) for i in range(2)], dout["conv_p"].ap[l], 1)
            fm_to_rows([st["lruh"][:, i:i + 1] for i in range(2)], dout["lru_p"].ap[l], 1)
            fm_to_rows([st["shst"][:, i:i + 1] for i in range(10)], dout["shift_p"].ap[l], 1)
            ssc = sb("ssc%d" % l, [128, 8, 2])
            kb.copy(ssc[:, :, 0], st["ssr"].v())
            kb.copy(ssc[:, :, 1], st["ssi"].v())
            kb.dma("sp", V(dout["ssm_p"].ap[l, 0].rearrange("(j p c) -> p j c", j=8, c=2), []), ssc.v())
            for hp in range(2):
                b = nb()
                kb.tr(b[0:64, 0:128], st["ST"][hp][st["par"]].v(), ident.v())
                kb.copy(stg[0:64, 0:128], b[0:64, 0:128], e="act")
                kb.dma("sp", V(dout["wkv_p"].ap[l, 0, hp * 128:(hp + 1) * 128, :].rearrange("(h i) j -> i h j", h=2), []),
                       stg[0:64, 0:128].re("p (h j) -> p h j", h=2))

        ckpt(10)
        end_phase()
        begin_phase(cfg.NS, False)
        SB_, Tq = cfg.SB, cfg.TS
        NS = cfg.NS
        SS = {}
        idx_i = sb("idx_i", [128, SB_ * cfg.NPG], I32)
        ld(idx_i.v(), din["page_table"].ap[0:1, :].to_broadcast([128, SB_ * cfg.NPG]))
        SS["idx"] = []
        for l in range(L):
            idx = sb("idx%d" % l, [128, SB_ * cfg.NPG], I32)
            kb.ts(idx.v(), idx_i.v(), 128.0, ALU.mult, s2=iop[:, 0:1], op1=ALU.add)
            if l > 0:
                kb.ts(idx.v(), idx.v(), float(l * cfg.POOL * 128), ALU.add)
            SS["idx"].append(idx)
        SS["kpg"] = [sb("kpg%d" % i, [128, 4, 256]) for i in range(2)]
        SS["vpg"] = [sb("vpg%d" % i, [128, 4, 256]) for i in range(2)]
        SS["kTg"] = [sb("kTg%d" % i, [128, 2, 512], BF16) for i in range(2)]
        SS["kTc"] = sb("kTc", [128, 2, NS], BF16)
        SS["wv"] = sb("wv", [128, 8, 256], BF16)
        SS["vb"] = [sb("vb%d" % i, [Tq, 256]) for i in range(2)]
        SS["PTs"] = sb("PTs", [128, 8 * 4 * Tq], BF16)
        SS["vpb"] = [sb("vpb%d" % i, [128, 4, 256], BF16) for i in range(2)]
        SS["PTr"] = sb("PTr", [128, 8 * Tq])
        SS["qS"] = sb("qS", [128, 2, SB_, 2, Tq], BF16)
        SS["PTc"] = sb("PTc", [Tq, 8 * Tq])
        SS["conv"] = sb("s_conv", [128, 2, SB_, 3])
        SS["lru"] = sb("s_lru", [128, 2, SB_])
        SS["lru_o"] = sb("s_lru_o", [128, 2, SB_])
        SS["shift"] = sb("s_shift", [128, 10, SB_])
        SS["shift_o"] = sb("s_shift_o", [128, 10, SB_])
        SS["ssr"] = sb("s_ssr", [128, 8, SB_])
        SS["ssi"] = sb("s_ssi", [128, 8, SB_])
        SS["ssr_o"] = sb("s_ssr_o", [128, 8, SB_])
        SS["ssi_o"] = sb("s_ssi_o", [128, 8, SB_])
        SS["ST"] = [[[sb("sST%d_%d_%d" % (b_, hp, pp), [128, 64]) for pp in range(2)] for hp in range(2)] for b_ in range(SB_)]
        SS["STpar"] = [0] * SB_
        SS["s0"] = [sb("s0_%d" % i, [64, 256]) for i in range(2)]
        rows = sb("rows", [SB_, 2048])
        U = Unit("s", 0, NS, SB_, Tq)

        def rows_to_fm(dst_fn, src_ap, ncols, stride=1, off=0):
            ld(rows[:, 0:ncols * 128 * stride], src_ap)
            for i in range(ncols):
                b = nb()
                src = rows[:, i * 128 * stride:(i + 1) * 128 * stride]
                if stride > 1:
                    src = src.re("p (n c) -> p n c", c=stride)[:, :, off]
                kb.tr(b[:, 0:SB_], src, ident[0:SB_, 0:SB_])
                kb.copy(dst_fn(i), b[:, 0:SB_], e="act")

        load_x(din["x_sample"].ap, NS)
        ckpt(11)
        for l in range(L):
            for j in range(3):
                rows_to_fm(lambda i, j=j: SS["conv"][:, i, :, j], din["state_conv"].ap[l][:, j * 256:(j + 1) * 256], 2)
            rows_to_fm(lambda i: SS["lru"][:, i, :], din["state_lru"].ap[l], 2)
            rows_to_fm(lambda i: SS["shift"][:, i, :], din["state_shift"].ap[l], 10)
            rows_to_fm(lambda i: SS["ssr"][:, i, :], din["state_ssm"].ap[l], 8, stride=2, off=0)
            rows_to_fm(lambda i: SS["ssi"][:, i, :], din["state_ssm"].ap[l], 8, stride=2, off=1)
            for b_ in range(SB_):
                s0 = SS["s0"][b_ % 2]
                ld(s0[0:64, 0:256].re("p (h j) -> p h j", h=4), din["state_wkv"].ap[l, b_].rearrange("(h i) j -> i h j", h=4))
                for hp in range(2):
                    b = nb()
                    kb.tr(b[:, 0:64], s0[0:64, hp * 128:(hp + 1) * 128], ident[0:64, 0:64])
                    kb.copy(SS["ST"][b_][hp][0].v(), b[:, 0:64], e="act")
            ckpt(12)
            layer(l, U, SS)
            ckpt(13)
            fm_to_rows([xp[:, i, 0:NS + 3 * SB_].re("p (b t) -> p b t", b=SB_)[:, :, Tq + j] for j in range(3) for i in range(2)],
                       dout["conv_s"].ap[l], SB_)
            fm_to_rows([SS["lru_o"][:, i, :] for i in range(2)], dout["lru_s"].ap[l], SB_)
            fm_to_rows([SS["shift_o"][:, i, :] for i in range(10)], dout["shift_s"].ap[l], SB_)
            for g0 in range(0, 8, 2):
                b = nb()
                for jj in range(2):
                    for c_, src in ((0, SS["ssr_o"]), (1, SS["ssi_o"])):
                        kb.tr(b[0:SB_, (jj * 2 + c_) * 128:(jj * 2 + c_ + 1) * 128], src[:, g0 + jj, :], ident.v())
                kb.copy(stg[0:SB_, 0:512].re("p (j n c) -> p j n c", j=2, c=2),
                        b[0:SB_, 0:512].re("p (j c n) -> p j n c", j=2, c=2), e="act")
                kb.dma("sp", V(dout["ssm_s"].ap[l][:, g0 * 256:(g0 + 2) * 256], []), stg[0:SB_, 0:512])
            for b_ in range(SB_):
                for hp in range(2):
                    b = nb()
                    kb.tr(b[0:64, 0:128], SS["ST"][b_][hp][SS["STpar"][b_]].v(), ident.v())
                    kb.copy(stg[0:64, 0:128], b[0:64, 0:128], e="act")
                    kb.dma("sp", V(dout["wkv_s"].ap[l, b_, hp * 128:(hp + 1) * 128, :].rearrange("(h i) j -> i h j", h=2), []),
                           stg[0:64, 0:128].re("p (h j) -> p h j", h=2))
        store_y(dout["y_s"].ap, NS)
        end_phase()

    try:
        drive()
    except StopBuild:
        pass
    kb.finish()
    kb.ck_log = ck_log
    ctx.__exit__(None, None, None)
    return nc, kb


def shard_inputs(inputs, cfg, ncores):
    maps = []
    L = cfg.DEPTH
    for c in range(ncores):
        sb0, sb1 = c * cfg.SB, (c + 1) * cfg.SB
        m = {}
        m["x_prompt"] = np.ascontiguousarray(inputs["x_prompt"][c])
        m["x_sample"] = np.ascontiguousarray(inputs["x_sample"][sb0:sb1]).reshape(cfg.NS, D_MODEL)
        m["cache_k"] = inputs["cache_k"].reshape(L, cfg.POOL * 128, 256)
        m["cache_v"] = inputs["cache_v"].reshape(L, cfg.POOL * 128, 256)
        m["page_table"] = np.ascontiguousarray(inputs["page_table"][sb0:sb1]).reshape(1, cfg.SB * cfg.NPG).astype(np.int32)
        m["state_conv"] = np.ascontiguousarray(inputs["state_conv"][:, sb0:sb1]).reshape(L, cfg.SB, 768)
        m["state_lru"] = np.ascontiguousarray(inputs["state_lru"][:, sb0:sb1])
        m["state_shift"] = np.ascontiguousarray(inputs["state_shift"][:, sb0:sb1])
        m["state_wkv"] = np.ascontiguousarray(inputs["state_wkv"][:, sb0:sb1]).reshape(L, cfg.SB, 256, 64)
        m["state_ssm"] = np.ascontiguousarray(inputs["state_ssm"][:, sb0:sb1]).reshape(L, cfg.SB, 2048)
        for n, f, dt in IN_SPECS[10:]:
            a = np.asarray(inputs[n])
            if n == "rw_rk":
                a = a.reshape(L, 256)
            m[n] = np.ascontiguousarray(a)
        maps.append(m)
    return maps


def gather_outputs(res, cfg, ncores):
    L = cfg.DEPTH
    cat = lambda n, ax: np.concatenate([r[n] for r in res], axis=ax)
    B = ncores
    y_p = np.stack([r["y_p"] for r in res]).reshape(B, cfg.TP, D_MODEL)
    y_s = cat("y_s", 0).reshape(B * cfg.SB, cfg.TS, D_MODEL)
    k_p = np.stack([r["k_p"] for r in res], axis=1).reshape(L, B, cfg.TP, 4, 64)
    v_p = np.stack([r["v_p"] for r in res], axis=1).reshape(L, B, cfg.TP, 4, 64)
    k_s = cat("k_s", 1).reshape(L, B * cfg.SB, cfg.TS, 4, 64)
    v_s = cat("v_s", 1).reshape(L, B * cfg.SB, cfg.TS, 4, 64)
    conv_p = cat("conv_p", 1).reshape(L, B, 3, 256)
    conv_s = cat("conv_s", 1).reshape(L, B * cfg.SB, 3, 256)
    lru_p = cat("lru_p", 1).reshape(L, B, 256)
    lru_s = cat("lru_s", 1).reshape(L, B * cfg.SB, 256)
    shift_p = cat("shift_p", 1).reshape(L, B, 1280)
    shift_s = cat("shift_s", 1).reshape(L, B * cfg.SB, 1280)
    wkv_p = cat("wkv_p", 1).reshape(L, B, 4, 64, 64)
    wkv_s = cat("wkv_s", 1).reshape(L, B * cfg.SB, 4, 64, 64)
    ssm_p = cat("ssm_p", 1).reshape(L, B, 16, 64, 2)
    ssm_s = cat("ssm_s", 1).reshape(L, B * cfg.SB, 16, 64, 2)
    return (y_p, y_s, k_p, k_s, v_p, v_s, conv_p, conv_s, lru_p, lru_s, shift_p, shift_s, wkv_p, wkv_s, ssm_p, ssm_s)


def kernel(**inputs):
    ncores = 8
    cfg = Cfg()
    inputs = {k: np.asarray(v) for k, v in inputs.items()}
    nc, kb = build(cfg)
    maps = shard_inputs(inputs, cfg, ncores)
    res = run_bass_kernel_spmd(nc, maps, core_ids=list(range(ncores)))
    outs = gather_outputs(res.results, cfg, ncores)
    return tuple(np.ascontiguousarray(o, dtype=np.float32) for o in outs)
```

```python
import contextlib
import math
import numpy as np
import concourse.bass as bass
import concourse.mybir as mybir
from concourse.bass_utils import run_bass_kernel_spmd

F32 = mybir.dt.float32
BF16 = mybir.dt.bfloat16
I32 = mybir.dt.int32
AF = mybir.ActivationFunctionType
ALU = mybir.AluOpType
AX = mybir.AxisListType

D_MODEL = 1024
D_IN = 3584
WB = 256
NORM_EPS = 1e-6
TWO_PI = 2.0 * math.pi


class Dep:
    __slots__ = ("w", "r")

    def __init__(self):
        self.w = None
        self.r = {}


class V:
    __slots__ = ("ap", "deps")

    def __init__(self, ap, deps):
        self.ap = ap
        self.deps = deps

    def __getitem__(self, idx):
        return V(self.ap[idx], self.deps)

    def bc(self, shape):
        return V(self.ap.to_broadcast(list(shape)), self.deps)

    def re(self, pattern_, **kw):
        return V(self.ap.rearrange(pattern_, **kw), self.deps)


class T:
    def __init__(self, kb, name, shape, dtype, space="sbuf"):
        nc = kb.nc
        if space == "sbuf":
            self.h = kb.cur.enter_context(nc.sbuf_tensor(name, list(shape), dtype))
        elif space == "psum":
            self.h = kb.es.enter_context(nc.psum_tensor(name, list(shape), dtype))
        else:
            self.h = nc.dram_tensor(name, list(shape), dtype, kind=space)
        self.ap = self.h.ap() if hasattr(self.h, "ap") else self.h[:]
        self.shape = list(shape)
        self.deps = [Dep()]

    def __getitem__(self, idx):
        return V(self.ap[idx], self.deps)

    def v(self):
        return V(self.ap, self.deps)


class Ref:
    def __init__(self):
        self.t = None

    def __getitem__(self, idx):
        return self.t[idx]

    def v(self):
        return self.t.v()

    @property
    def ap(self):
        return self.t.ap

    @property
    def deps(self):
        return self.t.deps


class KB:
    NSLOT = 8

    def __init__(self, nc):
        self.nc = nc
        self.es = contextlib.ExitStack()
        self.cur = self.es
        self.eng = {"pe": nc.tensor, "act": nc.scalar, "dve": nc.vector, "pool": nc.gpsimd, "sp": nc.sync}
        self.sem = {}
        self.cnt = {}
        self.seen = {e: {} for e in self.eng}
        self.epoch = {}
        self.ekey = {}
        for e in self.eng:
            key = (e, 0)
            self.sem[key] = self.es.enter_context(nc.semaphore("s_" + e))
            self.cnt[key] = 0
            self.epoch[e] = 0
            self.ekey[e] = key
        self.dq = {}
        for q in ("sp", "pool"):
            sl = []
            for i in range(self.NSLOT):
                key = ("d", q, i)
                self.sem[key] = self.es.enter_context(nc.semaphore("d_%s_%d" % (q, i)))
                self.cnt[key] = 0
                sl.append(key)
            self.dq[q] = [sl, 0]
        self.n_instr = 0
        self.n_wait = 0

    def _need(self, waits, tok):
        if tok is None:
            return
        k, v = tok
        if waits.get(k, 0) < v:
            waits[k] = v

    def _emit_waits(self, e, waits):
        for k, v in waits.items():
            if e == "pe" and k[0] == "pe":
                continue
            if self.seen[e].get(k, 0) >= v:
                continue
            self.eng[e].wait_ge(self.sem[k], v)
            self.seen[e][k] = v
            self.n_wait += 1

    def _collect(self, reads, writes):
        waits = {}
        for v in reads:
            if v is None:
                continue
            for d in v.deps:
                self._need(waits, d.w)
        for v in writes:
            if v is None:
                continue
            for d in v.deps:
                self._need(waits, d.w)
                for k, val in d.r.items():
                    self._need(waits, (k, val))
        return waits

    def _record(self, tok, reads, writes):
        k, val = tok
        for v in reads:
            if v is None:
                continue
            for d in v.deps:
                if d.r.get(k, 0) < val:
                    d.r[k] = val
        for v in writes:
            if v is None:
                continue
            for d in v.deps:
                d.w = tok
                d.r = {}

    EPOCH = 16000

    def op(self, e, fn, reads, writes):
        waits = self._collect(reads, writes)
        key = self.ekey[e]
        if self.cnt[key] >= self.EPOCH:
            self.epoch[e] += 1
            key = (e, self.epoch[e])
            self.sem[key] = self.es.enter_context(self.nc.semaphore("s_%s_%d" % (e, self.epoch[e])))
            self.cnt[key] = 0
            self.ekey[e] = key
        self._emit_waits(e, waits)
        ins = fn(self.eng[e])
        self.cnt[key] += 1
        ins.then_inc(self.sem[key], 1)
        self._record((key, self.cnt[key]), reads, writes)
        self.n_instr += 1
        return ins

    def dma(self, q, out, in_, fn=None):
        sl, idx = self.dq[q]
        key = sl[idx % self.NSLOT]
        self.dq[q][1] = idx + 1
        waits = self._collect([in_], [out])
        if self.cnt[key] > 0:
            self._need(waits, (key, self.cnt[key]))
        self._emit_waits(q, waits)
        if fn is None:
            ins = self.eng[q].dma_start(out=out.ap, in_=in_.ap)
        else:
            ins = fn(self.eng[q])
        self.cnt[key] += 16
        ins.then_inc(self.sem[key], 16)
        self._record((key, self.cnt[key]), [in_], [out])
        self.n_instr += 1
        return ins

    def barrier(self):
        allw = {k: v for k, v in self.cnt.items() if v > 0}
        for e in ("pe", "act", "dve", "pool", "sp"):
            for k, v in allw.items():
                if k[0] == e and e in ("pe", "sp"):
                    continue
                if self.seen[e].get(k, 0) >= v:
                    continue
                self.eng[e].wait_ge(self.sem[k], v)
                self.seen[e][k] = v
                self.n_wait += 1

    def finish(self):
        self.barrier()

    def mm(self, out, lhsT, rhs, start=True, stop=True, **kw):
        return self.op("pe", lambda E: E.matmul(out.ap, lhsT.ap, rhs.ap, start=start, stop=stop, **kw),
                       [lhsT, rhs] + ([] if start else [out]), [out])

    def tr(self, out, in_, ident):
        return self.op("pe", lambda E: E.transpose(out.ap, in_.ap, ident.ap), [in_, ident], [out])

    def act(self, out, in_, func, bias=None, scale=None):
        kw = {}
        rd = [in_]
        if bias is not None:
            if isinstance(bias, V):
                kw["bias"] = bias.ap
                rd.append(bias)
            else:
                kw["bias"] = bias
        if scale is not None:
            if isinstance(scale, V):
                kw["scale"] = scale.ap
                rd.append(scale)
            else:
                kw["scale"] = scale
        return self.op("act", lambda E: E.activation(out.ap, in_.ap, func, **kw), rd, [out])

    def ts(self, out, in0, s1, op0, s2=None, op1=None, e="dve"):
        rd = [in0]
        a1 = s1
        if isinstance(s1, V):
            a1 = s1.ap
            rd.append(s1)
        a2 = s2
        if isinstance(s2, V):
            a2 = s2.ap
            rd.append(s2)
        kw = {}
        if op1 is not None:
            kw["op1"] = op1
        return self.op(e, lambda E: E.tensor_scalar(out.ap, in0.ap, a1, a2, op0, **kw), rd, [out])

    def tt(self, out, in0, in1, op, e="dve"):
        return self.op(e, lambda E: E.tensor_tensor(out.ap, in0.ap, in1.ap, op), [in0, in1], [out])

    def stt(self, out, in0, s, in1, op0, op1):
        rd = [in0, in1]
        a = s
        if isinstance(s, V):
            a = s.ap
            rd.append(s)
        return self.op("dve", lambda E: E.scalar_tensor_tensor(out.ap, in0.ap, a, in1.ap, op0, op1), rd, [out])

    def scan(self, out, d0, d1, init):
        rd = [d0, d1]
        a = init
        if isinstance(init, V):
            a = init.ap
            rd.append(init)
        return self.op("dve", lambda E: E.tensor_tensor_scan(out.ap, d0.ap, d1.ap, a, ALU.mult, ALU.add), rd, [out])

    def copy(self, out, in_, e="dve"):
        if e == "act":
            return self.op("act", lambda E: E.copy(out.ap, in_.ap), [in_], [out])
        return self.op(e, lambda E: E.tensor_copy(out.ap, in_.ap), [in_], [out])

    def memset(self, out, val, e="dve"):
        return self.op(e, lambda E: E.memset(out.ap, val), [], [out])

    def recip(self, out, in_):
        return self.op("dve", lambda E: E.reciprocal(out.ap, in_.ap), [in_], [out])

    def reduce(self, out, in_, op=ALU.add, axis=AX.X):
        return self.op("dve", lambda E: E.tensor_reduce(out.ap, in_.ap, axis, op), [in_], [out])


class Cfg:
    def __init__(self, TP=2048, NCH=256, SB=16, TS=8, NPG=16, POOL=2560, DEPTH=2):
        self.TP, self.NCH, self.SB, self.TS, self.NPG, self.POOL, self.DEPTH = TP, NCH, SB, TS, NPG, POOL, DEPTH
        self.NS = SB * TS
        self.SC = 128
        self.STOP = 0
        self.SA = 0
        self.NORWKV = 0


IN_SPECS = [
    ("x_prompt", lambda c: [c.TP, D_MODEL], F32), ("x_sample", lambda c: [c.NS, D_MODEL], F32),
    ("cache_k", lambda c: [c.DEPTH, c.POOL * 128, 256], F32), ("cache_v", lambda c: [c.DEPTH, c.POOL * 128, 256], F32),
    ("page_table", lambda c: [1, c.SB * c.NPG], I32),
    ("state_conv", lambda c: [c.DEPTH, c.SB, 3 * WB], F32), ("state_lru", lambda c: [c.DEPTH, c.SB, WB], F32),
    ("state_shift", lambda c: [c.DEPTH, c.SB, 5 * WB], F32), ("state_wkv", lambda c: [c.DEPTH, c.SB, 4 * 64, 64], F32),
    ("state_ssm", lambda c: [c.DEPTH, c.SB, 2048], F32),
    ("norm_pre", lambda c: [c.DEPTH, D_MODEL], F32), ("norm_post", lambda c: [c.DEPTH, D_MODEL], F32),
    ("w_in", lambda c: [c.DEPTH, D_MODEL, D_IN], F32), ("w_out", lambda c: [c.DEPTH, D_MODEL, D_MODEL], F32),
    ("lam_q1", lambda c: [c.DEPTH, 32], F32), ("lam_k1", lambda c: [c.DEPTH, 32], F32),
    ("lam_q2", lambda c: [c.DEPTH, 32], F32), ("lam_k2", lambda c: [c.DEPTH, 32], F32),
    ("subln_w", lambda c: [c.DEPTH, 64], F32), ("conv_w", lambda c: [c.DEPTH, 4, WB], F32),
    ("conv_b", lambda c: [c.DEPTH, WB], F32), ("lru_wa", lambda c: [c.DEPTH, 4, 64, 64], F32),
    ("lru_ba", lambda c: [c.DEPTH, WB], F32), ("lru_wx", lambda c: [c.DEPTH, 4, 64, 64], F32),
    ("lru_bx", lambda c: [c.DEPTH, WB], F32), ("lru_lam", lambda c: [c.DEPTH, WB], F32),
    ("rw_mu", lambda c: [c.DEPTH, 5 * WB], F32), ("rw_w0", lambda c: [c.DEPTH, WB], F32),
    ("rw_w1", lambda c: [c.DEPTH, WB, 32], F32), ("rw_w2", lambda c: [c.DEPTH, 32, WB], F32),
    ("rw_a0", lambda c: [c.DEPTH, WB], F32), ("rw_a1", lambda c: [c.DEPTH, WB, 32], F32),
    ("rw_a2", lambda c: [c.DEPTH, 32, WB], F32), ("rw_kk", lambda c: [c.DEPTH, WB], F32),
    ("rw_ka", lambda c: [c.DEPTH, WB], F32), ("rw_rk", lambda c: [c.DEPTH, WB], F32),
    ("rw_gnw", lambda c: [c.DEPTH, WB], F32), ("rw_gnb", lambda c: [c.DEPTH, WB], F32),
    ("s5_lre", lambda c: [c.DEPTH, 16, 64], F32), ("s5_lim", lambda c: [c.DEPTH, 16, 64], F32),
    ("s5_logdt", lambda c: [c.DEPTH, 16], F32), ("s5_bre", lambda c: [c.DEPTH, 16, 64, 16], F32),
    ("s5_bim", lambda c: [c.DEPTH, 16, 64, 16], F32), ("s5_cre", lambda c: [c.DEPTH, 16, 16, 64], F32),
    ("s5_cim", lambda c: [c.DEPTH, 16, 16, 64], F32), ("s5_d", lambda c: [c.DEPTH, WB], F32),
    ("s5_wglu", lambda c: [c.DEPTH, WB, WB], F32), ("s5_bglu", lambda c: [c.DEPTH, WB], F32),
]

OUT_SPECS = [
    ("y_p", lambda c: [c.TP, D_MODEL]), ("y_s", lambda c: [c.NS, D_MODEL]),
    ("k_p", lambda c: [c.DEPTH, c.TP, 256]), ("k_s", lambda c: [c.DEPTH, c.NS, 256]),
    ("v_p", lambda c: [c.DEPTH, c.TP, 256]), ("v_s", lambda c: [c.DEPTH, c.NS, 256]),
    ("conv_p", lambda c: [c.DEPTH, 1, 768]), ("conv_s", lambda c: [c.DEPTH, c.SB, 768]),
    ("lru_p", lambda c: [c.DEPTH, 1, 256]), ("lru_s", lambda c: [c.DEPTH, c.SB, 256]),
    ("shift_p", lambda c: [c.DEPTH, 1, 1280]), ("shift_s", lambda c: [c.DEPTH, c.SB, 1280]),
    ("wkv_p", lambda c: [c.DEPTH, 1, 256, 64]), ("wkv_s", lambda c: [c.DEPTH, c.SB, 256, 64]),
    ("ssm_p", lambda c: [c.DEPTH, 1, 2048]), ("ssm_s", lambda c: [c.DEPTH, c.SB, 2048]),
]


class StopBuild(Exception):
    pass


class Unit:
    def __init__(self, kind, tok0, N, nseq, Tq):
        self.kind, self.tok0, self.N, self.nseq, self.T = kind, tok0, N, nseq, Tq


def build(cfg):
    nc = bass.Bass("TRN2", target_bir_lowering=False)
    kb = KB(nc)
    ctx = nc.allow_non_contiguous_dma(reason="small strided parameter / state loads")
    ctx.__enter__()
    L = cfg.DEPTH
    NMAX = max(cfg.NCH, cfg.NS)
    din = {n: T(kb, n, f(cfg), dt, space="ExternalInput") for n, f, dt in IN_SPECS}
    dout = {n: T(kb, n, f(cfg), F32, space="ExternalOutput") for n, f in OUT_SPECS}

    NBLK = {"w_in": D_IN // 512, "w_out": D_MODEL // 512}
    wbf = {w_: T(kb, "wbf_" + w_, [L, 128, NBLK[w_], 8, 512], BF16, space="Internal") for w_ in ("w_in", "w_out")}
    for l_ in range(L):
        for kt_ in range(8):
            for which_ in ("w_in", "w_out"):
                nb_ = NBLK[which_]
                for b0_ in range(0, nb_, 4):
                    b1_ = min(nb_, b0_ + 4)
                    src_ = din[which_].ap[l_, kt_ * 128:(kt_ + 1) * 128, b0_ * 512:b1_ * 512].rearrange("p (b c) -> p b c", c=512)
                    kb.dma("pool", wbf[which_][l_, :, b0_:b1_, kt_, :], V(src_, []))

    def dv(name):
        return V(din[name].ap, [])

    def sb(name, shape, dt=F32):
        return T(kb, name, shape, dt)

    def ld(dst, src_ap, q="sp"):
        kb.dma(q, dst, V(src_ap, []))

    banks = [T(kb, "bank%d" % i, [128, 512], F32, space="psum") for i in range(8)]
    rot = [0]

    def nb():
        b = banks[rot[0] % 3]
        rot[0] += 1
        return b

    iop = sb("iop", [128, 1])
    iof = sb("iof", [128, 256])
    kb.op("pool", lambda E: E.iota(iop.ap, [[0, 1]], base=0, channel_multiplier=1,
                                   allow_small_or_imprecise_dtypes=True), [], [iop.v()])
    kb.op("pool", lambda E: E.iota(iof.ap, [[1, 256]], base=0, channel_multiplier=0,
                                   allow_small_or_imprecise_dtypes=True), [], [iof.v()])
    ident = sb("ident", [128, 128])
    kb.ts(ident.v(), iof[:, 0:128], iop[:, 0:1], ALU.is_equal)
    ones_f = sb("ones_f", [128, 128])
    kb.memset(ones_f.v(), 1.0)
    ones_b = sb("ones_b", [128, 128], BF16)
    kb.memset(ones_b.v(), 1.0)
    zeros_f = sb("zeros_f", [128, 512], BF16)
    kb.memset(zeros_f.v(), 0.0)
    NQT = cfg.NCH // 128
    mk = sb("mk", [128, NQT, cfg.NCH], BF16)
    for i in range(NQT):
        kb.ts(mk[:, i, :], iof[:, 0:cfg.NCH], float(-128 * i), ALU.add, s2=iop[:, 0:1], op1=ALU.is_ge)
    bo = sb("bo", [128, 128])
    rowb = sb("rowb", [128, 1])
    kb.ts(rowb.v(), iop.v(), 64.0, ALU.is_ge)
    kb.ts(bo.v(), iof[:, 0:128], 64.0, ALU.is_ge, s2=rowb[:, 0:1], op1=ALU.is_equal)
    mc1 = sb("mc1", [128, 1])
    mc0 = sb("mc0", [128, 1])
    kb.ts(mc1.v(), rowb.v(), -64.0, ALU.mult, s2=iop[:, 0:1], op1=ALU.add)
    kb.ts(mc1.v(), mc1.v(), 32.0, ALU.is_ge)
    kb.ts(mc0.v(), mc1.v(), -1.0, ALU.mult, s2=1.0, op1=ALU.add)
    mcs = [mc0, mc1]
    pm64 = sb("pm64", [128, 1])
    kb.ts(pm64.v(), rowb.v(), -64.0, ALU.mult, s2=iop[:, 0:1], op1=ALU.add)
    mL = sb("mL", [128, 64])
    kb.ts(mL.v(), iof[:, 0:64], pm64[:, 0:1], ALU.is_lt)
    identD = sb("identD", [128, 64])
    kb.ts(identD.v(), iof[:, 0:64], pm64[:, 0:1], ALU.is_equal)
    m4P = sb("m4P", [128, 4, 64])
    m4S = sb("m4S", [128, 4, cfg.TS])
    for s_ in range(4):
        op_ = ALU.is_gt if s_ % 2 == 0 else ALU.is_ge
        kb.ts(m4P[:, s_, :], iof[:, 0:64], pm64[:, 0:1], op_)
        kb.ts(m4S[:, s_, :], iof[:, 0:cfg.TS], pm64[:, 0:1], op_)
    pat01 = sb("pat01", [128, cfg.SB, cfg.TS])
    kb.memset(pat01.v(), 1.0)
    kb.memset(pat01[:, :, 0:1], 0.0)

    NSCR = 12
    scr = [Ref() for i in range(NSCR)]
    scr_i = [0]

    def S(n=None):
        t = scr[scr_i[0] % NSCR]
        scr_i[0] += 1
        return t

    small_i = [0]

    def small(shape, dt=F32):
        small_i[0] += 1
        return sb("sm%d" % small_i[0], shape, dt)

    P = []
    wblkf = [sb("wblk%d" % i, [128, 2048]) for i in range(2)]
    wblk = [V(w_.ap.bitcast(BF16).rearrange("p (k c) -> p k c", k=8), w_.deps) for w_ in wblkf]
    ps_a = wblkf[0][:, 0:cfg.SC]
    ps_b = wblkf[0][:, cfg.SC:2 * cfg.SC]
    ps_ki = V(wblkf[1].ap.bitcast(I32)[:, 0:cfg.SC], wblkf[1].deps)
    SC = cfg.SC
    for l in range(L):
        p = {}
        lam_init = 0.8 - 0.6 * math.exp(-0.3 * l)

        def col(name, src, ncol):
            t = small([128, ncol])
            ld(t.v(), din[src].ap[l].rearrange("(i p) -> p i", p=128))
            p[name] = t
            return t

        col("gpre", "norm_pre", 8)
        col("gpost", "norm_post", 8)
        t = small([128, 64])
        ld(t.v(), din["subln_w"].ap[l:l + 1, :].to_broadcast([128, 64]))
        kb.ts(t.v(), t.v(), 1.0 - lam_init, ALU.mult)
        p["subln"] = t
        ee = []
        for qn, kn in (("lam_q1", "lam_k1"), ("lam_q2", "lam_k2")):
            a = small([128, 32])
            b = small([128, 32])
            ld(a.v(), din[qn].ap[l:l + 1, :].to_broadcast([128, 32]))
            ld(b.v(), din[kn].ap[l:l + 1, :].to_broadcast([128, 32]))
            kb.tt(a.v(), a.v(), b.v(), ALU.mult)
            s = small([128, 1])
            kb.reduce(s.v(), a.v())
            kb.act(s.v(), s.v(), AF.Exp)
            ee.append(s)
        nlam = small([128, 1])
        kb.tt(nlam.v(), ee[1].v(), ee[0].v(), ALU.subtract)
        kb.ts(nlam.v(), nlam.v(), -lam_init, ALU.add)
        p["nlam"] = nlam
        cw = small([128, 2, 4])
        for j in range(4):
            ld(cw[:, :, j], din["conv_w"].ap[l, j].rearrange("(i p) -> p i", p=128))
        p["cw"] = cw
        col("cb", "conv_b", 2)
        for nm, src in (("lwa", "lru_wa"), ("lwx", "lru_wx")):
            t = small([128, 2, 128])
            kb.memset(t.v(), 0.0)
            for n in range(4):
                r0 = (n % 2) * 64
                ld(t[r0:r0 + 64, n // 2, r0:r0 + 64], din[src].ap[l, n])
            p[nm] = t
        col("lba", "lru_ba", 2)
        col("lbx", "lru_bx", 2)
        ll = col("llam", "lru_lam", 2)
        sp_ = small([128, 2])
        kb.act(sp_.v(), ll.v(), AF.Exp, scale=-1.0)
        kb.act(sp_.v(), sp_.v(), AF.Ln, bias=1.0)
        p["m8"] = small([128, 2])
        p["m16"] = small([128, 2])
        kb.ts(p["m8"].v(), sp_.v(), -8.0, ALU.mult)
        kb.ts(p["m16"].v(), sp_.v(), -16.0, ALU.mult)
        col("mu", "rw_mu", 10)
        w0 = col("w0", "rw_w0", 2)
        p["nw0"] = small([128, 2])
        kb.ts(p["nw0"].v(), w0.v(), -1.0, ALU.mult)
        col("a0", "rw_a0", 2)
        col("kkc", "rw_kk", 2)
        ka = col("kac", "rw_ka", 2)
        p["omka"] = small([128, 2])
        kb.ts(p["omka"].v(), ka.v(), -1.0, ALU.mult, s2=1.0, op1=ALU.add)
        col("rkc", "rw_rk", 2)
        col("gnw", "rw_gnw", 2)
        col("gnb", "rw_gnb", 2)
        for nm, src in (("w1", "rw_w1"), ("a1", "rw_a1")):
            t = small([128, 2, 32])
            ld(t.v(), din[src].ap[l].rearrange("(t p) r -> p t r", p=128))
            p[nm] = t
        for nm, src in (("w2", "rw_w2"), ("a2", "rw_a2")):
            t = small([32, 256])
            ld(t.v(), din[src].ap[l])
            p[nm] = t
        lre = small([128, 8])
        lim = small([128, 8])
        ld(lre.v(), din["s5_lre"].ap[l].rearrange("(j g) n -> (g n) j", g=2))
        ld(lim.v(), din["s5_lim"].ap[l].rearrange("(j g) n -> (g n) j", g=2))
        dt_ = small([128, 8])
        ldt = din["s5_logdt"].ap[l].rearrange("(j g) -> g j", g=2)
        for g2 in range(2):
            ld(dt_[g2 * 64:(g2 + 1) * 64, :], ldt[g2:g2 + 1, :].to_broadcast([64, 8]))
        kb.act(dt_.v(), dt_.v(), AF.Exp)
        mag = small([128, 8])
        kb.tt(mag.v(), lre.v(), dt_.v(), ALU.mult)
        kb.act(mag.v(), mag.v(), AF.Exp)
        p["mag"] = mag
        th = small([128, 8])
        kb.tt(th.v(), lim.v(), dt_.v(), ALU.mult)
        cs = small([128, 8, SC])
        sn = small([128, 8, SC])
        ki = ps_ki
        for j in range(8):
            for tab, shift in ((sn, 0.0), (cs, math.pi / 2)):
                a = ps_a
                b = ps_b
                kb.ts(a, iof[:, 0:SC], th[:, j:j + 1], ALU.mult, s2=shift, op1=ALU.add)
                kb.ts(b, a, 1.0 / TWO_PI, ALU.mult)
                kb.copy(ki, b)
                kb.copy(b, ki)
                kb.stt(a, b, -TWO_PI, a, ALU.mult, ALU.add)
                kb.ts(b, a, math.pi, ALU.is_gt, s2=-TWO_PI, op1=ALU.mult)
                kb.tt(a, a, b, ALU.add)
                kb.ts(b, a, -math.pi, ALU.is_lt, s2=TWO_PI, op1=ALU.mult)
                kb.tt(a, a, b, ALU.add)
                kb.ts(a, a, math.pi, ALU.min, s2=-math.pi, op1=ALU.max)
                kb.act(tab[:, j, :], a, AF.Sin)
        p["cs"], p["sn"] = cs, sn
        c1 = small([128, 8])
        s1 = small([128, 8])
        kb.copy(c1.v(), cs[:, :, 1])
        kb.copy(s1.v(), sn[:, :, 1])
        ns1 = small([128, 8])
        kb.ts(ns1.v(), s1.v(), -1.0, ALU.mult)
        p["c1"], p["s1"], p["ns1"] = c1, s1, ns1
        abre = small([128, 8])
        abim = small([128, 8])
        kb.tt(abre.v(), mag.v(), c1.v(), ALU.mult)
        kb.tt(abim.v(), mag.v(), s1.v(), ALU.mult)
        den = small([128, 8])
        t2 = small([128, 8])
        kb.tt(den.v(), lre.v(), lre.v(), ALU.mult)
        kb.tt(t2.v(), lim.v(), lim.v(), ALU.mult)
        kb.tt(den.v(), den.v(), t2.v(), ALU.add)
        kb.recip(den.v(), den.v())
        pr = small([128, 8])
        kb.ts(pr.v(), abre.v(), -1.0, ALU.add)
        fre = small([128, 8])
        fim = small([128, 8])
        kb.tt(fre.v(), pr.v(), lre.v(), ALU.mult)
        kb.tt(t2.v(), abim.v(), lim.v(), ALU.mult)
        kb.tt(fre.v(), fre.v(), t2.v(), ALU.add)
        kb.tt(fre.v(), fre.v(), den.v(), ALU.mult)
        kb.tt(fim.v(), abim.v(), lre.v(), ALU.mult)
        kb.tt(t2.v(), pr.v(), lim.v(), ALU.mult)
        kb.tt(fim.v(), fim.v(), t2.v(), ALU.subtract)
        kb.tt(fim.v(), fim.v(), den.v(), ALU.mult)
        nfre = small([128, 8])
        kb.ts(nfre.v(), fre.v(), -1.0, ALU.mult)
        tA = small([128, 8, SC])
        tB = small([128, 8, SC])
        for j in range(8):
            kb.ts(tA[:, j, :], cs[:, j, :], fre[:, j:j + 1], ALU.mult)
            kb.stt(tA[:, j, :], sn[:, j, :], fim[:, j:j + 1], tA[:, j, :], ALU.mult, ALU.add)
            kb.ts(tB[:, j, :], cs[:, j, :], fim[:, j:j + 1], ALU.mult)
            kb.stt(tB[:, j, :], sn[:, j, :], nfre[:, j:j + 1], tB[:, j, :], ALU.mult, ALU.add)
        p["tA"], p["tB"] = tA, tB
        for nm, src in (("Bre", "s5_bre"), ("Bim", "s5_bim")):
            t = small([128, 2, 128])
            kb.memset(t.v(), 0.0)
            for g in range(16):
                r0 = (g % 8) * 16
                c0 = (g % 2) * 64
                ld(t[r0:r0 + 16, g // 8, c0:c0 + 64], din[src].ap[l, g].rearrange("n c -> c n"))
            p[nm] = t
        for nm, src in (("Cre", "s5_cre"), ("nCim", "s5_cim")):
            t = small([128, 8, 64])
            kb.memset(t.v(), 0.0)
            for g in range(16):
                r0 = (g % 2) * 64
                c0 = ((g // 2) % 2) * 32 + (g % 2) * 16
                ld(t[r0:r0 + 64, g // 2, c0:c0 + 16], din[src].ap[l, g].rearrange("c n -> n c"))
            if nm == "nCim":
                kb.ts(t.v(), t.v(), -1.0, ALU.mult)
            p[nm] = t
        col("dcol", "s5_d", 2)
        col("bglu", "s5_bglu", 2)
        t = small([128, 2, 256])
        ld(t.v(), din["s5_wglu"].ap[l].rearrange("(t p) m -> p t m", p=128))
        p["wglu"] = t
        P.append(p)

    ck_n = [0]
    ck_log = []

    def ckpt(k):
        ck_n[0] += 1
        ck_log.append((ck_n[0], k, kb.n_instr))
        if cfg.STOP == ck_n[0]:
            raise StopBuild()

    PS = []
    for l in range(L):
        st = {"convst": small([128, 2, 3]), "lruh": small([128, 2]), "shst": small([128, 10]),
              "ssr": small([128, 8]), "ssi": small([128, 8]),
              "ST": [[small([128, 64]) for _ in range(2)] for _ in range(2)], "par": 0, "parh": [0, 0]}
        for k_ in ("convst", "lruh", "shst", "ssr", "ssi"):
            kb.memset(st[k_].v(), 0.0)
        for hp in range(2):
            kb.memset(st["ST"][hp][0].v(), 0.0)
        PS.append(st)

    N_ = NMAX
    xT, hT, xp, pcs, u_t, sg, o_bf, big1, rstd_t, ysb, stg, oa_tm = [Ref() for _ in range(12)]
    qTs = [Ref(), Ref()]
    u_m = [Ref(), Ref()]
    kvst = [Ref(), Ref()]
    kvst_i = [0]
    PT = [Ref(), Ref()]
    pt_i = [0]
    KRt, Vt, BtT, KtT, Wsb, Usb, pCt = [Ref() for _ in range(7)]
    XYt = [Ref(), Ref()]
    phase_prompt = [True]
    Rt = [Ref() for _ in range(8)]
    acs = [Ref() for _ in range(3)]
    wb_i = [0]
    NKT = cfg.TP // 128
    Kh = [Ref() for l in range(L)]
    Vh = [Ref() for l in range(L)]
    phase_n = [0]

    def begin_phase(NA, prompt):
        kb.ph = contextlib.ExitStack()
        kb.cur = kb.ph
        phase_n[0] += 1
        tg = "_%d" % phase_n[0]
        for i in range(NSCR):
            scr[i].t = sb("scr%d" % i + tg, [128, NA])
        xT.t = sb("xT" + tg, [128, 8, NA])
        hT.t = sb("hT" + tg, [128, 8, NA], BF16)
        for c in range(2):
            qTs[c].t = sb("qT%d" % c + tg, [128, 2, NA], BF16)
            u_m[c].t = sb("u_m%d" % c + tg, [128, 2, NA])
            kvst[c].t = sb("kvst%d" % c + tg, [128, 256])
            PT[c].t = sb("PT%d" % c + tg, [128, NA], BF16)
        xp.t = sb("xp" + tg, [128, 2, NA + 3 * cfg.SB])
        pcs.t = sb("pcs" + tg, [128, 10, NA + cfg.SB])
        sg.t = sb("sg" + tg, [128, 8, NA], BF16)
        o_bf.t = sb("o_bf" + tg, [128, 8, NA], BF16)
        big1.t = sb("big1" + tg, [128, 10, NA])
        rstd_t.t = sb("rstd_t" + tg, [128, NA])
        ysb.t = sb("ysb" + tg, [128, 2, NA])
        stg.t = sb("stg" + tg, [128, 512])
        oa_tm.t = sb("oa_tm" + tg, [128, 4, 64])
        for i in range(8):
            Rt[i].t = sb("Rt%d" % i + tg, [128, NA])
        phase_prompt[0] = prompt
        if not prompt:
            for i in range(3):
                acs[i].t = sb("acs%d" % i + tg, [128, 256])
        Cc = 64 if prompt else cfg.TS
        Gc = (NA // Cc) if prompt else min(8, NA // Cc)
        KRt.t = sb("KRt" + tg, [128, 2 * NA])
        for i in range(2):
            XYt[i].t = sb("XYt%d" % i + tg, [128, 2 * Gc * Cc])
        Vt.t = sb("Vt" + tg, [128, Gc * 64])
        BtT.t = sb("BtT" + tg, [128, Gc * 64])
        KtT.t = sb("KtT" + tg, [128, Gc * 64])
        Wsb.t = sb("Wsb" + tg, [128, 64])
        Usb.t = sb("Usb" + tg, [128, 64])
        pCt.t = sb("pCt" + tg, [128, max(4, NA // Cc)])
        if prompt:
            for l in range(L):
                Kh[l].t = sb("Kh%d" % l, [128, 2, cfg.TP], BF16)
                Vh[l].t = sb("Vh%d" % l, [128, NKT, 4, 65], BF16)
                kb.memset(Vh[l][:, :, :, 64:65], 1.0)

    def end_phase():
        kb.barrier()
        kb.ph.close()
        kb.cur = kb.es
        eps_cache.clear()
        stmp.clear()

    def load_weights_block(l, which, blk):
        t = wblk[wb_i[0] % 2]
        wb_i[0] += 1
        src = wbf[which].ap[l, :, blk, :, :]
        kb.dma("sp", t[:, :, :], V(src, wbf[which].deps))
        return t

    def fm_to_rows(srcs, dst_ap, M):
        for g0 in range(0, len(srcs), 4):
            grp = srcs[g0:g0 + 4]
            b = nb()
            for i, s_ in enumerate(grp):
                kb.tr(b[0:M, i * 128:(i + 1) * 128], s_, ident.v())
            kb.copy(stg[0:M, 0:len(grp) * 128], b[0:M, 0:len(grp) * 128], e="act")
            kb.dma("sp", V(dst_ap[:, g0 * 128:(g0 + len(grp)) * 128], []), stg[0:M, 0:len(grp) * 128])

    def rms_stats(src_fn, n_t, N, scale, eps):
        b = nb()
        for kt in range(n_t):
            sq = S()
            sqb = V(sq.ap.bitcast(BF16)[:, 0:N], sq.deps)
            kb.act(sqb, src_fn(kt), AF.Square)
            kb.mm(b[:, 0:N], ones_b.v(), sqb, start=(kt == 0), stop=(kt == n_t - 1))
        kb.act(rstd_t[:, 0:N], b[:, 0:N], AF.Sqrt, bias=eps_col(eps), scale=scale)
        kb.recip(rstd_t[:, 0:N], rstd_t[:, 0:N])

    eps_cache = {}

    def eps_col(val):
        if val not in eps_cache:
            t = small([128, 1])
            kb.memset(t.v(), val)
            eps_cache[val] = t
        return eps_cache[val].v()

    def layer(l, U, SS):
        p = P[l]
        N, nseq, Tq = U.N, U.nseq, U.T
        isP = U.kind == "p"
        st = PS[l]

        def v3(view):
            return view.re("p (b t) -> p b t", b=nseq)

        rms_stats(lambda kt: xT[:, kt, 0:N], 8, N, 1.0 / D_MODEL, NORM_EPS)
        for kt in range(8):
            kb.stt(hT[:, kt, 0:N], xT[:, kt, 0:N], p["gpre"][:, kt:kt + 1], rstd_t[:, 0:N], ALU.mult, ALU.mult)

        XW = N + 3 * nseq
        PW = N + nseq
        xp3 = [xp[:, i, 0:XW].re("p (b t) -> p b t", b=nseq) for i in range(2)]
        pc3 = [pcs[:, i, 0:PW].re("p (b t) -> p b t", b=nseq) for i in range(10)]
        ev = [0]

        def evac(dst, src, func=None):
            ev[0] += 1
            if func is not None:
                kb.act(dst, src, func)
            elif ev[0] % 2 == 0:
                kb.copy(dst, src, e="act")
            else:
                kb.copy(dst, src)

        kT_cur = Kh[l] if isP else SS["kTc"]
        koff = U.tok0 if isP else 0
        for blk in range(7):
            wt = load_weights_block(l, "w_in", blk)
            for mm_ in range(4):
                m = blk * 4 + mm_
                if 4 <= m < 6:
                    continue
                b = nb()
                for kt in range(8):
                    kb.mm(b[:, 0:N], wt[:, kt, mm_ * 128:(mm_ + 1) * 128], hT[:, kt, 0:N], start=(kt == 0), stop=(kt == 7))
                src = b[:, 0:N]
                if m < 2:
                    for c in range(2):
                        kb.ts(qTs[c][:, m, 0:N], src, mcs[c][:, 0:1], ALU.mult)
                        if not isP:
                            kb.ts(SS["qS"][:, m, :, c, :], v3(src), mcs[c][:, 0:1], ALU.mult)
                elif m < 4:
                    evac(kT_cur[:, m - 2, koff:koff + N], src)
                elif m < 6:
                    pass
                elif m < 8:
                    evac(xp3[m - 6][:, :, 3:3 + Tq], v3(src))
                elif m < 18:
                    evac(pc3[m - 8][:, :, 1:1 + Tq], v3(src))
                elif m < 20:
                    for c in range(2):
                        kb.ts(u_m[c][:, m - 18, 0:N], src, mcs[c][:, 0:1], ALU.mult)
                else:
                    evac(sg[:, m - 20, 0:N], src, func=AF.Silu)
            if blk in (0, 1) and isP:
                for tt_ in range(N // 128):
                    b = nb()
                    c0 = 256 if blk == 0 else 0
                    for kt in range(8):
                        kb.mm(b[:, 0:256], hT[:, kt, tt_ * 128:(tt_ + 1) * 128], wt[:, kt, c0:c0 + 256],
                              start=(kt == 0), stop=(kt == 7))
                    ks = kvst[kvst_i[0] % 2]
                    kvst_i[0] += 1
                    kb.copy(ks.v(), b[:, 0:256], e="act")
                    tok = U.tok0 + tt_ * 128
                    kb.dma("sp", V(dout["k_p" if blk == 0 else "v_p"].ap[l, tok:tok + 128, :], []), ks.v())
                    if blk == 1:
                        kb.copy(Vh[l][:, tok // 128, :, 0:64], ks.v().re("p (h d) -> p h d", h=4))
            if blk == 0 and not isP:
                for b_ in range(nseq):
                    b = nb()
                    for kt in range(8):
                        kb.mm(b[0:Tq, 0:256], hT[:, kt, b_ * Tq:(b_ + 1) * Tq], wt[:, kt, 256:512],
                              start=(kt == 0), stop=(kt == 7))
                    ks = kvst[kvst_i[0] % 2]
                    kvst_i[0] += 1
                    kb.copy(ks[0:Tq, :], b[0:Tq, 0:256], e="act")
                    kb.dma("sp", V(dout["k_s"].ap[l, b_ * Tq:(b_ + 1) * Tq, :], []), ks[0:Tq, :])
            if blk == 1 and not isP:
                kb.copy(SS["wv"].v(), wt[:, :, 0:256])

        ckpt(3)
        inv_sqrt = 32.0 ** -0.5
        if isP:
            nq = N // 128
            accs = [[banks[3 + qt * 2 + hh] for hh in range(2)] for qt in range(nq)]
            for qt in range(nq):
                for hh in range(2):
                    kb.mm(accs[qt][hh].v(), zeros_f[:, 0:128], zeros_f[:, 0:512], start=True, stop=False,
                          skip_group_check=True)
            nkt = (U.tok0 + N) // 128
            kt0 = U.tok0 // 128
            its = [(h, c, kt) for h in range(4) for c in range(2) for kt in range(nkt)]

            def score(it_):
                h, c, kt = it_
                pb = (h % 2) * 64
                b = nb()
                kb.mm(b[:, 0:N], Kh[l][pb:pb + 64, h // 2, kt * 128:(kt + 1) * 128], qTs[c][pb:pb + 64, h // 2, 0:N])
                return b

            pend = score(its[0])
            for ii_, (h, c, kt) in enumerate(its):
                b = pend
                if ii_ + 1 < len(its):
                    pend = score(its[ii_ + 1])
                pt = PT[pt_i[0] % 2]
                pt_i[0] += 1
                kb.act(pt[:, 0:N], b[:, 0:N], AF.Exp, scale=inv_sqrt)
                if kt >= kt0:
                    kb.tt(pt[:, 0:N], pt[:, 0:N], mk[:, kt - kt0, 0:N], ALU.mult)
                for qt in range(nq):
                    if kt > kt0 + qt:
                        continue
                    slot = (h % 2) * 2 + c
                    kb.mm(accs[qt][h // 2][:, slot * 65:(slot + 1) * 65], pt[:, qt * 128:(qt + 1) * 128],
                          Vh[l][:, kt, h, :], start=False, stop=False, skip_group_check=True)
            for qt in range(nq):
                attn_combine(l, [accs[qt][0], accs[qt][1]], 128, o_bf, qt * 128)
        else:
            sample_attention(l, U, SS)

        ckpt(4)
        for i in range(2):
            if isP:
                kb.copy(xp3[i][:, :, 0:3], st["convst"][:, i:i + 1, :])
            else:
                kb.copy(xp3[i][:, :, 0:3], SS["conv"][:, i, :, :])
            xc = S()
            xc3 = v3(xc[:, 0:N])
            kb.ts(xc3, xp3[i][:, :, 0:Tq], p["cw"][:, i, 0:1], ALU.mult, s2=p["cb"][:, i:i + 1], op1=ALU.add)
            for j in range(1, 4):
                kb.stt(xc3, xp3[i][:, :, j:j + Tq], p["cw"][:, i, j:j + 1], xc3, ALU.mult, ALU.add)
            if isP:
                kb.copy(st["convst"][:, i:i + 1, :], xp3[i][:, :, Tq:Tq + 3])
            r_ = S()
            ig = S()
            b = nb()
            kb.mm(b[:, 0:N], p["lwa"][:, i, :], xc[:, 0:N])
            kb.act(r_[:, 0:N], b[:, 0:N], AF.Sigmoid, bias=p["lba"][:, i:i + 1])
            b = nb()
            kb.mm(b[:, 0:N], p["lwx"][:, i, :], xc[:, 0:N])
            kb.act(ig[:, 0:N], b[:, 0:N], AF.Sigmoid, bias=p["lbx"][:, i:i + 1])
            a_ = S()
            a2 = S()
            kb.act(a_[:, 0:N], r_[:, 0:N], AF.Exp, scale=p["m8"][:, i:i + 1])
            kb.act(a2[:, 0:N], r_[:, 0:N], AF.Exp, scale=p["m16"][:, i:i + 1])
            kb.ts(a2[:, 0:N], a2[:, 0:N], -1.0, ALU.mult, s2=1.0, op1=ALU.add)
            kb.ts(a2[:, 0:N], a2[:, 0:N], 1e-30, ALU.max)
            kb.act(a2[:, 0:N], a2[:, 0:N], AF.Sqrt)
            kb.tt(a2[:, 0:N], a2[:, 0:N], ig[:, 0:N], ALU.mult)
            kb.tt(a2[:, 0:N], a2[:, 0:N], xc[:, 0:N], ALU.mult)
            h_ = S()
            if isP:
                kb.scan(h_[:, 0:N], a_[:, 0:N], a2[:, 0:N], st["lruh"][:, i:i + 1])
                kb.copy(st["lruh"][:, i:i + 1], h_[:, N - 1:N])
            else:
                a3, b3 = v3(a_[:, 0:N]), v3(a2[:, 0:N])
                tmp = S()
                kb.tt(tmp[:, 0:nseq], a3[:, :, 0], SS["lru"][:, i, :], ALU.mult)
                kb.tt(b3[:, :, 0], b3[:, :, 0], tmp[:, 0:nseq], ALU.add)
                kb.memset(a3[:, :, 0:1], 0.0)
                kb.scan(h_[:, 0:N], a_[:, 0:N], a2[:, 0:N], 0.0)
                kb.copy(SS["lru_o"][:, i, :], v3(h_[:, 0:N])[:, :, Tq - 1])
            kb.tt(o_bf[:, 2 + i, 0:N], h_[:, 0:N], sg[:, 2 + i, 0:N], ALU.mult)

        ckpt(5)
        s5_branch(l, U, SS)
        ckpt(6)

        rwkv_branch(l, U, SS, pc3)
        ckpt(7)

        z = big1
        for blk in range(2):
            wt = load_weights_block(l, "w_out", blk)
            for mm_ in range(4):
                m = blk * 4 + mm_
                b = nb()
                for kt in range(8):
                    kb.mm(b[:, 0:N], wt[:, kt, mm_ * 128:(mm_ + 1) * 128], o_bf[:, kt, 0:N], start=(kt == 0), stop=(kt == 7))
                evac(z[:, m, 0:N], b[:, 0:N])
        rms_stats(lambda kt: z[:, kt, 0:N], 8, N, 1.0 / D_MODEL, NORM_EPS)
        for kt in range(8):
            kb.stt(z[:, kt, 0:N], z[:, kt, 0:N], p["gpost"][:, kt:kt + 1], rstd_t[:, 0:N], ALU.mult, ALU.mult)
            kb.tt(xT[:, kt, 0:N], xT[:, kt, 0:N], z[:, kt, 0:N], ALU.add)
        ckpt(8)

    def attn_combine(l, accl, M, dst, col0):
        p = P[l]
        for hh in range(2):
            acc = accl[hh]
            a4 = acc[0:M, 0:260].re("p (s d) -> p s d", s=4)
            rc = small_tmp([128, 4])
            kb.recip(rc[0:M, :], a4[:, :, 64])
            on = S() if phase_prompt[0] else acs[hh]
            on4 = on[0:M, 0:256].re("p (s d) -> p s d", s=4)
            kb.tt(on4, a4[:, :, 0:64], V(rc.ap[0:M, :].unsqueeze(2).to_broadcast([M, 4, 64]), rc.deps), ALU.mult)
            on5 = on[0:M, 0:256].re("p (h c d) -> p h c d", h=2, c=2)
            kb.stt(oa_tm[0:M, hh * 2:hh * 2 + 2, :], on5[:, :, 1, :], p["nlam"][0:M, 0:1], on5[:, :, 0, :], ALU.mult, ALU.add)
        sq = S() if phase_prompt[0] else acs[2]
        sq3 = sq[0:M, 0:256].re("p (h d) -> p h d", h=4)
        kb.tt(sq3, oa_tm[0:M, :, :], oa_tm[0:M, :, :], ALU.mult)
        ss = small_tmp([128, 4])
        kb.reduce(ss[0:M, :], sq3)
        kb.act(ss[0:M, :], ss[0:M, :], AF.Sqrt, bias=eps_col(NORM_EPS)[0:M, :], scale=1.0 / 64)
        kb.recip(ss[0:M, :], ss[0:M, :])
        kb.tt(oa_tm[0:M, :, :], oa_tm[0:M, :, :], V(ss.ap[0:M, :].unsqueeze(2).to_broadcast([M, 4, 64]), ss.deps), ALU.mult)
        kb.tt(oa_tm[0:M, :, :], oa_tm[0:M, :, :],
              V(p["subln"].ap[0:M, :].unsqueeze(1).to_broadcast([M, 4, 64]), p["subln"].deps), ALU.mult)
        b = nb()
        for i in range(2):
            kb.tr(b[:, i * 128:i * 128 + M], oa_tm[0:M, 2 * i:2 * i + 2, :].re("p h d -> p (h d)"), ident[0:M, 0:M])
        for i in range(2):
            kb.tt(dst[:, i, col0:col0 + M], b[:, i * 128:i * 128 + M], sg[:, i, col0:col0 + M], ALU.mult)

    stmp = {}

    def small_tmp(shape):
        key = tuple(shape)
        if key not in stmp:
            stmp[key] = [[small(shape) for _ in range(4)], 0]
        lst = stmp[key]
        t = lst[0][lst[1] % 4]
        lst[1] += 1
        return t

    def sample_attention(l, U, SS):
        p = P[l]
        Tq = U.T
        NPG = cfg.NPG
        G = min(4, NPG)
        NG = NPG // G
        inv_sqrt = 32.0 ** -0.5
        ck = din["cache_k"].ap.rearrange("l r c -> (l r) c")
        cvv = din["cache_v"].ap.rearrange("l r c -> (l r) c")
        groups = [(b_, gi_) for b_ in range(U.nseq) for gi_ in range(NG)]
        accO = [banks[3], banks[4]]
        accD = banks[5]

        def stageT(n_):
            b_, gi_ = groups[n_]
            g0 = gi_ * G
            kpg = SS["kpg"][n_ % 2]
            vpg = SS["vpg"][n_ % 2]
            for pg in range(G):
                ic = b_ * NPG + g0 + pg
                kb.dma("pool", kpg[:, pg, :], V(ck, SS["idx"][l].deps),
                       fn=lambda E, pg=pg, ic=ic, kpg=kpg: E.indirect_dma_start(
                           out=kpg.ap[:, pg, :], out_offset=None, in_=ck,
                           in_offset=bass.IndirectOffsetOnAxis(ap=SS["idx"][l].ap[:, ic:ic + 1], axis=0)))
                kb.dma("pool", vpg[:, pg, :], V(cvv, SS["idx"][l].deps),
                       fn=lambda E, pg=pg, ic=ic, vpg=vpg: E.indirect_dma_start(
                           out=vpg.ap[:, pg, :], out_offset=None, in_=cvv,
                           in_offset=bass.IndirectOffsetOnAxis(ap=SS["idx"][l].ap[:, ic:ic + 1], axis=0)))
            vpb = SS["vpb"][n_ % 2]
            kb.copy(vpb[:, 0:G, :], vpg[:, 0:G, :])
            kTg = SS["kTg"][n_ % 2]
            for tl in range(2):
                b = nb()
                for pg in range(G):
                    kb.tr(b[:, pg * 128:(pg + 1) * 128], kpg[:, pg, tl * 128:(tl + 1) * 128], ident.v())
                kb.copy(kTg[:, tl, 0:G * 128], b[:, 0:G * 128], e=("act" if tl else "dve"))

        def stageSP(n_):
            b_, gi_ = groups[n_]
            kTg = SS["kTg"][n_ % 2]
            vpb = SS["vpb"][n_ % 2]
            if gi_ == 0:
                for a in accO + [accD]:
                    kb.mm(a.v(), zeros_f[:, 0:128], zeros_f[:, 0:512], start=True, stop=False, skip_group_check=True)
            bsc = [nb(), nb()]
            for h in range(4):
                pb = (h % 2) * 64
                for pg in range(G):
                    c0 = ((h // 2) * G + pg) * 2 * Tq
                    kb.mm(bsc[h % 2][:, c0:c0 + 2 * Tq], kTg[pb:pb + 64, h // 2, pg * 128:(pg + 1) * 128],
                          SS["qS"][pb:pb + 64, h // 2, b_, :, :].re("p c q -> p (c q)"))
            ptt = SS["PTs"]
            HW_ = 4 * G * Tq
            for hs in range(2):
                kb.act(ptt[:, hs * HW_:(hs + 1) * HW_], bsc[hs][:, 0:HW_], AF.Exp, scale=inv_sqrt)
            pt6 = ptt[:, 0:8 * G * Tq].re("p (a g c q) -> p a g c q", a=4, g=G, c=2)
            pts = SS["PTr"]
            kb.reduce(pts[:, 0:8 * Tq].re("p (a c q) -> p a c q", a=4, c=2),
                      ptt[:, 0:8 * G * Tq].re("p (a g c q) -> p a c q g", a=4, g=G, c=2))
            for hc in range(8):
                h = hc // 2
                si = (h % 2) * 4 + (h // 2) * 2 + (hc % 2)
                kb.mm(accD[0:Tq, hc:hc + 1], pts[:, si * Tq:(si + 1) * Tq], ones_f[:, 0:1], start=False, stop=False,
                      skip_group_check=True)
                for pg in range(G):
                    slot = hc % 4
                    kb.mm(accO[h // 2][0:Tq, slot * 65:slot * 65 + 64], pt6[:, (h % 2) * 2 + h // 2, pg, hc % 2, :],
                          vpb[:, pg, h * 64:(h + 1) * 64], start=False, stop=False, skip_group_check=True)
            if gi_ == NG - 1:
                tail(b_)

        def tail(b_):
            b = nb()
            for kt in range(8):
                kb.mm(b[0:Tq, 0:256], hT[:, kt, b_ * Tq:(b_ + 1) * Tq], SS["wv"][:, kt, :], start=(kt == 0), stop=(kt == 7))
            vb = SS["vb"][b_ % 2]
            kb.copy(vb.v(), b[0:Tq, 0:256], e="act")
            kb.dma("sp", V(dout["v_s"].ap[l, b_ * Tq:(b_ + 1) * Tq, :], []), vb.v())
            bsc = [nb(), nb()]
            for h in range(4):
                for c in range(2):
                    pb = (h % 2) * 64
                    sl_ = (h // 2) * 2 + c
                    kb.mm(bsc[h % 2][0:Tq, sl_ * Tq:(sl_ + 1) * Tq], SS["kTc"][pb:pb + 64, h // 2, b_ * Tq:(b_ + 1) * Tq],
                          qTs[c][pb:pb + 64, h // 2, b_ * Tq:(b_ + 1) * Tq])
            ptc = SS["PTc"]
            for hs in range(2):
                kb.act(ptc[0:Tq, hs * 4 * Tq:(hs + 1) * 4 * Tq], bsc[hs][0:Tq, 0:4 * Tq], AF.Exp, scale=inv_sqrt)
            pc3_ = ptc[0:Tq, 0:8 * Tq].re("p (s q) -> p s q", s=8)
            kb.tt(pc3_, pc3_, V(m4P.ap[0:Tq, 1, 0:Tq].unsqueeze(1).to_broadcast([Tq, 8, Tq]), m4P.deps), ALU.mult)
            for hc in range(8):
                h = hc // 2
                slot = hc % 4
                si = (h % 2) * 4 + (h // 2) * 2 + (hc % 2)
                kb.mm(accD[0:Tq, hc:hc + 1], ptc[0:Tq, si * Tq:(si + 1) * Tq], ones_f[0:Tq, 0:1], start=False, stop=False,
                      skip_group_check=True)
                kb.mm(accO[h // 2][0:Tq, slot * 65:slot * 65 + 64], ptc[0:Tq, si * Tq:(si + 1) * Tq],
                      vb[0:Tq, h * 64:(h + 1) * 64], start=False, stop=False, skip_group_check=True)
            for hh in range(2):
                a4 = accO[hh][0:Tq, 0:260].re("p (s d) -> p s d", s=4)
                kb.copy(a4[:, :, 64], accD[0:Tq, hh * 4:hh * 4 + 4])
            attn_combine(l, accO, Tq, o_bf, b_ * Tq)

        stageT(0)
        for n_ in range(len(groups)):
            if n_ + 1 < len(groups):
                stageT(n_ + 1)
            stageSP(n_)

    def s5_prompt(l, U, psY):
        p = P[l]
        st = PS[l]
        N = U.N
        SCn = cfg.SC
        H = []
        for i in range(7):
            t_ = scr[i]
            for hf in range(2):
                d_ = Dep()
                d_.w = t_.deps[0].w
                d_.r = dict(t_.deps[0].r)
                H.append(V(t_.ap[:, hf * 128:(hf + 1) * 128], [d_]))
        its = [(c0, j) for c0 in range(0, N, SCn) for j in range(8)]

        def stageA(n_):
            c0, j = its[n_]
            ct = j // 4
            wb_ = 64 * ((j % 4) // 2)
            um = u_m[j % 2]
            bA, bB = nb(), nb()
            kb.mm(bA[:, 0:SCn], p["Bre"][wb_:wb_ + 64, ct, :], um[wb_:wb_ + 64, ct, c0:c0 + SCn])
            kb.mm(bB[:, 0:SCn], p["Bim"][wb_:wb_ + 64, ct, :], um[wb_:wb_ + 64, ct, c0:c0 + SCn])
            tA, tB = p["tA"][:, j, 0:SCn], p["tB"][:, j, 0:SCn]
            t1, t2, gr, gi = H[0], H[1], H[2], H[3]
            Gr, Gi = H[4 + 2 * (n_ % 2)], H[5 + 2 * (n_ % 2)]
            kb.tt(t1, bA[:, 0:SCn], tA, ALU.mult)
            kb.tt(t2, bB[:, 0:SCn], tB, ALU.mult)
            kb.tt(gr, t1, t2, ALU.subtract)
            kb.tt(t1, bB[:, 0:SCn], tA, ALU.mult)
            kb.tt(t2, bA[:, 0:SCn], tB, ALU.mult)
            kb.tt(gi, t1, t2, ALU.add)
            ir = small_tmp([128, 1])
            ii = small_tmp([128, 1])
            tm = small_tmp([128, 1])
            kb.ts(tm.v(), st["ssr"][:, j:j + 1], p["c1"][:, j:j + 1], ALU.mult)
            kb.stt(ir.v(), st["ssi"][:, j:j + 1], p["ns1"][:, j:j + 1], tm.v(), ALU.mult, ALU.add)
            kb.ts(tm.v(), st["ssi"][:, j:j + 1], p["c1"][:, j:j + 1], ALU.mult)
            kb.stt(ii.v(), st["ssr"][:, j:j + 1], p["s1"][:, j:j + 1], tm.v(), ALU.mult, ALU.add)
            dec = V(p["mag"].ap[:, j:j + 1].to_broadcast([128, SCn]), p["mag"].deps)
            kb.scan(Gr, dec, gr, ir[:, 0:1])
            kb.scan(Gi, dec, gi, ii[:, 0:1])

        def stageB(n_):
            c0, j = its[n_]
            ct = j // 4
            cs_, sn_ = p["cs"][:, j, 0:SCn], p["sn"][:, j, 0:SCn]
            Gr, Gi = H[4 + 2 * (n_ % 2)], H[5 + 2 * (n_ % 2)]
            t3, t4 = H[8], H[9]
            hr, hi = H[10 + 2 * (n_ % 2)], H[11 + 2 * (n_ % 2)]
            kb.tt(t3, Gr, cs_, ALU.mult, e="pool")
            kb.tt(t4, Gi, sn_, ALU.mult, e="pool")
            kb.tt(hr, t3, t4, ALU.subtract, e="pool")
            kb.tt(t3, Gi, cs_, ALU.mult, e="pool")
            kb.tt(t4, Gr, sn_, ALU.mult, e="pool")
            kb.tt(hi, t3, t4, ALU.add, e="pool")
            kb.copy(st["ssr"][:, j:j + 1], hr[:, SCn - 1:SCn])
            kb.copy(st["ssi"][:, j:j + 1], hi[:, SCn - 1:SCn])
            q0 = 64 * ((j % 4) // 2)
            kb.mm(psY[ct][q0:q0 + 64, c0:c0 + SCn], p["Cre"][:, j, :], hr, start=(j % 2 == 0), stop=False)
            kb.mm(psY[ct][q0:q0 + 64, c0:c0 + SCn], p["nCim"][:, j, :], hi, start=False, stop=(j % 2 == 1))

        stageA(0)
        for n_ in range(len(its)):
            if n_ + 1 < len(its):
                stageA(n_ + 1)
            stageB(n_)
        for i in range(7):
            D_ = scr[i].deps[0]
            for hf in range(2):
                d_ = H[2 * i + hf].deps[0]
                toks = list(d_.r.items()) + ([d_.w] if d_.w is not None else [])
                for k_, v_ in toks:
                    if D_.r.get(k_, 0) < v_:
                        D_.r[k_] = v_

    def s5_branch(l, U, SS):
        p = P[l]
        st = PS[l]
        N, nseq, Tq = U.N, U.nseq, U.T
        isP = U.kind == "p"
        SCn = cfg.SC if isP else N
        psY = [banks[3], banks[4]]
        if isP:
            s5_prompt(l, U, psY)
        for c0 in (range(0, N, SCn) if not isP else []):
            for j in range(8):
                ct = j // 4
                rb = 32 * (j % 4)
                bA = nb()
                bB = nb()
                wb_ = 64 * ((j % 4) // 2)
                um = u_m[j % 2]
                kb.mm(bA[:, 0:SCn], p["Bre"][wb_:wb_ + 64, ct, :], um[wb_:wb_ + 64, ct, c0:c0 + SCn])
                kb.mm(bB[:, 0:SCn], p["Bim"][wb_:wb_ + 64, ct, :], um[wb_:wb_ + 64, ct, c0:c0 + SCn])
                if isP:
                    tA, tB = p["tA"][:, j, 0:SCn], p["tB"][:, j, 0:SCn]
                    cs_, sn_ = p["cs"][:, j, 0:SCn], p["sn"][:, j, 0:SCn]
                    w = lambda x: x
                else:
                    def bcv(tab):
                        return V(tab.ap[:, j:j + 1, 0:Tq].to_broadcast([128, nseq, Tq]), tab.deps)
                    tA, tB, cs_, sn_ = bcv(p["tA"]), bcv(p["tB"]), bcv(p["cs"]), bcv(p["sn"])
                    w = lambda x: x.re("p (b t) -> p b t", b=nseq)
                t1, t2, gr, gi = S(), S(), S(), S()
                kb.tt(w(t1[:, 0:SCn]), w(bA[:, 0:SCn]), tA, ALU.mult)
                kb.tt(w(t2[:, 0:SCn]), w(bB[:, 0:SCn]), tB, ALU.mult)
                kb.tt(gr[:, 0:SCn], t1[:, 0:SCn], t2[:, 0:SCn], ALU.subtract)
                kb.tt(w(t1[:, 0:SCn]), w(bB[:, 0:SCn]), tA, ALU.mult)
                kb.tt(w(t2[:, 0:SCn]), w(bA[:, 0:SCn]), tB, ALU.mult)
                kb.tt(gi[:, 0:SCn], t1[:, 0:SCn], t2[:, 0:SCn], ALU.add)
                Gr, Gi = S(), S()
                if isP:
                    ir = small_tmp([128, 1])
                    ii = small_tmp([128, 1])
                    tm = small_tmp([128, 1])
                    kb.ts(tm.v(), st["ssr"][:, j:j + 1], p["c1"][:, j:j + 1], ALU.mult)
                    kb.stt(ir.v(), st["ssi"][:, j:j + 1], p["ns1"][:, j:j + 1], tm.v(), ALU.mult, ALU.add)
                    kb.ts(tm.v(), st["ssi"][:, j:j + 1], p["c1"][:, j:j + 1], ALU.mult)
                    kb.stt(ii.v(), st["ssr"][:, j:j + 1], p["s1"][:, j:j + 1], tm.v(), ALU.mult, ALU.add)
                    dec = V(p["mag"].ap[:, j:j + 1].to_broadcast([128, SCn]), p["mag"].deps)
                    kb.scan(Gr[:, 0:SCn], dec, gr[:, 0:SCn], ir[:, 0:1])
                    kb.scan(Gi[:, 0:SCn], dec, gi[:, 0:SCn], ii[:, 0:1])
                else:
                    dect = S()
                    kb.ts(w(dect[:, 0:N]), pat01.v(), p["mag"][:, j:j + 1], ALU.mult)
                    ir, ii, tm = S(), S(), S()
                    h0r, h0i = SS["ssr"][:, j, :], SS["ssi"][:, j, :]
                    kb.ts(tm[:, 0:nseq], h0r, p["c1"][:, j:j + 1], ALU.mult)
                    kb.stt(ir[:, 0:nseq], h0i, p["ns1"][:, j:j + 1], tm[:, 0:nseq], ALU.mult, ALU.add)
                    kb.ts(tm[:, 0:nseq], h0i, p["c1"][:, j:j + 1], ALU.mult)
                    kb.stt(ii[:, 0:nseq], h0r, p["s1"][:, j:j + 1], tm[:, 0:nseq], ALU.mult, ALU.add)
                    kb.stt(w(gr[:, 0:N])[:, :, 0], ir[:, 0:nseq], p["mag"][:, j:j + 1], w(gr[:, 0:N])[:, :, 0], ALU.mult, ALU.add)
                    kb.stt(w(gi[:, 0:N])[:, :, 0], ii[:, 0:nseq], p["mag"][:, j:j + 1], w(gi[:, 0:N])[:, :, 0], ALU.mult, ALU.add)
                    kb.scan(Gr[:, 0:N], dect[:, 0:N], gr[:, 0:N], 0.0)
                    kb.scan(Gi[:, 0:N], dect[:, 0:N], gi[:, 0:N], 0.0)
                hr, hi = S(), S()
                pe_ = "pool" if isP else "dve"
                if isP:
                    t1, t2 = S(), S()
                kb.tt(w(t1[:, 0:SCn]), w(Gr[:, 0:SCn]), cs_, ALU.mult, e=pe_)
                kb.tt(w(t2[:, 0:SCn]), w(Gi[:, 0:SCn]), sn_, ALU.mult, e=pe_)
                kb.tt(hr[:, 0:SCn], t1[:, 0:SCn], t2[:, 0:SCn], ALU.subtract, e=pe_)
                kb.tt(w(t1[:, 0:SCn]), w(Gi[:, 0:SCn]), cs_, ALU.mult, e=pe_)
                kb.tt(w(t2[:, 0:SCn]), w(Gr[:, 0:SCn]), sn_, ALU.mult, e=pe_)
                kb.tt(hi[:, 0:SCn], t1[:, 0:SCn], t2[:, 0:SCn], ALU.add, e=pe_)
                if isP:
                    kb.copy(st["ssr"][:, j:j + 1], hr[:, SCn - 1:SCn])
                    kb.copy(st["ssi"][:, j:j + 1], hi[:, SCn - 1:SCn])
                else:
                    kb.copy(SS["ssr_o"][:, j, :], w(hr[:, 0:N])[:, :, Tq - 1])
                    kb.copy(SS["ssi_o"][:, j, :], w(hi[:, 0:N])[:, :, Tq - 1])
                q0 = 64 * ((j % 4) // 2)
                kb.mm(psY[ct][q0:q0 + 64, c0:c0 + SCn], p["Cre"][:, j, :], hr[:, 0:SCn], start=(j % 2 == 0), stop=False)
                kb.mm(psY[ct][q0:q0 + 64, c0:c0 + SCn], p["nCim"][:, j, :], hi[:, 0:SCn], start=False, stop=(j % 2 == 1))
        zt = []
        for ct in range(2):
            yv = S()
            kb.stt(yv[:, 0:N], u_m[0][:, ct, 0:N], p["dcol"][:, ct:ct + 1], psY[ct][:, 0:N], ALU.mult, ALU.add)
            kb.stt(yv[:, 0:N], u_m[1][:, ct, 0:N], p["dcol"][:, ct:ct + 1], yv[:, 0:N], ALU.mult, ALU.add)
            z_ = S()
            kb.act(z_[:, 0:N], yv[:, 0:N], AF.Gelu_apprx_tanh)
            zt.append(z_)
        for m in range(2):
            b = nb()
            for kt in range(2):
                kb.mm(b[:, 0:N], p["wglu"][:, kt, m * 128:(m + 1) * 128], zt[kt][:, 0:N], start=(kt == 0), stop=(kt == 1))
            sgm = S()
            kb.act(sgm[:, 0:N], b[:, 0:N], AF.Sigmoid, bias=p["bglu"][:, m:m + 1])
            kb.tt(sgm[:, 0:N], sgm[:, 0:N], zt[m][:, 0:N], ALU.mult)
            kb.tt(o_bf[:, 6 + m, 0:N], sgm[:, 0:N], sg[:, 6 + m, 0:N], ALU.mult)

    def rwkv_branch(l, U, SS, pc3):
        p = P[l]
        st = PS[l]
        N, nseq, Tq = U.N, U.nseq, U.T
        isP = U.kind == "p"
        xm = big1

        def v3(view):
            return view.re("p (b t) -> p b t", b=nseq)

        for i in range(10):
            if isP:
                kb.copy(pc3[i][:, :, 0:1], st["shst"][:, i:i + 1].re("p (a b) -> p a b", a=1))
            else:
                kb.copy(pc3[i][:, :, 0], SS["shift"][:, i, :])
            d = S()
            kb.tt(v3(d[:, 0:N]), pc3[i][:, :, 0:Tq], pc3[i][:, :, 1:Tq + 1], ALU.subtract)
            kb.stt(v3(xm[:, i, 0:N]), v3(d[:, 0:N]), p["mu"][:, i:i + 1], pc3[i][:, :, 1:Tq + 1], ALU.mult, ALU.add)
            if isP:
                kb.copy(st["shst"][:, i:i + 1].re("p (a b) -> p a b", a=1), pc3[i][:, :, Tq:Tq + 1])
            else:
                kb.copy(SS["shift_o"][:, i, :], pc3[i][:, :, Tq])
        xr, xw, xk, xv, xa = [lambda i, o=o: xm[:, o + i, 0:N] for o in (0, 2, 4, 6, 8)]
        b32 = nb()
        for kt in range(2):
            kb.mm(b32[0:32, 0:N], p["w1"][:, kt, :], xw(kt), start=(kt == 0), stop=(kt == 1))
        th = S()
        kb.act(th[0:32, 0:N], b32[0:32, 0:N], AF.Tanh)
        b32 = nb()
        for kt in range(2):
            kb.mm(b32[0:32, 0:N], p["a1"][:, kt, :], xa(kt), start=(kt == 0), stop=(kt == 1))
        ta = S()
        kb.copy(ta[0:32, 0:N], b32[0:32, 0:N])
        R = {k_: [None, None] for k_ in ("w", "k", "ka", "nkk")}
        for i in range(2):
            b = nb()
            kb.mm(b[:, 0:N], p["w2"][0:32, i * 128:(i + 1) * 128], th[0:32, 0:N])
            e0 = S()
            kb.act(e0[:, 0:N], b[:, 0:N], AF.Exp, bias=p["nw0"][:, i:i + 1], scale=-1.0)
            kb.act(e0[:, 0:N], e0[:, 0:N], AF.Ln, bias=1.0)
            kb.act(Rt[i][:, 0:N], e0[:, 0:N], AF.Exp, bias=eps_col(-0.5), scale=-1.0)
            R["w"][i] = Rt[i]
            b = nb()
            kb.mm(b[:, 0:N], p["a2"][0:32, i * 128:(i + 1) * 128], ta[0:32, 0:N])
            a_ = S()
            kb.act(a_[:, 0:N], b[:, 0:N], AF.Sigmoid, bias=p["a0"][:, i:i + 1])
            k1 = S()
            kb.ts(k1[:, 0:N], xk(i), p["kkc"][:, i:i + 1], ALU.mult)
            sq = S()
            kb.tt(sq[:, 0:N], k1[:, 0:N], k1[:, 0:N], ALU.mult)
            b = nb()
            kb.mm(b[:, 0:N], bo.v(), sq[:, 0:N])
            kb.act(sq[:, 0:N], b[:, 0:N], AF.Sqrt)
            kb.ts(sq[:, 0:N], sq[:, 0:N], 1e-12, ALU.max)
            kb.recip(sq[:, 0:N], sq[:, 0:N])
            kk_ = Rt[6 + i]
            kb.tt(kk_[:, 0:N], k1[:, 0:N], sq[:, 0:N], ALU.mult)
            kv_ = Rt[2 + i]
            kb.ts(kv_[:, 0:N], a_[:, 0:N], p["kac"][:, i:i + 1], ALU.mult, s2=p["omka"][:, i:i + 1], op1=ALU.add)
            kb.tt(kv_[:, 0:N], kv_[:, 0:N], xk(i), ALU.mult)
            R["k"][i] = kv_
            kka = Rt[4 + i]
            kb.tt(kka[:, 0:N], kk_[:, 0:N], a_[:, 0:N], ALU.mult)
            R["ka"][i] = kka
            kb.ts(kk_[:, 0:N], kk_[:, 0:N], -1.0, ALU.mult)
            R["nkk"][i] = kk_
        C = 64 if isP else Tq
        nch = N // C
        G = nch if isP else min(8, nch)
        ngrp = nch // G
        KK = int(round(math.log2(C))) - 1
        yb = banks[7]
        fbi = [0]

        def fb():
            b_ = banks[3 + fbi[0] % 4]
            fbi[0] += 1
            return b_

        def rows(fn):
            if C == 64:
                fn(slice(0, 128))
            else:
                for pb_ in (0, 64):
                    fn(slice(pb_, pb_ + C))

        def v3c(view):
            return view.re("p (c t) -> p c t", t=C)

        m4 = m4P if isP else m4S
        MMf = big1[:, 2:6, :].re("p a n -> p (a n)")
        TTf = big1[:, 8:10, :].re("p a n -> p (a n)")
        MM = MMf[:, 0:G * 4 * C].re("p (c s t) -> p c s t", c=G, s=4)
        TT = TTf[:, 0:2 * G * C].re("p (x c t) -> p x c t", x=2, c=G)
        for hp in range(2):
            nlw, lp, p_, pinv, pm1 = S(), S(), S(), S(), S()
            kb.ts(nlw[:, 0:N], R["w"][hp][:, 0:N], -1.0, ALU.mult)
            if isP:
                for c_ in range(nch):
                    kb.scan(lp[:, c_ * C:(c_ + 1) * C], ones_f[:, 0:C], nlw[:, c_ * C:(c_ + 1) * C], 0.0)
            else:
                kb.scan(lp[:, 0:N], pat01.v().re("p c t -> p (c t)"), nlw[:, 0:N], 0.0)
            kb.act(p_[:, 0:N], lp[:, 0:N], AF.Exp)
            kb.act(pinv[:, 0:N], lp[:, 0:N], AF.Exp, scale=-1.0)
            kb.tt(pm1[:, 0:N], lp[:, 0:N], nlw[:, 0:N], ALU.subtract)
            kb.act(pm1[:, 0:N], pm1[:, 0:N], AF.Exp)
            kb.copy(pCt[:, 0:nch], v3c(p_[:, 0:N])[:, :, C - 1])
            KR4 = KRt[:, 0:2 * N].re("p (c x t) -> p c x t", x=2, t=C)
            kb.tt(KR4[:, :, 0, :], v3c(pm1[:, 0:N]), v3c(R["nkk"][hp][:, 0:N]), ALU.mult)
            kb.tt(KR4[:, :, 1, :], v3c(p_[:, 0:N]), v3c(xr(hp)), ALU.mult)
            Bt = R["ka"][hp]
            kb.tt(Bt[:, 0:N], Bt[:, 0:N], pinv[:, 0:N], ALU.mult)
            Kt = R["nkk"][hp]
            kb.tt(Kt[:, 0:N], R["k"][hp][:, 0:N], pinv[:, 0:N], ALU.mult)
            for g in range(ngrp):
                cpb = min(G, 512 // (4 * C))
                for c0_ in range(0, G, cpb):
                    bm = fb()
                    for c in range(c0_, c0_ + cpb):
                        cg = g * G + c
                        for h2 in range(2):
                            pb = 64 * h2
                            o0 = (c - c0_) * 4 * C
                            kr = KR4[pb:pb + 64, cg, :, :].re("p x t -> p (x t)")
                            kb.mm(bm[pb:pb + C, o0:o0 + 2 * C], Bt[pb:pb + 64, cg * C:(cg + 1) * C], kr)
                            kb.mm(bm[pb:pb + C, o0 + 2 * C:o0 + 4 * C], Kt[pb:pb + 64, cg * C:(cg + 1) * C], kr)
                    rows(lambda r_: kb.tt(MM[r_, c0_:c0_ + cpb, :, :],
                                          bm[r_, 0:cpb * 4 * C].re("p (c s t) -> p c s t", c=cpb, s=4),
                                          V(m4.ap[r_, :, :].unsqueeze(1).to_broadcast([r_.stop - r_.start, cpb, 4, C]), m4.deps),
                                          ALU.mult))
                XY = [XYt[0][:, 0:2 * G * C].re("p (x c t) -> p x c t", x=2, c=G),
                      XYt[1][:, 0:2 * G * C].re("p (x c t) -> p x c t", x=2, c=G)]
                bn = fb()
                for c in range(G):
                    cg = g * G + c
                    for h2 in range(2):
                        pb = 64 * h2
                        kb.mm(bn[pb:pb + C, c * C:(c + 1) * C], KR4[pb:pb + 64, cg, 0, :], Bt[pb:pb + 64, cg * C:(cg + 1) * C])
                rows(lambda r_: kb.tt(XY[0][r_, 1, :, :], bn[r_, 0:G * C].re("p (c t) -> p c t", c=G),
                                      V(mL.ap[r_, 0:C].unsqueeze(1).to_broadcast([r_.stop - r_.start, G, C]), mL.deps), ALU.mult))
                rows(lambda r_: kb.copy(XY[0][r_, 0, :, :], MM[r_, :, 0, :]))
                rows(lambda r_: kb.tt(TT[r_, :, :, :], XY[0][r_, :, :, :],
                                      V(identD.ap[r_, 0:C].unsqueeze(1).unsqueeze(1).to_broadcast([r_.stop - r_.start, 2, G, C]),
                                        identD.deps), ALU.add))
                toks = []
                for src, dst in ((xv(hp), Vt), (Bt[:, 0:N], BtT), (Kt[:, 0:N], KtT)):
                    tp = fb()
                    for c in range(G):
                        cg = g * G + c
                        for h2 in range(2):
                            pb = 64 * h2
                            kb.mm(tp[pb:pb + C, c * 64:(c + 1) * 64], src[pb:pb + 64, cg * C:(cg + 1) * C],
                                  ident[pb:pb + 64, pb:pb + 64])
                    rows(lambda r_, tp=tp, dst=dst: kb.copy(dst[r_, 0:G * 64], tp[r_, 0:G * 64], e="act"))
                    toks.append(dst[:, 0:G * 64].re("p (c i) -> p c i", c=G))
                Vt3, BtT3, KtT3 = toks
                for k_ in range(1, KK + 1):
                    cur, nxt = XY[(k_ - 1) % 2], XY[k_ % 2]
                    last = k_ == KK
                    bk = fb()
                    for c in range(G):
                        for h2 in range(2):
                            pb = 64 * h2
                            kb.mm(bk[pb:pb + C, c * C:(c + 1) * C], cur[pb:pb + C, 1, c, :], cur[pb:pb + C, 0, c, :])
                            if not last:
                                kb.mm(bk[pb:pb + C, (G + c) * C:(G + c + 1) * C], cur[pb:pb + C, 0, c, :], cur[pb:pb + C, 1, c, :])
                    nx = 1 if last else 2
                    rows(lambda r_: kb.copy(nxt[r_, 0:nx, :, :], bk[r_, 0:nx * G * C].re("p (x c t) -> p x c t", x=nx, c=G)))
                    bt_ = fb()
                    for c in range(G):
                        for h2 in range(2):
                            pb = 64 * h2
                            kb.mm(bt_[pb:pb + C, c * C:(c + 1) * C], TT[pb:pb + C, 1, c, :], nxt[pb:pb + C, 0, c, :])
                            if not last:
                                kb.mm(bt_[pb:pb + C, (G + c) * C:(G + c + 1) * C], nxt[pb:pb + C, 0, c, :], TT[pb:pb + C, 1, c, :])
                    rows(lambda r_: kb.tt(TT[r_, 0:nx, :, :], TT[r_, 0:nx, :, :],
                                          bt_[r_, 0:nx * G * C].re("p (x c t) -> p x c t", x=nx, c=G), ALU.add))
                for c in range(G):
                    cg = g * G + c
                    if isP:
                        ST0 = st["ST"][hp][st["parh"][hp]]
                        STn = st["ST"][hp][1 - st["parh"][hp]]
                    else:
                        ST0 = SS["ST"][cg][hp][0]
                        STn = SS["ST"][cg][hp][1]
                    bw = fb()
                    for h2 in range(2):
                        pb = 64 * h2
                        kb.mm(bw[pb:pb + C, 0:64], MM[pb:pb + C, c, 2, :], Vt3[pb:pb + C, c, :], start=True, stop=False)
                        kb.mm(bw[pb:pb + C, 0:64], KR4[pb:pb + 64, cg, 0, :], ST0[pb:pb + 64, :], start=False, stop=True)
                    rows(lambda r_: kb.copy(Wsb[r_, :], bw[r_, 0:64], e="act"))
                    bu = fb()
                    for h2 in range(2):
                        pb = 64 * h2
                        kb.mm(bu[pb:pb + C, 0:64], TT[pb:pb + C, 0, c, :], Wsb[pb:pb + C, :])
                    rows(lambda r_: kb.copy(Usb[r_, :], bu[r_, 0:64]))
                    for h2 in range(2):
                        pb = 64 * h2
                        yo = yb[pb:pb + 64, hp * 256 + cg * C:hp * 256 + (cg + 1) * C]
                        kb.mm(yo, ST0[pb:pb + 64, :], KR4[pb:pb + 64, cg, 1, :], start=True, stop=False)
                        kb.mm(yo, Usb[pb:pb + C, :], MM[pb:pb + C, c, 1, :], start=False, stop=False)
                        kb.mm(yo, Vt3[pb:pb + C, c, :], MM[pb:pb + C, c, 3, :], start=False, stop=True)
                    bs = fb()
                    for h2 in range(2):
                        pb = 64 * h2
                        kb.mm(bs[pb:pb + 64, 0:64], BtT3[pb:pb + C, c, :], Usb[pb:pb + C, :], start=True, stop=False)
                        kb.mm(bs[pb:pb + 64, 0:64], KtT3[pb:pb + C, c, :], Vt3[pb:pb + C, c, :], start=False, stop=True)
                    kb.tt(STn.v(), bs[:, 0:64], ST0.v(), ALU.add)
                    kb.ts(STn.v(), STn.v(), pCt[:, cg:cg + 1], ALU.mult)
                    if isP:
                        st["parh"][hp] = 1 - st["parh"][hp]
                    else:
                        SS["STpar"][cg] = 1
        if isP:
            st["par"] = st["parh"][0]
        for hp in range(2):
            kb.copy(ysb[:, hp, 0:N], yb[:, hp * 256:hp * 256 + N], e="act")
        for i in range(2):
            y = ysb[:, i, 0:N]
            b = nb()
            kb.mm(b[:, 0:N], bo.v(), y)
            yc = S()
            kb.stt(yc[:, 0:N], b[:, 0:N], -1.0 / 64, y, ALU.mult, ALU.add)
            sq = S()
            kb.tt(sq[:, 0:N], yc[:, 0:N], yc[:, 0:N], ALU.mult)
            b = nb()
            kb.mm(b[:, 0:N], bo.v(), sq[:, 0:N])
            kb.act(sq[:, 0:N], b[:, 0:N], AF.Sqrt, bias=eps_col(64 * 1e-5), scale=1.0 / 64)
            kb.recip(sq[:, 0:N], sq[:, 0:N])
            kb.tt(yc[:, 0:N], yc[:, 0:N], sq[:, 0:N], ALU.mult)
            kb.ts(yc[:, 0:N], yc[:, 0:N], p["gnw"][:, i:i + 1], ALU.mult, s2=p["gnb"][:, i:i + 1], op1=ALU.add)
            rk_ = S()
            kb.tt(rk_[:, 0:N], xr(i), R["k"][i][:, 0:N], ALU.mult)
            kb.ts(rk_[:, 0:N], rk_[:, 0:N], p["rkc"][:, i:i + 1], ALU.mult)
            b = nb()
            kb.mm(b[:, 0:N], bo.v(), rk_[:, 0:N])
            kb.tt(rk_[:, 0:N], b[:, 0:N], xv(i), ALU.mult)
            kb.tt(yc[:, 0:N], yc[:, 0:N], rk_[:, 0:N], ALU.add)
            kb.tt(o_bf[:, 4 + i, 0:N], yc[:, 0:N], sg[:, 4 + i, 0:N], ALU.mult)

    def load_x(src_ap, N):
        for tt_ in range(N // 128):
            for half in range(2):
                b = nb()
                for i in range(4):
                    xs_ = S()
                    k_ = half * 4 + i
                    ld(xs_[:, 0:128], src_ap[tt_ * 128:(tt_ + 1) * 128, k_ * 128:(k_ + 1) * 128])
                    kb.tr(b[:, i * 128:(i + 1) * 128], xs_[:, 0:128], ident.v())
                kb.copy(xT[:, half * 4:half * 4 + 4, tt_ * 128:(tt_ + 1) * 128], b.v().re("p (k t) -> p k t", k=4),
                        e=("act" if half else "dve"))

    def store_y(dst_ap, N):
        for tt_ in range(N // 128):
            for half in range(2):
                b = nb()
                for i in range(4):
                    kb.tr(b[:, i * 128:(i + 1) * 128], xT[:, half * 4 + i, tt_ * 128:(tt_ + 1) * 128], ident.v())
                kb.copy(stg.v(), b.v(), e="act")
                kb.dma("sp", V(dst_ap[tt_ * 128:(tt_ + 1) * 128, half * 512:(half + 1) * 512], []), stg.v())

    def drive():
        begin_phase(cfg.NCH, True)
        for u in range(cfg.TP // cfg.NCH):
            U = Unit("p", u * cfg.NCH, cfg.NCH, 1, cfg.NCH)
            ckpt(1)
            load_x(din["x_prompt"].ap[U.tok0:U.tok0 + cfg.NCH, :], cfg.NCH)
            for l in range(L):
                layer(l, U, None)
            store_y(dout["y_p"].ap[U.tok0:U.tok0 + cfg.NCH, :], cfg.NCH)
        ckpt(9)
        for l in range(L):
            st = PS[l]
            fm_to_rows([st["convst"][:, i, j:j + 1] for j in range(3) for i in range(2)], dout["conv_p"].ap[l], 1)
            fm_to_rows([st["lruh"][:, i:i + 1] for i in range(2)], dout["lru_p"].ap[l], 1)
            fm_to_rows([st["shst"][:, i:i + 1] for i in range(10)], dout["shift_p"].ap[l], 1)
            ssc = sb("ssc%d" % l, [128, 8, 2])
            kb.copy(ssc[:, :, 0], st["ssr"].v())
            kb.copy(ssc[:, :, 1], st["ssi"].v())
            kb.dma("sp", V(dout["ssm_p"].ap[l, 0].rearrange("(j p c) -> p j c", j=8, c=2), []), ssc.v())
            for hp in range(2):
                b = nb()
                kb.tr(b[0:64, 0:128], st["ST"][hp][st["par"]].v(), ident.v())
                kb.copy(stg[0:64, 0:128], b[0:64, 0:128], e="act")
                kb.dma("sp", V(dout["wkv_p"].ap[l, 0, hp * 128:(hp + 1) * 128, :].rearrange("(h i) j -> i h j", h=2), []),
                       stg[0:64, 0:128].re("p (h j) -> p h j", h=2))

        ckpt(10)
        end_phase()
        begin_phase(cfg.NS, False)
        SB_, Tq = cfg.SB, cfg.TS
        NS = cfg.NS
        SS = {}
        idx_i = sb("idx_i", [128, SB_ * cfg.NPG], I32)
        ld(idx_i.v(), din["page_table"].ap[0:1, :].to_broadcast([128, SB_ * cfg.NPG]))
        SS["idx"] = []
        for l in range(L):
            idx = sb("idx%d" % l, [128, SB_ * cfg.NPG], I32)
            kb.ts(idx.v(), idx_i.v(), 128.0, ALU.mult, s2=iop[:, 0:1], op1=ALU.add)
            if l > 0:
                kb.ts(idx.v(), idx.v(), float(l * cfg.POOL * 128), ALU.add)
            SS["idx"].append(idx)
        SS["kpg"] = [sb("kpg%d" % i, [128, 4, 256]) for i in range(2)]
        SS["vpg"] = [sb("vpg%d" % i, [128, 4, 256]) for i in range(2)]
        SS["kTg"] = [sb("kTg%d" % i, [128, 2, 512], BF16) for i in range(2)]
        SS["kTc"] = sb("kTc", [128, 2, NS], BF16)
        SS["wv"] = sb("wv", [128, 8, 256], BF16)
        SS["vb"] = [sb("vb%d" % i, [Tq, 256]) for i in range(2)]
        SS["PTs"] = sb("PTs", [128, 8 * 4 * Tq], BF16)
        SS["vpb"] = [sb("vpb%d" % i, [128, 4, 256], BF16) for i in range(2)]
        SS["PTr"] = sb("PTr", [128, 8 * Tq])
        SS["qS"] = sb("qS", [128, 2, SB_, 2, Tq], BF16)
        SS["PTc"] = sb("PTc", [Tq, 8 * Tq])
        SS["conv"] = sb("s_conv", [128, 2, SB_, 3])
        SS["lru"] = sb("s_lru", [128, 2, SB_])
        SS["lru_o"] = sb("s_lru_o", [128, 2, SB_])
        SS["shift"] = sb("s_shift", [128, 10, SB_])
        SS["shift_o"] = sb("s_shift_o", [128, 10, SB_])
        SS["ssr"] = sb("s_ssr", [128, 8, SB_])
        SS["ssi"] = sb("s_ssi", [128, 8, SB_])
        SS["ssr_o"] = sb("s_ssr_o", [128, 8, SB_])
        SS["ssi_o"] = sb("s_ssi_o", [128, 8, SB_])
        SS["ST"] = [[[sb("sST%d_%d_%d" % (b_, hp, pp), [128, 64]) for pp in range(2)] for hp in range(2)] for b_ in range(SB_)]
        SS["STpar"] = [0] * SB_
        SS["s0"] = [sb("s0_%d" % i, [64, 256]) for i in range(2)]
        rows = sb("rows", [SB_, 2048])
        U = Unit("s", 0, NS, SB_, Tq)

        def rows_to_fm(dst_fn, src_ap, ncols, stride=1, off=0):
            ld(rows[:, 0:ncols * 128 * stride], src_ap)
            for i in range(ncols):
                b = nb()
                src = rows[:, i * 128 * stride:(i + 1) * 128 * stride]
                if stride > 1:
                    src = src.re("p (n c) -> p n c", c=stride)[:, :, off]
                kb.tr(b[:, 0:SB_], src, ident[0:SB_, 0:SB_])
                kb.copy(dst_fn(i), b[:, 0:SB_], e="act")

        load_x(din["x_sample"].ap, NS)
        ckpt(11)
        for l in range(L):
            for j in range(3):
                rows_to_fm(lambda i, j=j: SS["conv"][:, i, :, j], din["state_conv"].ap[l][:, j * 256:(j + 1) * 256], 2)
            rows_to_fm(lambda i: SS["lru"][:, i, :], din["state_lru"].ap[l], 2)
            rows_to_fm(lambda i: SS["shift"][:, i, :], din["state_shift"].ap[l], 10)
            rows_to_fm(lambda i: SS["ssr"][:, i, :], din["state_ssm"].ap[l], 8, stride=2, off=0)
            rows_to_fm(lambda i: SS["ssi"][:, i, :], din["state_ssm"].ap[l], 8, stride=2, off=1)
            for b_ in range(SB_):
                s0 = SS["s0"][b_ % 2]
                ld(s0[0:64, 0:256].re("p (h j) -> p h j", h=4), din["state_wkv"].ap[l, b_].rearrange("(h i) j -> i h j", h=4))
                for hp in range(2):
                    b = nb()
                    kb.tr(b[:, 0:64], s0[0:64, hp * 128:(hp + 1) * 128], ident[0:64, 0:64])
                    kb.copy(SS["ST"][b_][hp][0].v(), b[:, 0:64], e="act")
            ckpt(12)
            layer(l, U, SS)
            ckpt(13)
            fm_to_rows([xp[:, i, 0:NS + 3 * SB_].re("p (b t) -> p b t", b=SB_)[:, :, Tq + j] for j in range(3) for i in range(2)],
                       dout["conv_s"].ap[l], SB_)
            fm_to_rows([SS["lru_o"][:, i, :] for i in range(2)], dout["lru_s"].ap[l], SB_)
            fm_to_rows([SS["shift_o"][:, i, :] for i in range(10)], dout["shift_s"].ap[l], SB_)
            for g0 in range(0, 8, 2):
                b = nb()
                for jj in range(2):
                    for c_, src in ((0, SS["ssr_o"]), (1, SS["ssi_o"])):
                        kb.tr(b[0:SB_, (jj * 2 + c_) * 128:(jj * 2 + c_ + 1) * 128], src[:, g0 + jj, :], ident.v())
                kb.copy(stg[0:SB_, 0:512].re("p (j n c) -> p j n c", j=2, c=2),
                        b[0:SB_, 0:512].re("p (j c n) -> p j n c", j=2, c=2), e="act")
                kb.dma("sp", V(dout["ssm_s"].ap[l][:, g0 * 256:(g0 + 2) * 256], []), stg[0:SB_, 0:512])
            for b_ in range(SB_):
                for hp in range(2):
                    b = nb()
                    kb.tr(b[0:64, 0:128], SS["ST"][b_][hp][SS["STpar"][b_]].v(), ident.v())
                    kb.copy(stg[0:64, 0:128], b[0:64, 0:128], e="act")
                    kb.dma("sp", V(dout["wkv_s"].ap[l, b_, hp * 128:(hp + 1) * 128, :].rearrange("(h i) j -> i h j", h=2), []),
                           stg[0:64, 0:128].re("p (h j) -> p h j", h=2))
        store_y(dout["y_s"].ap, NS)
        end_phase()

    try:
        drive()
    except StopBuild:
        pass
    kb.finish()
    kb.ck_log = ck_log
    ctx.__exit__(None, None, None)
    return nc, kb


def shard_inputs(inputs, cfg, ncores):
    maps = []
    L = cfg.DEPTH
    for c in range(ncores):
        sb0, sb1 = c * cfg.SB, (c + 1) * cfg.SB
        m = {}
        m["x_prompt"] = np.ascontiguousarray(inputs["x_prompt"][c])
        m["x_sample"] = np.ascontiguousarray(inputs["x_sample"][sb0:sb1]).reshape(cfg.NS, D_MODEL)
        m["cache_k"] = inputs["cache_k"].reshape(L, cfg.POOL * 128, 256)
        m["cache_v"] = inputs["cache_v"].reshape(L, cfg.POOL * 128, 256)
        m["page_table"] = np.ascontiguousarray(inputs["page_table"][sb0:sb1]).reshape(1, cfg.SB * cfg.NPG).astype(np.int32)
        m["state_conv"] = np.ascontiguousarray(inputs["state_conv"][:, sb0:sb1]).reshape(L, cfg.SB, 768)
        m["state_lru"] = np.ascontiguousarray(inputs["state_lru"][:, sb0:sb1])
        m["state_shift"] = np.ascontiguousarray(inputs["state_shift"][:, sb0:sb1])
        m["state_wkv"] = np.ascontiguousarray(inputs["state_wkv"][:, sb0:sb1]).reshape(L, cfg.SB, 256, 64)
        m["state_ssm"] = np.ascontiguousarray(inputs["state_ssm"][:, sb0:sb1]).reshape(L, cfg.SB, 2048)
        for n, f, dt in IN_SPECS[10:]:
            a = np.asarray(inputs[n])
            if n == "rw_rk":
                a = a.reshape(L, 256)
            m[n] = np.ascontiguousarray(a)
        maps.append(m)
    return maps


def gather_outputs(res, cfg, ncores):
    L = cfg.DEPTH
    cat = lambda n, ax: np.concatenate([r[n] for r in res], axis=ax)
    B = ncores
    y_p = np.stack([r["y_p"] for r in res]).reshape(B, cfg.TP, D_MODEL)
    y_s = cat("y_s", 0).reshape(B * cfg.SB, cfg.TS, D_MODEL)
    k_p = np.stack([r["k_p"] for r in res], axis=1).reshape(L, B, cfg.TP, 4, 64)
    v_p = np.stack([r["v_p"] for r in res], axis=1).reshape(L, B, cfg.TP, 4, 64)
    k_s = cat("k_s", 1).reshape(L, B * cfg.SB, cfg.TS, 4, 64)
    v_s = cat("v_s", 1).reshape(L, B * cfg.SB, cfg.TS, 4, 64)
    conv_p = cat("conv_p", 1).reshape(L, B, 3, 256)
    conv_s = cat("conv_s", 1).reshape(L, B * cfg.SB, 3, 256)
    lru_p = cat("lru_p", 1).reshape(L, B, 256)
    lru_s = cat("lru_s", 1).reshape(L, B * cfg.SB, 256)
    shift_p = cat("shift_p", 1).reshape(L, B, 1280)
    shift_s = cat("shift_s", 1).reshape(L, B * cfg.SB, 1280)
    wkv_p = cat("wkv_p", 1).reshape(L, B, 4, 64, 64)
    wkv_s = cat("wkv_s", 1).reshape(L, B * cfg.SB, 4, 64, 64)
    ssm_p = cat("ssm_p", 1).reshape(L, B, 16, 64, 2)
    ssm_s = cat("ssm_s", 1).reshape(L, B * cfg.SB, 16, 64, 2)
    return (y_p, y_s, k_p, k_s, v_p, v_s, conv_p, conv_s, lru_p, lru_s, shift_p, shift_s, wkv_p, wkv_s, ssm_p, ssm_s)


def kernel(**inputs):
    ncores = 8
    cfg = Cfg()
    inputs = {k: np.asarray(v) for k, v in inputs.items()}
    nc, kb = build(cfg)
    maps = shard_inputs(inputs, cfg, ncores)
    res = run_bass_kernel_spmd(nc, maps, core_ids=list(range(ncores)))
    outs = gather_outputs(res.results, cfg, ncores)
    return tuple(np.ascontiguousarray(o, dtype=np.float32) for o in outs)
```
